# Optimizing a Trainium2 kernel written in Bass

```python
import math
import jax, jax.numpy as jnp
from jax import lax
import numpy as np


D_MODEL = 1024
BATCH = 8
SEQ = 4096
DEPTH = 1

HEAD_DIM = 64
N_HEADS_A = 8
N_KV_A = 2
N_HEADS_B = 8
N_KV_B = 2
N_HEADS = N_HEADS_A + N_HEADS_B
MIX_WIDTH = N_HEADS * HEAD_DIM
WINDOW_A = 128
CMP_LEN = 32
CMP_STRIDE = 16
CMP_HIDDEN = 128
SEL_LEN = 64
SEL_TOPK = 16
WINDOW_B = 512
FORCE_BONUS = 1000.0
BAND_BLOCK = 128
NSA_BLOCK = 32
NUM_BUCKETS = 32
MAX_DISTANCE = 128
D_FF = 2816
EPS = 1e-6
NEG = -1e9

QA = N_HEADS_A * HEAD_DIM
KVA = N_KV_A * HEAD_DIM
QB = N_HEADS_B * HEAD_DIM
KVB = N_KV_B * HEAD_DIM
N_GATES = 3 * N_HEADS_B
OFF_KA = QA
OFF_VA = OFF_KA + KVA
OFF_QB = OFF_VA + KVA
OFF_KVB = OFF_QB + QB
OFF_GB = OFF_KVB + 6 * KVB
D_IN = OFF_GB + N_GATES

kernel_name = 'hymba_swa_sink_nsa_macaron_block'


def rms_norm(x, g):
    x32 = x.astype(jnp.float32)
    y = x32 * lax.rsqrt(jnp.mean(x32 * x32, axis=-1, keepdims=True) + EPS) * g.astype(jnp.float32)
    return y.astype(x.dtype)


def swiglu(h, w_in, w_out):
    gate, up = jnp.split(h @ w_in, 2, axis=-1)
    return (jax.nn.silu(gate) * up) @ w_out


def rel_bucket(rel):
    n = jnp.maximum(rel, 0)
    max_exact = NUM_BUCKETS // 2
    nf = jnp.maximum(n, 1).astype(jnp.float32)
    large = max_exact + (jnp.log(nf / max_exact) / math.log(MAX_DISTANCE / max_exact)
                         * (NUM_BUCKETS - max_exact)).astype(jnp.int32)
    large = jnp.minimum(large, NUM_BUCKETS - 1)
    return jnp.where(n < max_exact, n, large)


def rel_bias(tbl, rel):
    return tbl[rel_bucket(rel)].astype(jnp.float32)


def banded_attention(q, k, v, tbl, window, sinks):
    B, S, G, R, Dh = q.shape
    kv_len = window + BAND_BLOCK
    k_pad = jnp.pad(k, ((0, 0), (window, 0), (0, 0), (0, 0)))
    v_pad = jnp.pad(v, ((0, 0), (window, 0), (0, 0), (0, 0)))

    def block(i):
        start = i * BAND_BLOCK
        qb = lax.dynamic_slice_in_dim(q, start, BAND_BLOCK, axis=1)
        kb = lax.dynamic_slice_in_dim(k_pad, start, kv_len, axis=1)
        vb = lax.dynamic_slice_in_dim(v_pad, start, kv_len, axis=1)
        q_pos = start + jnp.arange(BAND_BLOCK)
        k_pos = start - window + jnp.arange(kv_len)
        rel = q_pos[:, None] - k_pos[None, :]
        mask = (rel >= 0) & (rel < window) & (k_pos[None, :] >= 0)
        bias = rel_bias(tbl, rel).reshape(BAND_BLOCK, kv_len, G, R).transpose(2, 3, 0, 1)
        logits = jnp.einsum('bqgrd,bkgd->bgrqk', qb, kb).astype(jnp.float32) + bias
        logits = jnp.where(mask, logits, NEG)
        m = logits.max(axis=-1, keepdims=True)
        if sinks is None:
            p = jnp.exp(logits - m)
            denom = p.sum(axis=-1, keepdims=True)
        else:
            s = sinks.astype(jnp.float32).reshape(1, G, R, 1, 1)
            m = jnp.maximum(m, s)
            p = jnp.exp(logits - m)
            denom = p.sum(axis=-1, keepdims=True) + jnp.exp(s - m)
        p = p / denom
        return jnp.einsum('bgrqk,bkgd->bqgrd', p.astype(vb.dtype), vb)

    out = lax.map(block, jnp.arange(S // BAND_BLOCK))
    return out.transpose(1, 0, 2, 3, 4, 5).reshape(B, S, G, R, Dh)


def compress_kv(kv, pos, w1, b1, w2, b2):
    _, B, S, G, Dh = kv.shape
    r = CMP_LEN // CMP_STRIDE
    n_chunk = S // CMP_STRIDE
    n_cmp = n_chunk - r + 1
    chunks = kv.reshape(2, B, n_chunk, CMP_STRIDE, G, Dh)
    w1r = w1.reshape(2, r, CMP_STRIDE, Dh, CMP_HIDDEN)
    pre = b1[:, None, None, None, :]
    for j in range(r):
        pe = pos[:, None, None, j * CMP_STRIDE:(j + 1) * CMP_STRIDE, None, :]
        part = jnp.einsum('kbnpgd,kpdh->kbngh', chunks + pe, w1r[:, j])
        pre = pre + part[:, :, j:j + n_cmp]
    hid = jax.nn.silu(pre)
    return jnp.einsum('kbcgh,khd->kbcgd', hid, w2) + b2[:, None, None, None, :]


def nsa_cmp_sel(q, k_c, v_c, k_s, v_s, tbl):
    B, S, G, R, Dh = q.shape
    n_cmp = k_c.shape[1]
    n_slc = S // SEL_LEN
    n_top = min(SEL_TOPK, n_slc)
    cmp_start = jnp.arange(n_cmp) * CMP_STRIDE
    cmp_end = cmp_start + CMP_LEN - 1
    slc_start = jnp.arange(n_slc) * SEL_LEN
    overlap = ((cmp_start[:, None] < slc_start[None, :] + SEL_LEN)
               & (cmp_end[:, None] >= slc_start[None, :])).astype(jnp.float32)
    kb = k_s.reshape(B, n_slc, SEL_LEN, G, Dh).transpose(0, 3, 1, 2, 4)
    vb = v_s.reshape(B, n_slc, SEL_LEN, G, Dh).transpose(0, 3, 1, 2, 4)
    tbl3 = tbl.reshape(NUM_BUCKETS, G, R)
    b_idx = jnp.arange(B)[:, None, None, None]
    g_idx = jnp.arange(G)[None, :, None, None]
    blk = jnp.arange(n_slc)
    offs = jnp.arange(SEL_LEN)

    def block(i):
        start = i * NSA_BLOCK
        qb = lax.dynamic_slice_in_dim(q, start, NSA_BLOCK, axis=1)
        q_pos = start + jnp.arange(NSA_BLOCK)
        rel_c = q_pos[:, None] - cmp_end[None, :]
        mask_c = rel_c >= 0
        bias_c = rel_bias(tbl, rel_c).reshape(NSA_BLOCK, n_cmp, G, R).transpose(2, 3, 0, 1)
        lc = jnp.einsum('bqgrd,bcgd->bgrqc', qb, k_c).astype(jnp.float32) + bias_c
        lc = jnp.where(mask_c, lc, NEG)
        pc = jnp.where(mask_c, jnp.exp(lc - lc.max(axis=-1, keepdims=True)), 0.0)
        pc = pc / jnp.maximum(pc.sum(axis=-1, keepdims=True), jnp.finfo(jnp.float32).tiny)
        o_cmp = jnp.einsum('bgrqc,bcgd->bqgrd', pc.astype(v_c.dtype), v_c)
        imp = jnp.einsum('bgrqc,cn->bgqn', pc, overlap)
        cur = q_pos[:, None] // SEL_LEN
        valid = blk[None, :] <= cur
        forced = (blk[None, :] == 0) | (blk[None, :] == cur) | (blk[None, :] == cur - 1)
        score = jnp.where(valid, imp + jnp.where(forced, FORCE_BONUS, 0.0), NEG)
        top_val, top_idx = lax.top_k(score, n_top)
        sel_ok = top_val > 0.5 * NEG
        ks = kb[b_idx, g_idx, top_idx]
        vs = vb[b_idx, g_idx, top_idx]
        tok = top_idx[..., None] * SEL_LEN + offs
        rel_s = q_pos[None, None, :, None, None] - tok
        mask_s = (sel_ok[..., None] & (rel_s >= 0))[:, :, None]
        bias_s = jnp.moveaxis(tbl3[rel_bucket(rel_s), g_idx[..., None]], -1, 2).astype(jnp.float32)
        ls = jnp.einsum('bqgrd,bgqnld->bgrqnl', qb, ks).astype(jnp.float32) + bias_s
        ls = jnp.where(mask_s, ls, NEG)
        ps = jax.nn.softmax(ls.reshape(B, G, R, NSA_BLOCK, n_top * SEL_LEN), axis=-1).reshape(ls.shape)
        o_sel = jnp.einsum('bgrqnl,bgqnld->bqgrd', ps.astype(vs.dtype), vs)
        return o_cmp, o_sel

    o_cmp, o_sel = lax.map(block, jnp.arange(S // NSA_BLOCK))
    o_cmp = o_cmp.transpose(1, 0, 2, 3, 4, 5).reshape(B, S, G, R, Dh)
    o_sel = o_sel.transpose(1, 0, 2, 3, 4, 5).reshape(B, S, G, R, Dh)
    return o_cmp, o_sel


def hybrid_layer(x, tbl, ffn1_norm, ffn1_w_in, ffn1_w_out, mix_norm, w_mix_in, w_mix_out,
                 q_norm_a, k_norm_a, sinks_a, q_norm_b, k_norm_b,
                 cmp_pos, cmp_w1, cmp_b1, cmp_w2, cmp_b2,
                 ffn2_norm, ffn2_w_in, ffn2_w_out):
    B, S, _ = x.shape
    r_a = N_HEADS_A // N_KV_A
    r_b = N_HEADS_B // N_KV_B
    scale = HEAD_DIM ** -0.5
    x = x + 0.5 * swiglu(rms_norm(x, ffn1_norm), ffn1_w_in, ffn1_w_out)
    h = rms_norm(x, mix_norm)
    proj = h @ w_mix_in
    q_a = rms_norm(proj[..., :OFF_KA].reshape(B, S, N_KV_A, r_a, HEAD_DIM), q_norm_a) * scale
    k_a = rms_norm(proj[..., OFF_KA:OFF_VA].reshape(B, S, N_KV_A, HEAD_DIM), k_norm_a)
    v_a = proj[..., OFF_VA:OFF_QB].reshape(B, S, N_KV_A, HEAD_DIM)
    o_a = banded_attention(q_a, k_a, v_a, tbl[:, :N_HEADS_A], WINDOW_A, sinks_a)
    q_b = rms_norm(proj[..., OFF_QB:OFF_KVB].reshape(B, S, N_KV_B, r_b, HEAD_DIM), q_norm_b) * scale
    kv_b = proj[..., OFF_KVB:OFF_GB].reshape(B, S, 6, N_KV_B, HEAD_DIM)
    kv_c = compress_kv(jnp.moveaxis(kv_b[:, :, 0:2], 2, 0), cmp_pos, cmp_w1, cmp_b1, cmp_w2, cmp_b2)
    k_c = rms_norm(kv_c[0], k_norm_b)
    v_c = kv_c[1]
    k_s = rms_norm(kv_b[:, :, 2], k_norm_b)
    v_s = kv_b[:, :, 3]
    k_w = rms_norm(kv_b[:, :, 4], k_norm_b)
    v_w = kv_b[:, :, 5]
    tbl_b = tbl[:, N_HEADS_A:]
    o_cmp, o_sel = nsa_cmp_sel(q_b, k_c, v_c, k_s, v_s, tbl_b)
    o_win = banded_attention(q_b, k_w, v_w, tbl_b, WINDOW_B, None)
    gates = jax.nn.sigmoid(proj[..., OFF_GB:].astype(jnp.float32)).reshape(B, S, 3, N_KV_B, r_b, 1)
    o_b = gates[:, :, 0] * o_cmp + gates[:, :, 1] * o_sel + gates[:, :, 2] * o_win
    mix = jnp.concatenate([o_a.reshape(B, S, QA).astype(x.dtype),
                           o_b.reshape(B, S, QB).astype(x.dtype)], axis=-1)
    x = x + mix @ w_mix_out
    x = x + 0.5 * swiglu(rms_norm(x, ffn2_norm), ffn2_w_in, ffn2_w_out)
    return x


def setup_inputs(seed: int = 0) -> dict:
    key = jax.random.key(seed)
    ks = jax.random.split(key, 24)
    f32 = jnp.float32

    def nrm(k, shape, scale):
        return jax.random.normal(k, shape, f32) * scale

    def gain(k, shape):
        return 1.0 + 0.05 * jax.random.normal(k, shape, f32)

    L = DEPTH
    return {
        'x': nrm(ks[0], (BATCH, SEQ, D_MODEL), 1.0),
        'rel_bias_table': nrm(ks[1], (NUM_BUCKETS, N_HEADS), 0.5),
        'ffn1_norm': gain(ks[2], (L, D_MODEL)),
        'ffn1_w_in': nrm(ks[3], (L, D_MODEL, 2 * D_FF), D_MODEL ** -0.5),
        'ffn1_w_out': nrm(ks[4], (L, D_FF, D_MODEL), D_FF ** -0.5),
        'mix_norm': gain(ks[5], (L, D_MODEL)),
        'w_mix_in': nrm(ks[6], (L, D_MODEL, D_IN), D_MODEL ** -0.5),
        'w_mix_out': nrm(ks[7], (L, MIX_WIDTH, D_MODEL), MIX_WIDTH ** -0.5),
        'q_norm_a': gain(ks[8], (L, HEAD_DIM)),
        'k_norm_a': gain(ks[9], (L, HEAD_DIM)),
        'sinks_a': nrm(ks[10], (L, N_HEADS_A), 0.5),
        'q_norm_b': gain(ks[11], (L, HEAD_DIM)),
        'k_norm_b': gain(ks[12], (L, HEAD_DIM)),
        'cmp_pos': nrm(ks[13], (L, 2, CMP_LEN, HEAD_DIM), 0.5),
        'cmp_w1': nrm(ks[14], (L, 2, CMP_LEN * HEAD_DIM, CMP_HIDDEN), (CMP_LEN * HEAD_DIM) ** -0.5),
        'cmp_b1': nrm(ks[15], (L, 2, CMP_HIDDEN), 0.02),
        'cmp_w2': nrm(ks[16], (L, 2, CMP_HIDDEN, HEAD_DIM), CMP_HIDDEN ** -0.5),
        'cmp_b2': nrm(ks[17], (L, 2, HEAD_DIM), 0.02),
        'ffn2_norm': gain(ks[18], (L, D_MODEL)),
        'ffn2_w_in': nrm(ks[19], (L, D_MODEL, 2 * D_FF), D_MODEL ** -0.5),
        'ffn2_w_out': nrm(ks[20], (L, D_FF, D_MODEL), D_FF ** -0.5),
    }


def reference(x, rel_bias_table, ffn1_norm, ffn1_w_in, ffn1_w_out, mix_norm, w_mix_in, w_mix_out,
              q_norm_a, k_norm_a, sinks_a, q_norm_b, k_norm_b,
              cmp_pos, cmp_w1, cmp_b1, cmp_w2, cmp_b2,
              ffn2_norm, ffn2_w_in, ffn2_w_out):
    for l in range(DEPTH):
        x = hybrid_layer(x, rel_bias_table, ffn1_norm[l], ffn1_w_in[l], ffn1_w_out[l],
                         mix_norm[l], w_mix_in[l], w_mix_out[l],
                         q_norm_a[l], k_norm_a[l], sinks_a[l], q_norm_b[l], k_norm_b[l],
                         cmp_pos[l], cmp_w1[l], cmp_b1[l], cmp_w2[l], cmp_b2[l],
                         ffn2_norm[l], ffn2_w_in[l], ffn2_w_out[l])
    return x
```

```python
import contextlib
import numpy as np
import concourse.bass as bass
import concourse.mybir as mybir
from concourse.bass_utils import run_bass_kernel_spmd

F32 = mybir.dt.float32
BF16 = mybir.dt.bfloat16
ALU = mybir.AluOpType
AF = mybir.ActivationFunctionType
AX = mybir.AxisListType

D = 1024
DFF = 2816
NJ = DFF // 128
SEQ = 4096
NEGM = -30000.0
ARENA_KEYS = {"pT", "s_sb", "oaug", "accb"}
EPS = 1e-6


class Op:
    __slots__ = ("q", "fn", "reads", "writes", "dsem", "waits", "inc", "idx", "hasdep")

    def __init__(self, q, fn, reads, writes, dsem):
        self.q = q
        self.fn = fn
        self.reads = tuple(reads)
        self.writes = tuple(writes)
        self.dsem = dsem
        self.waits = []
        self.inc = None
        self.hasdep = False


class Prog:
    def __init__(self, nc):
        self.nc = nc
        self.ops = []
        self.stack = contextlib.ExitStack()
        self.sems = {}
        self.nsb = 0

    def sb(self, shape, dtype, name=None):
        self.nsb += 1
        name = "sb_" + (name or f"t{self.nsb}")
        return self.stack.enter_context(self.nc.sbuf_tensor(name, list(shape), dtype))

    def ps(self, shape, dtype, name=None):
        self.nsb += 1
        name = "ps_" + (name or f"t{self.nsb}")
        return self.stack.enter_context(self.nc.psum_tensor(name, list(shape), dtype))

    def sem(self, key):
        if key not in self.sems:
            nm = "s_" + "_".join(str(k) for k in (key if isinstance(key, tuple) else (key,)))
            nm = nm.replace("(", "").replace(")", "").replace(",", "_").replace(" ", "").replace("'", "")
            self.sems[key] = self.stack.enter_context(self.nc.semaphore(nm))
        return self.sems[key]

    def op(self, q, fn, reads=(), writes=(), dsem=None):
        pr = [k for k in reads if isinstance(k, tuple) and k[0] == "pb"]
        if pr:
            reads = [k for k in reads if k not in pr]
            writes = list(writes) + [k for k in pr if k not in writes]
        if any((k if isinstance(k, str) else k[0]) in ARENA_KEYS for k in list(reads) + list(writes)):
            if "arena" not in reads and "arena" not in writes:
                reads = list(reads) + ["arena"]
        o = Op(q, fn, reads, writes, dsem)
        o.idx = len(self.ops)
        self.ops.append(o)
        return o

    def pe(self, fn, reads=(), writes=()):
        return self.op("pe", fn, reads, writes)

    def act(self, fn, reads=(), writes=()):
        return self.op("act", fn, reads, writes)

    def dve(self, fn, reads=(), writes=()):
        return self.op("dve", fn, reads, writes)

    def dma(self, q, fn, dsem, reads=(), writes=()):
        return self.op(q, fn, reads, writes, dsem=dsem)

    def schedule(self):
        last_write = {}
        readers = {}
        deps_of = []
        for o in self.ops:
            deps = {}
            for r in o.reads:
                j = last_write.get(r)
                if j is not None:
                    deps[j] = True
            for w in o.writes:
                j = last_write.get(w)
                if j is not None:
                    deps.setdefault(j, False)
                for j in readers.get(w, ()):
                    deps.setdefault(j, False)
            keep = []
            for j, raw in deps.items():
                p = self.ops[j]
                if j == o.idx:
                    continue
                if p.dsem is None and p.q == o.q and o.dsem is None:
                    if not raw or o.q == "pe":
                        continue
                keep.append(j)
            deps_of.append(keep)
            for j in keep:
                self.ops[j].hasdep = True
            for r in o.reads:
                readers.setdefault(r, []).append(o.idx)
            for w in o.writes:
                last_write[w] = o.idx
                readers[w] = []
        cnt = {}
        for o in self.ops:
            if o.dsem is not None:
                k = ("d", o.dsem)
                cnt[k] = cnt.get(k, 0) + 16
                o.inc = (k, 16, cnt[k])
            elif o.hasdep:
                k = ("e", o.q)
                cnt[k] = cnt.get(k, 0) + 1
                o.inc = (k, 1, cnt[k])
        self.final_counts = dict(cnt)
        for o in self.ops:
            if o.dsem in ("c0", "c1"):
                k = ("d", o.dsem)
                o.inc = (k, 16, cnt[k])
        known = {}
        for o in self.ops:
            kn = known.setdefault(o.q, {})
            need = {}
            for j in deps_of[o.idx]:
                k, _, v = self.ops[j].inc
                if kn.get(k, 0) >= v:
                    continue
                need[k] = max(need.get(k, 0), v)
            for k, v in need.items():
                kn[k] = v
            o.waits = list(need.items())
        for k in cnt:
            self.sem(k)

    def replay(self, q, eng):
        for o in self.ops:
            if o.q != q:
                continue
            for k, v in o.waits:
                eng.wait_ge(self.sems[k], v)
            ins = o.fn(eng)
            if o.inc is not None:
                k, step, _ = o.inc
                ins.then_inc(self.sems[k], step)

    def emit(self, final_q="sp"):
        self.schedule()
        nc = self.nc
        qs = sorted({o.q for o in self.ops} | {final_q})
        with nc.Block() as block:
            def mk(q):
                def body(eng):
                    self.replay(q, eng)
                    if q == final_q:
                        for k, v in self.final_counts.items():
                            if k[0] == "d":
                                eng.wait_ge(self.sems[k], v)
                return body
            for q in qs:
                dec = {"pe": block.tensor, "act": block.scalar, "dve": block.vector,
                       "pool": block.gpsimd, "sp": block.sync}[q]
                dec(mk(q))
        self.stack.close()


def _bucket(n):
    n = np.maximum(n, 0)
    nf = np.maximum(n, 1).astype(np.float32)
    large = 16 + (np.log(nf / np.float32(16)) / np.float32(np.log(8.0)) * np.float32(16)).astype(np.int32)
    large = np.minimum(large, 31)
    return np.where(n < 16, n, large)


def _host_tables(tbl):
    ext = np.concatenate([tbl, np.full((1, 16), NEGM, np.float32)], axis=0)
    kl = np.arange(128)[:, None]
    v = np.arange(256)[None, :]
    rel = v - kl
    idxA = np.where((rel >= 0) & (rel < 128), _bucket(rel), 32)
    idxB = np.where(rel >= 0, _bucket(rel), 32)
    TAB = np.empty((128, 16, 256), np.float32)
    for h in range(8):
        TAB[:, h, :] = ext[idxA, h]
        TAB[:, 8 + h, :] = ext[idxB, 8 + h]
    ql = np.arange(128)[:, None]
    c2 = np.arange(-8, 8)[None, :]
    relc = ql - 16 * c2 - 15
    idxC = np.where(relc >= 0, _bucket(relc), 32)
    MC = np.empty((128, 8, 3, 16), np.float32)
    for var in range(3):
        idx = idxC.copy()
        if var == 0:
            idx[:, :9] = 32
        elif var == 1:
            idx[:, 0] = 32
        for h in range(8):
            MC[:, h, var, :] = ext[idx, 8 + h]
    return TAB, MC


def _const_tables():
    kl = np.arange(128)[:, None]
    v2 = np.arange(128)[None, :]
    FE = np.where(v2 >= kl, NEGM, 0.0).astype(np.float32)
    ql = np.arange(128)[:, None]
    u = np.arange(128)[None, :]
    m = u - 62
    cur = ql // 64
    WADD = np.where(m > cur, -1e9, np.where(m >= cur - 1, 1000.0, 0.0)).astype(np.float32)
    E = np.zeros((128, SEQ), np.float32)
    k = np.arange(SEQ)
    E[k // 64, k] = 1.0
    ident = np.eye(128, dtype=np.float32)
    return FE, WADD, E, ident


def build(NB=8, taps=()):
    S = NB * 512
    nc = bass.Bass("TRN2", target_bir_lowering=False)

    def din(name, shape):
        return nc.dram_tensor(name, list(shape), F32, kind="ExternalInput").ap()

    def dscr(name, shape, dt=BF16):
        return nc.dram_tensor(name, list(shape), dt).ap()

    x_d = din("x", [S, D])
    w1in_d = din("w1in", [NJ, 128, 2048])
    w1out_d = din("w1out", [NJ, 128, D])
    w2in_d = din("w2in", [NJ, 128, 2048])
    w2out_d = din("w2out", [NJ, 128, D])
    wmi_d = din("wmi", [9, 128, 2048])
    wmo_d = din("wmo", [8, 128, D])
    gT_d = din("gT", [128, 24])
    hg_d = din("hg", [128, 256])
    tbl_d = din("tblrep", [128, 512])
    sink_d = din("sinkrep", [128, 8])
    b2_d = din("b2rep", [128, 128])
    b1_d = din("b1T", [128, 2])
    pos_d = din("posT", [128, 32])
    w1c_d = din("w1c", [128, 4096])
    w2c_d = din("w2c", [128, 128])
    tab_d = din("TAB", [128, 4096])
    mc_d = din("MC", [128, 384])
    fe_d = din("FE", [128, 128])
    wadd_d = din("WADD", [128, 128])
    e_d = din("E", [128, SEQ])
    id_d = din("ident", [128, 128])
    out_d = nc.dram_tensor("out", [S, D], F32, kind="ExternalOutput").ap()
    tap_d = {}
    for t in taps:
        w = {"x1": D, "mix": D, "x2": D}[t]
        tap_d[t] = nc.dram_tensor("tap_" + t, [S, w], F32, kind="ExternalOutput").ap()

    s_w1in = dscr("s_w1in", [NJ, 128, 2048])
    s_w1out = dscr("s_w1out", [NJ, 128, D])
    s_w2in = dscr("s_w2in", [NJ, 128, 2048])
    s_w2out = dscr("s_w2out", [NJ, 128, D])
    s_wmi = dscr("s_wmi", [9, 128, 2048])
    s_wmo = dscr("s_wmo", [8, 128, D])

    P = Prog(nc)
    xs = P.sb([128, 4, D], F32, "xs")
    hT = P.sb([128, 8, 512], BF16, "hT")
    hb = [P.sb([128, D], BF16, f"hb{i}") for i in range(2)]
    arena = P.sb([128, NJ * 512], BF16, "arena")
    aT = arena[:].rearrange("p (j t) -> p j t", j=NJ)
    NWA, NWB = 4, 4
    wA = [P.sb([128, 2048], BF16, f"wA{i}") for i in range(NWA)]
    wB = [P.sb([128, D], BF16, f"wB{i}") for i in range(NWB)]
    sg = [P.sb([128, 512], F32, f"sg{i}") for i in range(2)]
    qT_a = P.sb([128, 8, 512], BF16, "qT_a")
    qT_b = P.sb([128, 8, 512], BF16, "qT_b")
    qst = P.sb([128, 4, 1024], BF16, "qst")
    kst = P.sb([128, 4, 128], BF16, "kst")
    rawst = P.sb([128, 4, 256], BF16, "rawst")
    negmT = P.sb([128, 2, 512], BF16, "negmT")
    gates = P.sb([128, 4, 24], F32, "gates")
    mixtm = P.sb([128, 4, D], BF16, "mixtm")
    kT_a = P.sb([128, 1024], BF16, "kT_a")
    kT_w = P.sb([128, 1024], BF16, "kT_w")
    kT_s = P.sb([128, SEQ], BF16, "kT_s")
    v_a = P.sb([128, 8, 2, 65], BF16, "v_a")
    v_w = P.sb([128, 8, 2, 65], BF16, "v_w")
    v_s = P.sb([128, 32, 2, 65], BF16, "v_s")
    Esb = P.sb([128, SEQ], BF16, "Esb")
    k_cT = P.sb([128, 256], BF16, "k_cT")
    v_c = P.sb([128, 2, 2, 64], BF16, "v_c")
    rawT = P.sb([128, 2, 528], BF16, "rawT")
    W1 = P.sb([128, 32, 128], BF16, "W1")
    w2c = P.sb([128, 2, 64], BF16, "w2c")
    TABh = P.sb([128, 16, 256], BF16, "TABh")
    TABl = P.sb([128, 16, 256], BF16, "TABl")
    tabf = [P.sb([128, 256], F32, f"tabf{i}") for i in range(2)]
    FEb = P.sb([128, 128], BF16, "FEb")
    MC = P.sb([128, 8, 3, 16], F32, "MC")
    FE = P.sb([128, 128], F32, "FE")
    WADD = P.sb([128, 128], F32, "WADD")
    identf = P.sb([128, 128], F32, "identf")
    identb = P.sb([128, 128], BF16, "identb")
    gT = P.sb([128, 3, 8], F32, "gT")
    hg = P.sb([128, 4, 64], F32, "hg")
    tblr = P.sb([128, 32, 16], F32, "tblr")
    sinkr = P.sb([128, 8], F32, "sinkr")
    b2r = P.sb([128, 2, 64], F32, "b2r")
    b1T = P.sb([128, 2], F32, "b1T")
    posT = P.sb([128, 32], BF16, "posT")
    b1pw = P.sb([128, 2], F32, "b1pw")
    Ch = P.sb([128, 16], F32, "Ch")
    negC = P.sb([128, 16], F32, "negC")
    b31mC = P.sb([128, 16], F32, "b31mC")
    esink = P.sb([128, 8], F32, "esink")
    small = P.sb([128, 64], F32, "small")
    ss4 = P.sb([128, 4], F32, "ss4")
    rstd4 = P.sb([128, 4], F32, "rstd4")
    sq = P.sb([128, 4, 256], F32, "sq")
    t1 = P.sb([128, 4, 256], F32, "t1")
    ssh = P.sb([128, 16], F32, "ssh")
    rsh = P.sb([128, 16], F32, "rsh")
    NPT, NSSB, NOA = 6, 3, 2
    accb = arena[:, 0:4096].bitcast(F32).rearrange("p (t c) -> p t c", t=4)
    oaug = [arena[:, 4096 + i * 1024:4096 + (i + 1) * 1024].bitcast(F32) for i in range(NOA)]
    NSP = 4
    pTp = [arena[:, 6144 + i * 1024:6144 + (i + 1) * 1024] for i in range(NSP)]
    s4 = [P.sb([128, 4], F32, f"s4_{i}") for i in range(NOA)]
    e_t = [P.sb([128, 256], F32, f"e_t{i}") for i in range(4)]
    den2 = [P.sb([128, 2], F32, f"den2_{i}") for i in range(4)]
    rden = [P.sb([128, 1], F32, f"rden{i}") for i in range(4)]
    pcn = [P.sb([128, 256], BF16, f"pcn{i}") for i in range(4)]
    pcs = [P.sb([128, 264], F32, f"pcs{i}") for i in range(2)]
    pcT = [P.sb([128, 2, 128], BF16, f"pcT{i}") for i in range(4)]
    snear = [P.sb([128, 16], F32, f"snear{i}") for i in range(4)]
    imp = [P.sb([128, 64], F32, f"imp{i}") for i in range(2)]
    score = [P.sb([128, 64], F32, f"score{i}") for i in range(2)]
    sc2 = [P.sb([128, 64], F32, f"sc2{i}") for i in range(2)]
    mx8 = [P.sb([128, 16], F32, f"mx8{i}") for i in range(2)]
    selm = [P.sb([128, 64], F32, f"selm{i}") for i in range(2)]
    nmb = [P.sb([128, 64], BF16, f"nmb{i}") for i in range(2)]
    hid = P.sb([128, 2, 64], BF16, "hid")
    kcf = P.sb([128, 2, 64], F32, "kcf")
    kcst = P.sb([128, 128], BF16, "kcst")
    vcst = P.sb([128, 2, 64], BF16, "vcst")
    pbd = [P.ps([128, 1024], F32, f"pbd{m}") for m in range(4)]
    pb = [pbd[i // 2][:, (i % 2) * 512:(i % 2 + 1) * 512] for i in range(8)]
    pbb = [p.bitcast(BF16) for p in pb]

    st = {"tr": 0, "trd": 0, "acc": 0, "wa": 0, "wb": 0, "sg": 0, "pt": 0, "ssb": 0, "oa": 0, "cm": 0}

    def bank():
        b = st["tr"] % 6
        st["tr"] += 1
        return b

    def accbank():
        b = 6 + st["acc"] % 2
        st["acc"] += 1
        return b

    def rot(name, n):
        i = st[name] % n
        st[name] += 1
        return i

    def ld(q, dst, src, key, dsem="c0"):
        P.dma(q, lambda e, d=dst, s=src: e.dma_start(out=d, in_=s), dsem, writes=[key])

    ld("sp", gT[:].rearrange("p a b -> p (a b)"), gT_d, "gT")
    ld("sp", hg[:].rearrange("p a b -> p (a b)"), hg_d, "hg")
    ld("sp", tblr[:].rearrange("p a b -> p (a b)"), tbl_d, "tblr")
    ld("sp", sinkr[:], sink_d, "sinkr")
    ld("sp", b2r[:].rearrange("p a b -> p (a b)"), b2_d, "b2r")
    ld("sp", b1T[:], b1_d, "b1T")
    ld("sp", MC[:].rearrange("p a b c -> p (a b c)"), mc_d, "MC")
    ld("sp", FE[:], fe_d, "FE")
    ld("sp", WADD[:], wadd_d, "WADD")
    ld("sp", identf[:], id_d, "identf")
    ld("pool", posT[:], pos_d, "posT", "c1")
    ld("pool", W1[:].rearrange("p a b -> p (a b)"), w1c_d, "W1", "c1")
    ld("pool", w2c[:].rearrange("p a b -> p (a b)"), w2c_d, "w2c", "c1")
    ld("pool", identb[:], id_d, "identb", "c1")
    ld("pool", Esb[:], e_d, "Esb", "c1")

    cvn = [0]

    def wload(tb, dst_tile, slotkey, s_w, w_d, key, j):
        if tb == 0:
            P.dma("pool", lambda e: e.dma_start(out=dst_tile[:], in_=w_d[j]), slotkey, writes=[slotkey])
            k = cvn[0] % 8
            cvn[0] += 1
            P.dma("sp", lambda e: e.dma_start(out=s_w[j], in_=dst_tile[:]), ("wst", k), reads=[slotkey],
                  writes=[(key, j), ("wstslot", k)])
        else:
            P.dma("sp", lambda e: e.dma_start(out=dst_tile[:], in_=s_w[j]), slotkey, reads=[(key, j)], writes=[slotkey])

    P.dve(lambda e: e.memset(v_a[:], 1.0), writes=["v_a_all"])
    P.dve(lambda e: e.memset(v_w[:], 1.0), writes=["v_w_all"])
    P.dve(lambda e: e.memset(v_s[:], 1.0), writes=["v_s_all"])
    P.dve(lambda e: e.memset(v_c[:], 0.0), writes=["v_c"])
    P.dve(lambda e: e.memset(rawT[:], 0.0), writes=["rawT"])
    P.dve(lambda e: e.memset(negmT[:], 0.0), writes=["negmT"])
    for i in range(2):
        P.dve(lambda e, i=i: e.memset(pcs[i][:], 0.0), writes=[("pcs", i)])
    for i in range(4):
        P.dve(lambda e, i=i: e.memset(pcn[i][:], 0.0), writes=[("pcn", i)])
    P.dve(lambda e: e.memset(k_cT[:], 0.0), writes=["k_cT"])
    P.dve(lambda e: e.memset(qT_a[:], 0.0), writes=["qT" + str(id(qT_a))])
    P.dve(lambda e: e.memset(qT_b[:], 0.0), writes=["qT" + str(id(qT_b))])

    P.dve(lambda e: e.tensor_scalar(out=hg[:, 0, :], in0=hg[:, 0, :], scalar1=0.125, scalar2=None, op0=ALU.mult),
          reads=["hg"], writes=["hg"])
    P.dve(lambda e: e.tensor_scalar(out=hg[:, 2, :], in0=hg[:, 2, :], scalar1=0.125, scalar2=None, op0=ALU.mult),
          reads=["hg"], writes=["hg"])
    for br, col in ((0, 0), (2, 1)):
        P.dve(lambda e, br=br: e.tensor_tensor(out=small[:, 0:64], in0=hg[:, br, :], in1=hg[:, br + 1, :], op=ALU.mult),
              reads=["hg", "small"], writes=["small"])
        P.dve(lambda e, col=col: e.tensor_reduce(out=ssh[:, col:col + 1], in_=small[:, 0:64], axis=AX.X, op=ALU.max,
                                                 apply_absolute_value=True),
              reads=["small"], writes=["ssh"])
    P.dve(lambda e: e.tensor_reduce(out=Ch[:], in_=tblr[:].rearrange("p b h -> p h b"), axis=AX.X, op=ALU.max),
          reads=["tblr"], writes=["Ch"])
    for half, col in ((0, 0), (1, 1)):
        P.dve(lambda e, half=half, col=col: e.scalar_tensor_tensor(
            out=Ch[:, half * 8:(half + 1) * 8], in0=ssh[:, col:col + 1].to_broadcast([128, 8]), scalar=64.0,
            in1=Ch[:, half * 8:(half + 1) * 8], op0=ALU.mult, op1=ALU.add), reads=["ssh", "Ch"], writes=["Ch"])
    P.dve(lambda e: e.tensor_scalar(out=negC[:], in0=Ch[:], scalar1=-1.0, scalar2=None, op0=ALU.mult),
          reads=["Ch"], writes=["negC"])
    P.dve(lambda e: e.tensor_tensor(out=b31mC[:], in0=tblr[:, 31, :], in1=Ch[:], op=ALU.subtract),
          reads=["tblr", "Ch"], writes=["b31mC"])
    P.dve(lambda e: e.tensor_copy(out=FEb[:], in_=FE[:]), reads=["FE"], writes=["FEb"])
    for h in range(16):
        i = h % 2
        P.dma("sp", lambda e, h=h, i=i: e.dma_start(out=tabf[i][:], in_=tab_d[:, h * 256:(h + 1) * 256]), ("tabf", i),
              writes=[("tabf", i)])
        P.dve(lambda e, h=h, i=i: e.tensor_scalar(out=tabf[i][:], in0=tabf[i][:], scalar1=tblr[:, 31, h:h + 1], scalar2=None,
                                                  op0=ALU.subtract), reads=[("tabf", i), "tblr"], writes=[("tabf", i)])
        P.dve(lambda e, h=h, i=i: e.tensor_copy(out=TABh[:, h, :], in_=tabf[i][:]), reads=[("tabf", i)], writes=["TAB"])
        P.dve(lambda e, h=h, i=i: e.tensor_tensor(out=tabf[i][:], in0=tabf[i][:], in1=TABh[:, h, :], op=ALU.subtract),
              reads=[("tabf", i), "TAB"], writes=[("tabf", i)])
        P.dve(lambda e, h=h, i=i: e.tensor_copy(out=TABl[:, h, :], in_=tabf[i][:]), reads=[("tabf", i)], writes=["TAB"])
    P.dve(lambda e: e.tensor_tensor(out=esink[:], in0=sinkr[:], in1=Ch[:, 0:8], op=ALU.subtract),
          reads=["sinkr", "Ch"], writes=["esink"])
    P.act(lambda e: e.activation(out=esink[:], in_=esink[:], func=AF.Exp), reads=["esink"], writes=["esink"])
    for kv in range(2):
        def f(e, kv=kv):
            r = slice(64 * kv, 64 * kv + 64)
            ins = None
            for p in range(32):
                ins = e.matmul(pb[kv][:, 0:1], lhsT=W1[r, p, :], rhs=posT[r, p:p + 1], start=(p == 0), stop=(p == 31))
            return ins
        P.pe(f, reads=["W1", "posT"], writes=[("pb", kv)])
        P.dve(lambda e, kv=kv: e.tensor_tensor(out=b1pw[:, kv:kv + 1], in0=pb[kv][:, 0:1], in1=b1T[:, kv:kv + 1], op=ALU.add),
              reads=[("pb", kv), "b1T"], writes=["b1pw"])

    def rms_to_hT(tb, nidx):
        P.dve(lambda e: e.memset(ss4[:], 0.0), reads=["ss4"], writes=["ss4"])
        junkf = t1[:].rearrange("p t c -> p (t c)")
        for tt in range(4):
            P.act(lambda e, tt=tt: e.activation(out=junkf, in_=xs[:, tt, :], func=AF.Square, accum_out=ss4[:, tt:tt + 1]),
                  reads=[("xs", tt), "ss4"], writes=[("t1", t) for t in range(4)] + [("ss4", tt)])
            P.act(lambda e, tt=tt: e.activation(out=rstd4[:, tt:tt + 1], in_=ss4[:, tt:tt + 1], func=AF.Ln, scale=1.0 / D, bias=EPS),
                  reads=[("ss4", tt)], writes=[("rstd4", tt)])
            P.act(lambda e, tt=tt: e.activation(out=rstd4[:, tt:tt + 1], in_=rstd4[:, tt:tt + 1], func=AF.Exp, scale=-0.5),
                  reads=[("rstd4", tt)], writes=[("rstd4", tt)])

        def scale(tt):
            P.dve(lambda e: e.tensor_scalar(out=hb[tt % 2][:], in0=xs[:, tt, :], scalar1=rstd4[:, tt:tt + 1], scalar2=None,
                                            op0=ALU.mult), reads=[("xs", tt), ("rstd4", tt)], writes=[("hb", tt % 2)])

        def trev(tt):
            b = bank()

            def f(e):
                ins = None
                for c in range(8):
                    ins = e.transpose(out=pbb[b][:, c * 128:(c + 1) * 128], in_=hb[tt % 2][:, c * 128:(c + 1) * 128],
                                      identity=identb[:])
                return ins
            P.pe(f, reads=[("hb", tt % 2), "identb"], writes=[("pb", b)])
            P.dve(lambda e: e.tensor_tensor(
                out=hT[:, :, tt * 128:(tt + 1) * 128], in0=pbb[b].rearrange("p (c t) -> p c t", c=8),
                in1=gT[:, nidx, :].unsqueeze(2).to_broadcast([128, 8, 128]), op=ALU.mult),
                reads=[("pb", b), "gT"], writes=["hT"])
        scale(0)
        scale(1)
        trev(0)
        scale(2)
        trev(1)
        scale(3)
        trev(2)
        trev(3)

    def ffn(tb, s_in, s_out, kin, kout, w_in_d, w_out_d):
        for j in range(NJ):
            sl = rot("wa", NWA)
            wload(tb, wA[sl], ("wa", sl), s_in, w_in_d, kin, j)
            wv = wA[sl][:].rearrange("p (u c f) -> p u c f", u=2, c=8)
            bg, bu = bank(), bank()
            for u, b in ((0, bg), (1, bu)):
                def f(e, u=u, b=b, wv=wv):
                    ins = None
                    for c in range(8):
                        ins = e.matmul(pb[b][:, :], lhsT=wv[:, u, c, :], rhs=hT[:, c, :], start=(c == 0), stop=(c == 7))
                    return ins
                P.pe(f, reads=[("wa", sl), "hT"], writes=[("pb", b)])
            k = rot("sg", 2)
            P.act(lambda e, k=k, bg=bg: e.activation(out=sg[k][:], in_=pb[bg][:, :], func=AF.Silu),
                  reads=[("pb", bg)], writes=[("sg", k)])
            P.dve(lambda e, k=k, bu=bu, j=j: e.tensor_tensor(out=aT[:, j, :], in0=pb[bu][:, :], in1=sg[k][:], op=ALU.mult),
                  reads=[("pb", bu), ("sg", k), "arena"], writes=[("aT", j)])
        rowproj(tb, s_out, kout, NJ, lambda j, tt: aT[:, j, tt * 128:(tt + 1) * 128], lambda j: [("aT", j)], 0.5, w_out_d)

    def rowproj(tb, s_w, kw, nchunk, lhs_of, lhs_keys, scale, w_d):
        for j in range(nchunk):
            sl = rot("wb", NWB)
            wload(tb, wB[sl], ("wb", sl), s_w, w_d, kw, j)

            def f(e, sl=sl, j=j):
                ins = None
                for tt in range(4):
                    for hf in range(2):
                        ins = e.matmul(pb[tt * 2 + hf][:, :], lhsT=lhs_of(j, tt), rhs=wB[sl][:, hf * 512:(hf + 1) * 512],
                                       start=(j == 0), stop=(j == nchunk - 1))
                return ins
            P.pe(f, reads=[("wb", sl)] + lhs_keys(j), writes=[("pb", i) for i in range(8)])
        for tt in range(4):
            for hf in range(2):
                b = tt * 2 + hf
                P.dve(lambda e, tt=tt, hf=hf, b=b: e.scalar_tensor_tensor(
                    out=xs[:, tt, hf * 512:(hf + 1) * 512], in0=pb[b][:, :], scalar=scale,
                    in1=xs[:, tt, hf * 512:(hf + 1) * 512], op0=ALU.mult, op1=ALU.add),
                    reads=[("pb", b), ("xs", tt)], writes=[("xs", tt)])

    def tap(name, tb, src_of_tt, keys_of_tt):
        if name not in tap_d:
            return
        for tt in range(4):
            r0 = tb * 512 + tt * 128
            P.dma("pool", lambda e, tt=tt, r0=r0: e.dma_start(out=tap_d[name][r0:r0 + 128, :], in_=src_of_tt(tt)),
                  ("tap", name, tt), reads=keys_of_tt(tt))

    def norm_heads(srcs, nh, gidx, dst4, rkeys_l, wkeys, npart=128):
        T = len(srcs)
        n = nh * 64
        pr = slice(0, npart)
        for t, src in enumerate(srcs):
            P.act(lambda e, t=t, src=src: e.activation(out=sq[pr, t, 0:n], in_=src, func=AF.Square),
                  reads=rkeys_l[t], writes=[("sq", t)])
        P.dve(lambda e: e.tensor_reduce(out=ssh[pr, 0:T * nh].rearrange("p (t h) -> p t h", t=T),
                                        in_=sq[pr, 0:T, 0:n].rearrange("p t (h d) -> p t h d", h=nh),
                                        axis=AX.X, op=ALU.add), reads=[("sq", t) for t in range(T)], writes=["ssh"])
        P.act(lambda e: e.activation(out=rsh[pr, 0:T * nh], in_=ssh[pr, 0:T * nh], func=AF.Ln, scale=1.0 / 64, bias=EPS),
              reads=["ssh"], writes=["rsh"])
        P.act(lambda e: e.activation(out=rsh[pr, 0:T * nh], in_=rsh[pr, 0:T * nh], func=AF.Exp, scale=-0.5),
              reads=["rsh"], writes=["rsh"])
        for t, src in enumerate(srcs):
            P.dve(lambda e, t=t, src=src: e.tensor_tensor(
                out=t1[pr, t, 0:n].rearrange("p (h d) -> p h d", h=nh), in0=src.rearrange("p (h d) -> p h d", h=nh),
                in1=rsh[pr, t * nh:(t + 1) * nh].unsqueeze(2).to_broadcast([npart, nh, 64]), op=ALU.mult),
                reads=rkeys_l[t] + ["rsh"], writes=[("t1", t)])
        P.dve(lambda e: e.tensor_tensor(out=dst4.rearrange("p t (h d) -> p t h d", h=nh),
                                        in0=t1[pr, 0:T, 0:n].rearrange("p t (h d) -> p t h d", h=nh),
                                        in1=hg[pr, gidx, :].unsqueeze(1).unsqueeze(1).to_broadcast([npart, T, nh, 64]), op=ALU.mult),
              reads=[("t1", t) for t in range(T)] + ["hg"], writes=wkeys)

    def tr_to(dst, src_list, wkeys, rkeys, nrow_out=128, use_act=True, b=None):
        if b is None:
            b = bank()

        def f(e):
            ins = None
            for i, s in enumerate(src_list):
                ins = e.transpose(out=pbb[b][0:nrow_out, i * 128:(i + 1) * 128], in_=s, identity=identb[:])
            return ins
        P.pe(f, reads=rkeys + ["identb"], writes=[("pb", b)])
        n = len(src_list) * 128
        if use_act:
            P.act(lambda e: e.copy(out=dst, in_=pbb[b][0:nrow_out, 0:n]), reads=[("pb", b)], writes=wkeys)
        else:
            P.dve(lambda e: e.tensor_copy(out=dst, in_=pbb[b][0:nrow_out, 0:n]), reads=[("pb", b)], writes=wkeys)

    def pieces_for(j, c0, c1, edge):
        out = []
        for qs in range(c0 // 128, c1 // 128):
            d = qs - j
            kind = "near" if d <= 1 else ("edge" if (edge and d == 4) else "far")
            if out and out[-1][0] == kind and kind != "edge":
                out[-1][2] = (qs + 1) * 128
            else:
                out.append([kind, qs * 128, (qs + 1) * 128])
        return out

    def do_block(tb):
        t0 = tb * 512
        for tt in range(4):
            P.dma("sp", lambda e, t0=t0, tt=tt: e.dma_start(out=xs[:, tt, :], in_=x_d[t0 + tt * 128:t0 + (tt + 1) * 128, :]),
                  ("xld", tt), writes=[("xs", tt)])
        rms_to_hT(tb, 0)
        ffn(tb, s_w1in, s_w1out, "s_w1in", "s_w1out", w1in_d, w1out_d)
        tap("x1", tb, lambda tt: xs[:, tt, :], lambda tt: [("xs", tt)])
        rms_to_hT(tb, 1)
        P.dve(lambda e: e.memset(small[:, 63:64], 0.0), writes=[("aT", j) for j in range(NJ)] + ["arena"])
        P.dve(lambda e: e.memset(accb, 0.0), writes=[("accb", t_, h_) for t_ in range(4) for h_ in range(8)])
        def proj_mm(cg):
            sl = rot("wa", NWA)
            wload(tb, wA[sl], ("wa", sl), s_wmi, wmi_d, "s_wmi", cg)
            wv = wA[sl][:].rearrange("p (c f) -> p c f", c=8)
            bks = [4 * (cg % 2) + i for i in range(4)]
            for tt in range(4):
                def f(e, b=bks[tt], wv=wv, tt=tt):
                    ins = None
                    for c in range(8):
                        ins = e.matmul(pb[b][:, 0:256], lhsT=hT[:, c, tt * 128:(tt + 1) * 128], rhs=wv[:, c, :],
                                       start=(c == 0), stop=(c == 7))
                    return ins
                P.pe(f, reads=[("wa", sl), "hT"], writes=[("pb", bks[tt])])
            return bks

        def proj_post(cg, bks):
            rkl = [[("pb", bks[tt])] for tt in range(4)]
            kt0 = 4 * tb
            if cg in (0, 1, 3, 4):
                off = cg * 256 if cg < 2 else 512 + (cg - 3) * 256
                norm_heads([pb[bks[tt]][:, 0:256] for tt in range(4)], 4, 0 if cg < 2 else 2,
                           qst[:, :, off:off + 256], rkl, [("qst", cg)])
            elif cg in (2, 6, 7):
                norm_heads([pb[bks[tt]][:, 0:128] for tt in range(4)], 2, 1 if cg == 2 else 3, kst[:, :, :], rkl, ["kst"])
                if cg == 2:
                    s0 = (kt0 % 8)
                    dstk, dkey, vt_, vkey, vs0 = kT_a[:, s0 * 128:(s0 + 4) * 128], "kT" + str(id(kT_a)), v_a, "v" + str(id(v_a)), s0
                elif cg == 6:
                    dstk, dkey, vt_, vkey, vs0 = kT_s[:, kt0 * 128:(kt0 + 4) * 128], "kT" + str(id(kT_s)), v_s, "v" + str(id(v_s)), kt0
                else:
                    s0 = (kt0 % 8)
                    dstk, dkey, vt_, vkey, vs0 = kT_w[:, s0 * 128:(s0 + 4) * 128], "kT" + str(id(kT_w)), v_w, "v" + str(id(v_w)), s0
                for tt in range(4):
                    P.act(lambda e, b=bks[tt], dstv=vt_[:, vs0 + tt, :, 0:64]: e.copy(
                        out=dstv, in_=pb[b][:, 128:256].rearrange("p (g d) -> p g d", g=2)),
                        reads=rkl[tt] + ["v_a_all", "v_w_all", "v_s_all"], writes=[vkey])
                tr_to(dstk, [kst[:, tt, :] for tt in range(4)], [dkey], ["kst"], b=bks[0])
            elif cg == 5:
                for tt in range(4):
                    P.act(lambda e, b=bks[tt], tt=tt: e.copy(out=rawst[:, tt, :], in_=pb[b][:, 0:256]), reads=rkl[tt],
                          writes=[("rawst", tt)])
                for tt in range(4):
                    bb = bks[tt]

                    def f2(e, bb=bb, tt=tt):
                        e.transpose(out=pbb[bb][:, 0:128], in_=rawst[:, tt, 0:128], identity=identb[:])
                        return e.transpose(out=pbb[bb][:, 128:256], in_=rawst[:, tt, 128:256], identity=identb[:])
                    P.pe(f2, reads=[("rawst", tt), "identb"], writes=[("pb", bb)])
                    P.dve(lambda e, bb=bb, tt=tt: e.tensor_copy(
                        out=rawT[:, :, 16 + tt * 128:16 + (tt + 1) * 128],
                        in_=pbb[bb][:, 0:256].rearrange("p (g t) -> p g t", g=2)), reads=[("pb", bb)], writes=["rawT"])
            else:
                for tt in range(4):
                    P.act(lambda e, b=bks[tt], tt=tt: e.activation(out=gates[:, tt, :], in_=pb[b][:, 0:24], func=AF.Exp, scale=-1.0),
                          reads=rkl[tt], writes=["gates"])
                P.dve(lambda e: e.tensor_scalar(out=gates[:], in0=gates[:], scalar1=1.0, scalar2=None, op0=ALU.add),
                      reads=["gates"], writes=["gates"])
                P.dve(lambda e: e.reciprocal(out=gates[:], in_=gates[:]), reads=["gates"], writes=["gates"])
            if cg in (1, 4):
                off = 0 if cg == 1 else 512
                qT = qT_a if cg == 1 else qT_b
                for r in range(4):
                    b = bks[r]

                    def ftr(e, b=b, r=r):
                        ins = None
                        for tt in range(4):
                            ins = e.transpose(out=pbb[b][:, tt * 128:(tt + 1) * 128],
                                              in_=qst[:, tt, off + r * 128:off + (r + 1) * 128], identity=identb[:])
                        return ins
                    P.pe(ftr, reads=[("qst", cg - 1), ("qst", cg), "identb"], writes=[("pb", b)])
                    P.act(lambda e, b=b, r=r: e.copy(out=qT[0:64, r * 2, :], in_=pbb[b][0:64, 0:512]),
                          reads=[("pb", b)], writes=["qT" + str(id(qT))])
                    P.dve(lambda e, b=b, r=r: e.tensor_copy(out=qT[64:128, r * 2 + 1, :], in_=pbb[b][64:128, 0:512]),
                          reads=[("pb", b)], writes=["qT" + str(id(qT))])

        pend_b = proj_mm(0)
        for cg in range(9):
            nxt_b = proj_mm(cg + 1) if cg + 1 < 9 else None
            proj_post(cg, pend_b)
            pend_b = nxt_b

        for kv in range(2):
            bq = bank()

            def f(e, kv=kv, bq=bq):
                rr = slice(64 * kv, 64 * kv + 64)
                ins = None
                for g in range(2):
                    for p in range(32):
                        ins = e.matmul(pb[bq][:, g * 32:(g + 1) * 32], lhsT=W1[rr, p, :], rhs=rawT[rr, g, p:p + 497:16],
                                       start=(p == 0), stop=(p == 31), skip_group_check=True)
                return ins
            P.pe(f, reads=["W1", "rawT"], writes=[("pb", bq)])
            P.act(lambda e, kv=kv, bq=bq: e.activation(out=hid[:, kv, :], in_=pb[bq][:, 0:64], func=AF.Silu,
                                                       bias=b1pw[:, kv:kv + 1]), reads=[("pb", bq), "b1pw"], writes=[("hid", kv)])
            bo = bank()

            def f2(e, kv=kv, bo=bo):
                ins = None
                for g in range(2):
                    ins = e.matmul(pb[bo][0:32, g * 64:(g + 1) * 64], lhsT=hid[:, kv, g * 32:(g + 1) * 32], rhs=w2c[:, kv, :],
                                   start=True, stop=True, skip_group_check=True)
                return ins
            P.pe(f2, reads=[("hid", kv), "w2c"], writes=[("pb", bo)])
            if kv == 0:
                P.dve(lambda e, bo=bo: e.tensor_tensor(out=kcf[0:32, :, :], in0=pb[bo][0:32, 0:128].rearrange("p (g d) -> p g d", g=2),
                                                       in1=b2r[0:32, 0, :].unsqueeze(1).to_broadcast([32, 2, 64]), op=ALU.add),
                      reads=[("pb", bo), "b2r"], writes=["kcf"])
                norm_heads([kcf[0:32, :, :].rearrange("p g d -> p (g d)")], 2, 3, kcst[0:32, :].unsqueeze(1), [["kcf"]], ["kcst"],
                           npart=32)
                bt = bank()
                P.pe(lambda e, bt=bt: e.transpose(out=pbb[bt][:, 0:32], in_=kcst[0:32, :], identity=identb[0:32, 0:32]),
                     reads=["kcst", "identb"], writes=[("pb", bt)])
                P.act(lambda e, bt=bt, tb=tb: e.copy(out=k_cT[:, 32 * tb:32 * tb + 32], in_=pbb[bt][:, 0:32]),
                      reads=[("pb", bt)], writes=["k_cT"])
            else:
                P.dve(lambda e, bo=bo: e.tensor_tensor(out=vcst[0:32, :, :], in0=pb[bo][0:32, 0:128].rearrange("p (g d) -> p g d", g=2),
                                                       in1=b2r[0:32, 1, :].unsqueeze(1).to_broadcast([32, 2, 64]), op=ALU.add),
                      reads=[("pb", bo), "b2r"], writes=["vcst"])
                p0 = (32 * tb) % 128
                ct = (32 * tb) // 128
                P.dma("pool", lambda e, p0=p0, ct=ct: e.dma_start(out=v_c[p0:p0 + 32, ct, :, :], in_=vcst[0:32, :, :]),
                      "vc", reads=["vcst"], writes=["v_c"])
        if tb + 1 < NB:
            P.dve(lambda e: e.tensor_copy(out=rawT[:, :, 0:16], in_=rawT[:, :, 512:528]), reads=["rawT"], writes=["rawT"])

        def cmp_tt(tt):
            qt = 4 * tb + tt
            ncol = min(8 * qt + 8, 256)
            ns = max(0, 8 * qt - 8)
            var = min(qt, 2)
            moff = 8 if qt == 0 else 0
            w = ncol - ns
            lo = 0 if qt < 2 else 1
            nct = 1 if ncol <= 128 else 2
            def cmp_g(g):
                rows = slice(64 * g, 64 * g + 64)
                H = [(r, g * 4 + r, 8 + g * 4 + r) for r in range(4)]
                bs = [bank() for _ in range(4)]
                for r, hB, hidx in H:
                    P.pe(lambda e, b=bs[r], r=r: e.matmul(
                        pb[b][:, 0:ncol], lhsT=qT_b[:, r * 2 + g, tt * 128:(tt + 1) * 128], rhs=k_cT[:, 0:ncol],
                        start=True, stop=True), reads=["qT" + str(id(qT_b)), "k_cT"], writes=[("pb", bs[r])])
                    P.dve(lambda e, r=r: e.memset(den2[r][:], 0.0), reads=[("den2", r)], writes=[("den2", r)])
                for r, hB, hidx in H:
                    if ns > 1:
                        P.act(lambda e, b=bs[r], r=r, hidx=hidx: e.activation(
                            out=e_t[r][:, 1:ns], in_=pb[b][:, 1:ns], func=AF.Exp, bias=b31mC[:, hidx:hidx + 1],
                            accum_out=den2[r][:, 0:1]), reads=[("pb", bs[r]), "b31mC", ("den2", r)],
                            writes=[("e_t", r), ("den2", r)])
                    P.dve(lambda e, b=bs[r], r=r, hB=hB: e.tensor_tensor(
                        out=snear[r][:, 0:w], in0=pb[b][:, ns:ncol], in1=MC[:, hB, var, moff:moff + w], op=ALU.add),
                        reads=[("pb", bs[r]), "MC"], writes=[("snear", r)])
                for r, hB, hidx in H:
                    P.act(lambda e, r=r, hidx=hidx: e.activation(
                        out=e_t[r][:, ns:ncol], in_=snear[r][:, 0:w], func=AF.Exp, bias=negC[:, hidx:hidx + 1],
                        accum_out=den2[r][:, 1:2]), reads=[("snear", r), "negC", ("den2", r)],
                        writes=[("e_t", r), ("den2", r)])
                for r, hB, hidx in H:
                    P.dve(lambda e, r=r: e.tensor_scalar(out=rden[r][:], in0=den2[r][:, 0:1], scalar1=den2[r][:, 1:2],
                                                         scalar2=1e-30, op0=ALU.add, op1=ALU.max),
                          reads=[("den2", r)], writes=[("rden", r)])
                for r, hB, hidx in H:
                    P.dve(lambda e, r=r: e.reciprocal(out=rden[r][:], in_=rden[r][:]), reads=[("rden", r)], writes=[("rden", r)])
                for r, hB, hidx in H:
                    P.dve(lambda e, r=r: e.tensor_scalar(
                        out=pcn[r][:, lo:ncol], in0=e_t[r][:, lo:ncol], scalar1=rden[r][:, 0:1], scalar2=None, op0=ALU.mult),
                        reads=[("e_t", r), ("rden", r)], writes=[("pcn", r)])
                bts = []
                for r, hB, hidx in H:
                    bt = bank()
                    bts.append(bt)

                    def f(e, bt=bt, r=r):
                        ins = None
                        for c in range(nct):
                            ins = e.transpose(out=pbb[bt][:, c * 128:(c + 1) * 128], in_=pcn[r][:, c * 128:(c + 1) * 128],
                                              identity=identb[:])
                        return ins
                    P.pe(f, reads=[("pcn", r), "identb"], writes=[("pb", bt)])
                    P.act(lambda e, bt=bt, r=r: e.copy(out=pcT[r][:, 0:nct, :],
                                                       in_=pbb[bt][:, 0:nct * 128].rearrange("p (c q) -> p c q", c=nct)),
                          reads=[("pb", bt)], writes=[("pcT", r)])
                for r, hB, hidx in H:
                    if r == 0:
                        P.dve(lambda e, r=r: e.tensor_scalar(
                            out=pcs[g][:, lo:ncol], in0=e_t[r][:, lo:ncol], scalar1=rden[r][:, 0:1], scalar2=None, op0=ALU.mult),
                            reads=[("e_t", r), ("rden", r)], writes=[("pcs", g)])
                    else:
                        P.dve(lambda e, r=r: e.scalar_tensor_tensor(
                            out=pcs[g][:, lo:ncol], in0=e_t[r][:, lo:ncol], scalar=rden[r][:, 0:1], in1=pcs[g][:, lo:ncol],
                            op0=ALU.mult, op1=ALU.add), reads=[("e_t", r), ("rden", r), ("pcs", g)], writes=[("pcs", g)])
                for r, hB, hidx in H:
                    bo = bank()

                    def f2(e, bo=bo, r=r):
                        ins = None
                        for c in range(nct):
                            ins = e.matmul(pb[bo][:, 0:64], lhsT=pcT[r][:, c, :], rhs=v_c[:, c, g, :], start=(c == 0),
                                           stop=(c == nct - 1))
                        return ins
                    P.pe(f2, reads=[("pcT", r), "v_c"], writes=[("pb", bo)])
                    P.dve(lambda e, bo=bo, hB=hB: e.scalar_tensor_tensor(
                        out=accb[:, tt, hB * 64:(hB + 1) * 64], in0=pb[bo][:, 0:64], scalar=gates[:, tt, hB:hB + 1],
                        in1=accb[:, tt, hB * 64:(hB + 1) * 64], op0=ALU.mult, op1=ALU.add),
                        reads=[("pb", bo), "gates", ("accb", tt, hB)], writes=[("accb", tt, hB)])
            for g in range(2):
                cmp_g(g)
            for stage in range(10):
                for g in range(2):
                    if stage == 0:
                        P.dve(lambda e, g=g: e.tensor_reduce(out=imp[g][:], in_=pcs[g][:, 0:256].rearrange("p (n f) -> p n f", f=4),
                                                             axis=AX.X, op=ALU.add), reads=[("pcs", g)], writes=[("imp", g)])
                    elif stage == 1:
                        P.dve(lambda e, g=g: e.tensor_tensor(
                            out=imp[g][:], in0=imp[g][:], in1=pcs[g][:, 4:260].rearrange("p (n f) -> p n f", f=4)[:, :, 0],
                            op=ALU.add), reads=[("pcs", g), ("imp", g)], writes=[("imp", g)])
                    elif stage == 2:
                        P.dve(lambda e, g=g: e.tensor_tensor(out=score[g][:], in0=imp[g][:], in1=WADD[:, 62 - 2 * qt:126 - 2 * qt],
                                                             op=ALU.add), reads=[("imp", g), "WADD"], writes=[("score", g)])
                    elif stage == 3:
                        if qt >= 1:
                            P.dve(lambda e, g=g: e.tensor_scalar(out=score[g][:, 0:1], in0=score[g][:, 0:1], scalar1=1000.0,
                                                                 scalar2=None, op0=ALU.add), reads=[("score", g)], writes=[("score", g)])
                    elif stage == 4:
                        P.dve(lambda e, g=g: e.max(out=mx8[g][:, 0:8], in_=score[g][:]), reads=[("score", g)], writes=[("mx8", g)])
                    elif stage == 5:
                        P.dve(lambda e, g=g: e.match_replace(out=sc2[g][:], in_to_replace=mx8[g][:, 0:8], in_values=score[g][:],
                                                             imm_value=-3.0e38), reads=[("mx8", g), ("score", g)], writes=[("sc2", g)])
                    elif stage == 6:
                        P.dve(lambda e, g=g: e.max(out=mx8[g][:, 8:16], in_=sc2[g][:]), reads=[("sc2", g), ("mx8", g)],
                              writes=[("mx8b", g)])
                    elif stage == 7:
                        P.dve(lambda e, g=g: e.tensor_scalar(out=selm[g][:], in0=score[g][:], scalar1=mx8[g][:, 15:16], scalar2=None,
                                                             op0=ALU.is_ge), reads=[("score", g), ("mx8b", g)], writes=[("selm", g)])
                    elif stage == 8:
                        P.dve(lambda e, g=g: e.scalar_tensor_tensor(out=selm[g][:], in0=score[g][:], scalar=-5.0e8, in1=selm[g][:],
                                                                    op0=ALU.is_gt, op1=ALU.mult),
                              reads=[("score", g), ("selm", g)], writes=[("selm", g)])
                    else:
                        P.dve(lambda e, g=g: e.tensor_scalar(out=nmb[g][:], in0=selm[g][:], scalar1=-1.0, scalar2=-NEGM,
                                                             op0=ALU.add, op1=ALU.mult), reads=[("selm", g)], writes=[("nmb", g)])
            for g in range(2):
                bt = bank()
                P.pe(lambda e, bt=bt, g=g: e.transpose(out=pbb[bt][0:64, 0:128], in_=nmb[g][:], identity=identb[:]),
                     reads=[("nmb", g), "identb"], writes=[("pb", bt)])
                P.act(lambda e, bt=bt, g=g: e.copy(out=negmT[0:64, g, tt * 128:(tt + 1) * 128], in_=pbb[bt][0:64, 0:128]),
                      reads=[("pb", bt)], writes=[("negmT", g)])

        DPIPE = 3
        tiles = []
        heads = []
        for g in range(2):
            for r in range(4):
                hA = g * 4 + r
                kts = [kt for kt in range(4 * tb - 1, 4 * tb + 4) if kt >= 0]
                heads.append(dict(kind="A", g=g, r=r, hidx=hA, h=hA, kts=kts, kT=kT_a, ks=lambda kt: kt % 8, vt=v_a,
                                  vs=lambda kt: kt % 8, qT=qT_a, edge=False, col_end=256, masked=False))
        for g in range(2):
            for r in range(4):
                hB = g * 4 + r
                kts = [kt for kt in range(4 * tb - 4, 4 * tb + 4) if kt >= 0]
                heads.append(dict(kind="W", g=g, r=r, hidx=8 + hB, h=hB, kts=kts, kT=kT_w, ks=lambda kt: kt % 8, vt=v_w,
                                  vs=lambda kt: kt % 8, qT=qT_b, edge=True, col_end=640, masked=False))
        for g in range(2):
            for r in range(4):
                hB = g * 4 + r
                heads.append(dict(kind="S", g=g, r=r, hidx=8 + hB, h=hB, kts=list(range(0, 4 * tb + 4)), kT=kT_s,
                                  ks=lambda kt: kt, vt=v_s, vs=lambda kt: kt, qT=qT_b, edge=False, col_end=100000, masked=True))
        for hi, hd in enumerate(heads):
            kl = hd["kts"]
            idx = 0
            while idx < len(kl):
                kt = kl[idx]
                if hd["kind"] == "S" and idx + 1 < len(kl) and kl[idx + 1] <= 4 * tb - 2:
                    tiles.append((hi, idx, [kt, kl[idx + 1]]))
                    idx += 2
                else:
                    tiles.append((hi, idx, [kt]))
                    idx += 1
        NT = len(tiles)
        tinfo = {}

        def stage_qk(i):
            hi, idx, kts_u = tiles[i]
            hd = heads[hi]
            g, r, hidx = hd["g"], hd["r"], hd["hidx"]
            kT, qT = hd["kT"], hd["qT"]
            if idx == 0:
                hd["ba"] = accbank()
            bd = rot("trd", 3)
            sp = i % NSP
            masked = hd["masked"]
            rk = ["kT" + str(id(kT)), "qT" + str(id(qT)), "TAB", "FEb", "identb"]
            if masked:
                rk += ["Esb", ("negmT", g)]
            if len(kts_u) == 2:
                tinfo[i] = (0, 512, 512, sp)

                def f2(e):
                    ins = None
                    for t, kt in enumerate(kts_u):
                        ksl = hd["ks"](kt)
                        dst = pbd[bd][:, t * 512:(t + 1) * 512]
                        e.matmul(dst, lhsT=kT[:, ksl * 128:(ksl + 1) * 128], rhs=qT[:, r * 2 + g, 0:512], start=True, stop=False)
                        ins = e.matmul(dst, lhsT=Esb[:, kt * 128:(kt + 1) * 128], rhs=negmT[:, g, 0:512], start=False, stop=True,
                                       skip_group_check=True)
                    return ins
                P.pe(f2, reads=rk, writes=[("pb", 2 * bd), ("pb", 2 * bd + 1)])
                P.act(lambda e: e.activation(out=pTp[sp][:, 0:1024], in_=pbd[bd][:, 0:1024], func=AF.Exp,
                                             bias=b31mC[:, hidx:hidx + 1]),
                      reads=[("pb", 2 * bd), ("pb", 2 * bd + 1), "b31mC"], writes=[("pT", sp)])
                return
            kt = kts_u[0]
            j = kt - 4 * tb
            c0 = max(0, 128 * j)
            c1 = min(512, 128 * j + hd["col_end"])
            n = c1 - c0
            bs = 2 * bd
            ksl = hd["ks"](kt)
            tinfo[i] = (c0, c1, n, sp)
            pcs_ = pieces_for(j, c0, c1, hd["edge"])
            extra = []
            for kind, a, b_ in pcs_:
                lo_, hi_ = a - c0, b_ - c0
                if kind == "near":
                    v0 = a - 128 * j
                    extra.append((lo_, hi_, TABh[:, hidx, v0:v0 + (hi_ - lo_)]))
                    extra.append((lo_, hi_, TABl[:, hidx, v0:v0 + (hi_ - lo_)]))
                elif kind == "edge":
                    extra.append((lo_, hi_, FEb[:, 0:hi_ - lo_]))

            def f(e):
                last_is_qk = (not masked) and (not extra)
                ins = e.matmul(pb[bs][:, 0:n], lhsT=kT[:, ksl * 128:(ksl + 1) * 128], rhs=qT[:, r * 2 + g, c0:c1],
                               start=True, stop=last_is_qk)
                if masked:
                    ins = e.matmul(pb[bs][:, 0:n], lhsT=Esb[:, kt * 128:(kt + 1) * 128], rhs=negmT[:, g, c0:c1],
                                   start=False, stop=(not extra), skip_group_check=True)
                for xi, (lo_, hi_, ap_) in enumerate(extra):
                    ins = e.matmul(pb[bs][:, lo_:hi_], lhsT=identb[:], rhs=ap_, start=False, stop=(xi == len(extra) - 1),
                                   skip_group_check=True)
                return ins
            P.pe(f, reads=rk, writes=[("pb", bs)])
            P.act(lambda e: e.activation(out=pTp[sp][:, 0:n], in_=pb[bs][:, 0:n], func=AF.Exp, bias=b31mC[:, hidx:hidx + 1]),
                  reads=[("pb", bs), "b31mC"], writes=[("pT", sp)])

        def stage_pv(i):
            hi, idx, kts_u = tiles[i]
            hd = heads[hi]
            c0, c1, n, sp = tinfo[i]
            ba = hd["ba"]
            vt, g = hd["vt"], hd["g"]
            nk = len(hd["kts"])
            last = (idx + len(kts_u) - 1 == nk - 1)

            def f(e):
                ins = None
                for t, kt in enumerate(kts_u):
                    ins = e.matmul(pb[ba][0:65, c0:c1], lhsT=vt[:, hd["vs"](kt), g, :], rhs=pTp[sp][:, t * 512:t * 512 + n],
                                   start=(idx + t == 0), stop=(idx + t == nk - 1), skip_group_check=True)
                return ins
            P.pe(f, reads=[("pT", sp), "v" + str(id(vt))], writes=[("pb", ba)])
            return last

        def fin_copy(hi):
            hd = heads[hi]
            io = rot("oa", NOA)
            hd["io"] = io
            ba = hd["ba"]
            if hd["kind"] == "S":
                P.dve(lambda e: e.tensor_copy(out=oaug[io][0:65, :], in_=pb[ba][0:65, :]), reads=[("pb", ba)], writes=[("oaug", io)])
            else:
                P.act(lambda e: e.copy(out=oaug[io][0:65, :], in_=pb[ba][0:65, :]), reads=[("pb", ba)], writes=[("oaug", io)])

        def fin_rest(hi):
            hd = heads[hi]
            io = hd["io"]
            h, kind = hd["h"], hd["kind"]
            b = bank()

            def f(e):
                ins = None
                for tt in range(4):
                    ins = e.transpose(out=pb[b][:, tt * 65:(tt + 1) * 65], in_=oaug[io][0:65, tt * 128:(tt + 1) * 128],
                                      identity=identf[0:65, 0:65])
                return ins
            P.pe(f, reads=[("oaug", io), "identf"], writes=[("pb", b)])
            pv = pb[b][:, 0:260].rearrange("p (t c) -> p t c", t=4)
            if kind == "A":
                P.dve(lambda e: e.tensor_scalar(out=s4[io][:], in0=pv[:, :, 64], scalar1=esink[:, h:h + 1], scalar2=None,
                                                op0=ALU.add), reads=[("pb", b), "esink"], writes=[("s4", io)])
                P.dve(lambda e: e.reciprocal(out=s4[io][:], in_=s4[io][:]), reads=[("s4", io)], writes=[("s4", io)])
                for tt in range(4):
                    P.dve(lambda e, tt=tt: e.tensor_scalar(out=mixtm[:, tt, h * 64:(h + 1) * 64], in0=pv[:, tt, 0:64],
                                                           scalar1=s4[io][:, tt:tt + 1], scalar2=None, op0=ALU.mult),
                          reads=[("pb", b), ("s4", io)], writes=[("mixtm", tt, h)])
            else:
                branch = 2 if kind == "W" else 1
                P.dve(lambda e: e.tensor_scalar(out=s4[io][:], in0=pv[:, :, 64], scalar1=1e-30, scalar2=None, op0=ALU.max),
                      reads=[("pb", b)], writes=[("s4", io)])
                P.dve(lambda e: e.reciprocal(out=s4[io][:], in_=s4[io][:]), reads=[("s4", io)], writes=[("s4", io)])
                P.dve(lambda e: e.tensor_tensor(out=s4[io][:], in0=s4[io][:], in1=gates[:, :, branch * 8 + h], op=ALU.mult),
                      reads=[("s4", io), "gates"], writes=[("s4", io)])
                for tt in range(4):
                    if kind == "W":
                        P.dve(lambda e, tt=tt: e.scalar_tensor_tensor(
                            out=accb[:, tt, h * 64:(h + 1) * 64], in0=pv[:, tt, 0:64], scalar=s4[io][:, tt:tt + 1],
                            in1=accb[:, tt, h * 64:(h + 1) * 64], op0=ALU.mult, op1=ALU.add),
                            reads=[("pb", b), ("s4", io), ("accb", tt, h)], writes=[("accb", tt, h)])
                    else:
                        P.dve(lambda e, tt=tt: e.scalar_tensor_tensor(
                            out=mixtm[:, tt, 512 + h * 64:512 + (h + 1) * 64], in0=pv[:, tt, 0:64],
                            scalar=s4[io][:, tt:tt + 1], in1=accb[:, tt, h * 64:(h + 1) * 64], op0=ALU.mult, op1=ALU.add),
                            reads=[("pb", b), ("s4", io), ("accb", tt, h)], writes=[("mixtm", tt, 8 + h)])

        pending = []
        n1 = sum(len(hd["kts"]) for hd in heads if hd["kind"] != "S")
        inject = {max(1, (k + 1) * n1 // 5): k for k in range(4)}
        def mix_tr(c):
            tr_to(hT[:, c, :], [mixtm[:, tt, c * 128:(c + 1) * 128] for tt in range(4)], ["hT"],
                  [("mixtm", tt, h) for tt in range(4) for h in (2 * c, 2 * c + 1)], use_act=False)
        early = {n1 + 8 + 3 * c: c for c in range(4)} if NT > n1 + 24 else {}
        for i in range(NT + DPIPE + 3):
            if i in inject:
                cmp_tt(inject[i])
            if i in early:
                mix_tr(early[i])
            if i < NT:
                stage_qk(i)
            if DPIPE <= i < NT + DPIPE:
                if stage_pv(i - DPIPE):
                    hi_ = tiles[i - DPIPE][0]
                    fin_copy(hi_)
                    pending.append((i + 2, hi_))
            while pending and pending[0][0] <= i:
                fin_rest(pending.pop(0)[1])
        while pending:
            fin_rest(pending.pop(0)[1])
        tap("mix", tb, lambda tt: mixtm[:, tt, :], lambda tt: [("mixtm", tt, h) for h in range(16)])
        for c in range(8):
            if c < 4 and early:
                continue
            tr_to(hT[:, c, :], [mixtm[:, tt, c * 128:(c + 1) * 128] for tt in range(4)], ["hT"],
                  [("mixtm", tt, h) for tt in range(4) for h in (2 * c, 2 * c + 1)], use_act=(c % 2 == 0))
        rowproj(tb, s_wmo, "s_wmo", 8, lambda j, tt: hT[:, j, tt * 128:(tt + 1) * 128], lambda j: ["hT"], 1.0, wmo_d)
        tap("x2", tb, lambda tt: xs[:, tt, :], lambda tt: [("xs", tt)])
        rms_to_hT(tb, 2)
        P.dve(lambda e: e.memset(small[:, 62:63], 0.0), reads=["arena"], writes=["arena"])
        ffn(tb, s_w2in, s_w2out, "s_w2in", "s_w2out", w2in_d, w2out_d)
        for tt in range(4):
            r0 = t0 + tt * 128
            P.dma("pool", lambda e, tt=tt, r0=r0: e.dma_start(out=out_d[r0:r0 + 128, :], in_=xs[:, tt, :]), "ost",
                  reads=[("xs", tt)])
    for tb in range(NB):
        do_block(tb)
    P.emit()
    return nc


def _perm_cols():
    QA, KVA, QB = 512, 128, 512
    OFF_KA, OFF_VA, OFF_QB, OFF_KVB, OFF_GB = 512, 640, 768, 1280, 2048
    cols = []
    for r in range(4):
        for g in range(2):
            h = g * 4 + r
            cols += list(range(h * 64, h * 64 + 64))
    cols += list(range(OFF_KA, OFF_KA + 256))
    for r in range(4):
        for g in range(2):
            h = g * 4 + r
            cols += list(range(OFF_QB + h * 64, OFF_QB + h * 64 + 64))
    kvb = lambda i, g: list(range(OFF_KVB + i * 128 + g * 64, OFF_KVB + i * 128 + g * 64 + 64))
    cols += kvb(0, 0) + kvb(1, 0) + kvb(0, 1) + kvb(1, 1)
    cols += list(range(OFF_KVB + 256, OFF_KVB + 512))
    cols += list(range(OFF_KVB + 512, OFF_KVB + 768))
    cols += list(range(OFF_GB, OFF_GB + 24))
    return np.array(cols)


def prep_shared(inp):
    f = np.float32
    out = {}

    def win(w):
        w = np.asarray(w, f).reshape(8, 128, 2, NJ, 128)
        return np.ascontiguousarray(w.transpose(3, 1, 2, 0, 4)).reshape(NJ, 128, 2048)

    out["w1in"] = win(inp["ffn1_w_in"][0])
    out["w2in"] = win(inp["ffn2_w_in"][0])
    out["w1out"] = np.ascontiguousarray(np.asarray(inp["ffn1_w_out"][0], f).reshape(NJ, 128, D))
    out["w2out"] = np.ascontiguousarray(np.asarray(inp["ffn2_w_out"][0], f).reshape(NJ, 128, D))
    wm = np.asarray(inp["w_mix_in"][0], f)[:, _perm_cols()]
    wmp = np.zeros((1024, 9 * 256), f)
    wmp[:, :wm.shape[1]] = wm
    wmp = wmp.reshape(8, 128, 9, 256)
    out["wmi"] = np.ascontiguousarray(wmp.transpose(2, 1, 0, 3)).reshape(9, 128, 2048)
    out["wmo"] = np.ascontiguousarray(np.asarray(inp["w_mix_out"][0], f).reshape(8, 128, D))
    g3 = np.stack([inp["ffn1_norm"][0], inp["mix_norm"][0], inp["ffn2_norm"][0]]).astype(f)
    out["gT"] = np.ascontiguousarray(g3.reshape(3, 8, 128).transpose(2, 0, 1)).reshape(128, 24)
    hg = np.stack([inp["q_norm_a"][0], inp["k_norm_a"][0], inp["q_norm_b"][0], inp["k_norm_b"][0]]).astype(f)
    out["hg"] = np.ascontiguousarray(np.broadcast_to(hg.reshape(1, 256), (128, 256)))
    tbl = np.asarray(inp["rel_bias_table"], f)
    out["tblrep"] = np.ascontiguousarray(np.broadcast_to(tbl.reshape(1, 512), (128, 512)))
    out["sinkrep"] = np.ascontiguousarray(np.broadcast_to(np.asarray(inp["sinks_a"][0], f).reshape(1, 8), (128, 8)))
    out["b2rep"] = np.ascontiguousarray(np.broadcast_to(np.asarray(inp["cmp_b2"][0], f).reshape(1, 128), (128, 128)))
    out["b1T"] = np.ascontiguousarray(np.asarray(inp["cmp_b1"][0], f).T)
    pos = np.asarray(inp["cmp_pos"][0], f)
    out["posT"] = np.ascontiguousarray(pos.transpose(0, 2, 1)).reshape(128, 32)
    w1 = np.asarray(inp["cmp_w1"][0], f).reshape(2, 32, 64, 128)
    out["w1c"] = np.ascontiguousarray(w1.transpose(0, 2, 1, 3)).reshape(128, 4096)
    w2 = np.asarray(inp["cmp_w2"][0], f)
    out["w2c"] = np.ascontiguousarray(w2.transpose(1, 0, 2)).reshape(128, 128)
    TAB, MC = _host_tables(tbl)
    out["TAB"] = TAB.reshape(128, 4096)
    out["MC"] = MC.reshape(128, 384)
    FE, WADD, E, ident = _const_tables()
    out["FE"], out["WADD"], out["E"], out["ident"] = FE, WADD, E, ident
    return out


_CACHE = {}


def kernel(**inputs):
    x = np.asarray(inputs["x"], np.float32)
    B = x.shape[0]
    shared = prep_shared(inputs)
    if "nc" not in _CACHE:
        _CACHE["nc"] = build(8)
    nc = _CACHE["nc"]
    in_maps = []
    for b in range(B):
        m = dict(shared)
        m["x"] = np.ascontiguousarray(x[b])
        in_maps.append(m)
    res = run_bass_kernel_spmd(nc, in_maps, core_ids=list(range(B)))
    return np.stack([np.asarray(r["out"], np.float32) for r in res.results], axis=0)
```

```python
import contextlib
import numpy as np
import concourse.bass as bass
import concourse.mybir as mybir
from concourse.bass_utils import run_bass_kernel_spmd

F32 = mybir.dt.float32
BF16 = mybir.dt.bfloat16
ALU = mybir.AluOpType
AF = mybir.ActivationFunctionType
AX = mybir.AxisListType

D = 1024
DFF = 2816
NJ = DFF // 128
SEQ = 4096
NEGM = -30000.0
ARENA_KEYS = {"pT", "s_sb", "oaug", "accb"}
EPS = 1e-6


class Op:
    __slots__ = ("q", "fn", "reads", "writes", "dsem", "waits", "inc", "idx", "hasdep")

    def __init__(self, q, fn, reads, writes, dsem):
        self.q = q
        self.fn = fn
        self.reads = tuple(reads)
        self.writes = tuple(writes)
        self.dsem = dsem
        self.waits = []
        self.inc = None
        self.hasdep = False


class Prog:
    def __init__(self, nc):
        self.nc = nc
        self.ops = []
        self.stack = contextlib.ExitStack()
        self.sems = {}
        self.nsb = 0

    def sb(self, shape, dtype, name=None):
        self.nsb += 1
        name = "sb_" + (name or f"t{self.nsb}")
        return self.stack.enter_context(self.nc.sbuf_tensor(name, list(shape), dtype))

    def ps(self, shape, dtype, name=None):
        self.nsb += 1
        name = "ps_" + (name or f"t{self.nsb}")
        return self.stack.enter_context(self.nc.psum_tensor(name, list(shape), dtype))

    def sem(self, key):
        if key not in self.sems:
            nm = "s_" + "_".join(str(k) for k in (key if isinstance(key, tuple) else (key,)))
            nm = nm.replace("(", "").replace(")", "").replace(",", "_").replace(" ", "").replace("'", "")
            self.sems[key] = self.stack.enter_context(self.nc.semaphore(nm))
        return self.sems[key]

    def op(self, q, fn, reads=(), writes=(), dsem=None):
        pr = [k for k in reads if isinstance(k, tuple) and k[0] == "pb"]
        if pr:
            reads = [k for k in reads if k not in pr]
            writes = list(writes) + [k for k in pr if k not in writes]
        if any((k if isinstance(k, str) else k[0]) in ARENA_KEYS for k in list(reads) + list(writes)):
            if "arena" not in reads and "arena" not in writes:
                reads = list(reads) + ["arena"]
        o = Op(q, fn, reads, writes, dsem)
        o.idx = len(self.ops)
        self.ops.append(o)
        return o

    def pe(self, fn, reads=(), writes=()):
        return self.op("pe", fn, reads, writes)

    def act(self, fn, reads=(), writes=()):
        return self.op("act", fn, reads, writes)

    def dve(self, fn, reads=(), writes=()):
        return self.op("dve", fn, reads, writes)

    def dma(self, q, fn, dsem, reads=(), writes=()):
        return self.op(q, fn, reads, writes, dsem=dsem)

    def schedule(self):
        last_write = {}
        readers = {}
        deps_of = []
        for o in self.ops:
            deps = {}
            for r in o.reads:
                j = last_write.get(r)
                if j is not None:
                    deps[j] = True
            for w in o.writes:
                j = last_write.get(w)
                if j is not None:
                    deps.setdefault(j, False)
                for j in readers.get(w, ()):
                    deps.setdefault(j, False)
            keep = []
            for j, raw in deps.items():
                p = self.ops[j]
                if j == o.idx:
                    continue
                if p.dsem is None and p.q == o.q and o.dsem is None:
                    if not raw or o.q == "pe":
                        continue
                keep.append(j)
            deps_of.append(keep)
            for j in keep:
                self.ops[j].hasdep = True
            for r in o.reads:
                readers.setdefault(r, []).append(o.idx)
            for w in o.writes:
                last_write[w] = o.idx
                readers[w] = []
        cnt = {}
        for o in self.ops:
            if o.dsem is not None:
                k = ("d", o.dsem)
                cnt[k] = cnt.get(k, 0) + 16
                o.inc = (k, 16, cnt[k])
            elif o.hasdep:
                k = ("e", o.q)
                cnt[k] = cnt.get(k, 0) + 1
                o.inc = (k, 1, cnt[k])
        self.final_counts = dict(cnt)
        for o in self.ops:
            if o.dsem in ("c0", "c1"):
                k = ("d", o.dsem)
                o.inc = (k, 16, cnt[k])
        known = {}
        for o in self.ops:
            kn = known.setdefault(o.q, {})
            need = {}
            for j in deps_of[o.idx]:
                k, _, v = self.ops[j].inc
                if kn.get(k, 0) >= v:
                    continue
                need[k] = max(need.get(k, 0), v)
            for k, v in need.items():
                kn[k] = v
            o.waits = list(need.items())
        for k in cnt:
            self.sem(k)

    def replay(self, q, eng):
        for o in self.ops:
            if o.q != q:
                continue
            for k, v in o.waits:
                eng.wait_ge(self.sems[k], v)
            ins = o.fn(eng)
            if o.inc is not None:
                k, step, _ = o.inc
                ins.then_inc(self.sems[k], step)

    def emit(self, final_q="sp"):
        self.schedule()
        nc = self.nc
        qs = sorted({o.q for o in self.ops} | {final_q})
        with nc.Block() as block:
            def mk(q):
                def body(eng):
                    self.replay(q, eng)
                    if q == final_q:
                        for k, v in self.final_counts.items():
                            if k[0] == "d":
                                eng.wait_ge(self.sems[k], v)
                return body
            for q in qs:
                dec = {"pe": block.tensor, "act": block.scalar, "dve": block.vector,
                       "pool": block.gpsimd, "sp": block.sync}[q]
                dec(mk(q))
        self.stack.close()


def _bucket(n):
    n = np.maximum(n, 0)
    nf = np.maximum(n, 1).astype(np.float32)
    large = 16 + (np.log(nf / np.float32(16)) / np.float32(np.log(8.0)) * np.float32(16)).astype(np.int32)
    large = np.minimum(large, 31)
    return np.where(n < 16, n, large)


def _host_tables(tbl):
    ext = np.concatenate([tbl, np.full((1, 16), NEGM, np.float32)], axis=0)
    kl = np.arange(128)[:, None]
    v = np.arange(256)[None, :]
    rel = v - kl
    idxA = np.where((rel >= 0) & (rel < 128), _bucket(rel), 32)
    idxB = np.where(rel >= 0, _bucket(rel), 32)
    TAB = np.empty((128, 16, 256), np.float32)
    for h in range(8):
        TAB[:, h, :] = ext[idxA, h]
        TAB[:, 8 + h, :] = ext[idxB, 8 + h]
    ql = np.arange(128)[:, None]
    c2 = np.arange(-8, 8)[None, :]
    relc = ql - 16 * c2 - 15
    idxC = np.where(relc >= 0, _bucket(relc), 32)
    MC = np.empty((128, 8, 3, 16), np.float32)
    for var in range(3):
        idx = idxC.copy()
        if var == 0:
            idx[:, :9] = 32
        elif var == 1:
            idx[:, 0] = 32
        for h in range(8):
            MC[:, h, var, :] = ext[idx, 8 + h]
    return TAB, MC


def _const_tables():
    kl = np.arange(128)[:, None]
    v2 = np.arange(128)[None, :]
    FE = np.where(v2 >= kl, NEGM, 0.0).astype(np.float32)
    ql = np.arange(128)[:, None]
    u = np.arange(128)[None, :]
    m = u - 62
    cur = ql // 64
    WADD = np.where(m > cur, -1e9, np.where(m >= cur - 1, 1000.0, 0.0)).astype(np.float32)
    E = np.zeros((128, SEQ), np.float32)
    k = np.arange(SEQ)
    E[k // 64, k] = 1.0
    ident = np.eye(128, dtype=np.float32)
    return FE, WADD, E, ident


def build(NB=8, taps=()):
    S = NB * 512
    nc = bass.Bass("TRN2", target_bir_lowering=False)

    def din(name, shape):
        return nc.dram_tensor(name, list(shape), F32, kind="ExternalInput").ap()

    def dscr(name, shape, dt=BF16):
        return nc.dram_tensor(name, list(shape), dt).ap()

    x_d = din("x", [S, D])
    w1in_d = din("w1in", [NJ, 128, 2048])
    w1out_d = din("w1out", [NJ, 128, D])
    w2in_d = din("w2in", [NJ, 128, 2048])
    w2out_d = din("w2out", [NJ, 128, D])
    wmi_d = din("wmi", [9, 128, 2048])
    wmo_d = din("wmo", [8, 128, D])
    gT_d = din("gT", [128, 24])
    hg_d = din("hg", [128, 256])
    tbl_d = din("tblrep", [128, 512])
    sink_d = din("sinkrep", [128, 8])
    b2_d = din("b2rep", [128, 128])
    b1_d = din("b1T", [128, 2])
    pos_d = din("posT", [128, 32])
    w1c_d = din("w1c", [128, 4096])
    w2c_d = din("w2c", [128, 128])
    tab_d = din("TAB", [128, 4096])
    mc_d = din("MC", [128, 384])
    fe_d = din("FE", [128, 128])
    wadd_d = din("WADD", [128, 128])
    e_d = din("E", [128, SEQ])
    id_d = din("ident", [128, 128])
    out_d = nc.dram_tensor("out", [S, D], F32, kind="ExternalOutput").ap()
    tap_d = {}
    for t in taps:
        w = {"x1": D, "mix": D, "x2": D}[t]
        tap_d[t] = nc.dram_tensor("tap_" + t, [S, w], F32, kind="ExternalOutput").ap()

    s_w1in = dscr("s_w1in", [NJ, 128, 2048])
    s_w1out = dscr("s_w1out", [NJ, 128, D])
    s_w2in = dscr("s_w2in", [NJ, 128, 2048])
    s_w2out = dscr("s_w2out", [NJ, 128, D])
    s_wmi = dscr("s_wmi", [9, 128, 2048])
    s_wmo = dscr("s_wmo", [8, 128, D])

    P = Prog(nc)
    xs = P.sb([128, 4, D], F32, "xs")
    hT = P.sb([128, 8, 512], BF16, "hT")
    hb = [P.sb([128, D], BF16, f"hb{i}") for i in range(2)]
    arena = P.sb([128, NJ * 512], BF16, "arena")
    aT = arena[:].rearrange("p (j t) -> p j t", j=NJ)
    NWA, NWB = 4, 4
    wA = [P.sb([128, 2048], BF16, f"wA{i}") for i in range(NWA)]
    wB = [P.sb([128, D], BF16, f"wB{i}") for i in range(NWB)]
    sg = [P.sb([128, 512], F32, f"sg{i}") for i in range(2)]
    qT_a = P.sb([128, 8, 512], BF16, "qT_a")
    qT_b = P.sb([128, 8, 512], BF16, "qT_b")
    qst = P.sb([128, 4, 1024], BF16, "qst")
    kst = P.sb([128, 4, 128], BF16, "kst")
    rawst = P.sb([128, 4, 256], BF16, "rawst")
    negmT = P.sb([128, 2, 512], BF16, "negmT")
    gates = P.sb([128, 4, 24], F32, "gates")
    mixtm = P.sb([128, 4, D], BF16, "mixtm")
    kT_a = P.sb([128, 1024], BF16, "kT_a")
    kT_w = P.sb([128, 1024], BF16, "kT_w")
    kT_s = P.sb([128, SEQ], BF16, "kT_s")
    v_a = P.sb([128, 8, 2, 65], BF16, "v_a")
    v_w = P.sb([128, 8, 2, 65], BF16, "v_w")
    v_s = P.sb([128, 32, 2, 65], BF16, "v_s")
    Esb = P.sb([128, SEQ], BF16, "Esb")
    k_cT = P.sb([128, 256], BF16, "k_cT")
    v_c = P.sb([128, 2, 2, 64], BF16, "v_c")
    rawT = P.sb([128, 2, 528], BF16, "rawT")
    W1 = P.sb([128, 32, 128], BF16, "W1")
    w2c = P.sb([128, 2, 64], BF16, "w2c")
    TABh = P.sb([128, 16, 256], BF16, "TABh")
    TABl = P.sb([128, 16, 256], BF16, "TABl")
    tabf = [P.sb([128, 256], F32, f"tabf{i}") for i in range(2)]
    FEb = P.sb([128, 128], BF16, "FEb")
    MC = P.sb([128, 8, 3, 16], F32, "MC")
    FE = P.sb([128, 128], F32, "FE")
    WADD = P.sb([128, 128], F32, "WADD")
    identf = P.sb([128, 128], F32, "identf")
    identb = P.sb([128, 128], BF16, "identb")
    gT = P.sb([128, 3, 8], F32, "gT")
    hg = P.sb([128, 4, 64], F32, "hg")
    tblr = P.sb([128, 32, 16], F32, "tblr")
    sinkr = P.sb([128, 8], F32, "sinkr")
    b2r = P.sb([128, 2, 64], F32, "b2r")
    b1T = P.sb([128, 2], F32, "b1T")
    posT = P.sb([128, 32], BF16, "posT")
    b1pw = P.sb([128, 2], F32, "b1pw")
    Ch = P.sb([128, 16], F32, "Ch")
    negC = P.sb([128, 16], F32, "negC")
    b31mC = P.sb([128, 16], F32, "b31mC")
    esink = P.sb([128, 8], F32, "esink")
    small = P.sb([128, 64], F32, "small")
    ss4 = P.sb([128, 4], F32, "ss4")
    rstd4 = P.sb([128, 4], F32, "rstd4")
    sq = P.sb([128, 4, 256], F32, "sq")
    t1 = P.sb([128, 4, 256], F32, "t1")
    ssh = P.sb([128, 16], F32, "ssh")
    rsh = P.sb([128, 16], F32, "rsh")
    NPT, NSSB, NOA = 6, 3, 2
    accb = arena[:, 0:4096].bitcast(F32).rearrange("p (t c) -> p t c", t=4)
    oaug = [arena[:, 4096 + i * 1024:4096 + (i + 1) * 1024].bitcast(F32) for i in range(NOA)]
    pT = [arena[:, 6144 + i * 512:6144 + (i + 1) * 512] for i in range(NPT)]
    s_sb = [arena[:, 9216 + i * 512:9216 + (i + 1) * 512].bitcast(F32) for i in range(NSSB)]
    s4 = [P.sb([128, 4], F32, f"s4_{i}") for i in range(NOA)]
    e_t = [P.sb([128, 256], F32, f"e_t{i}") for i in range(4)]
    den2 = [P.sb([128, 2], F32, f"den2_{i}") for i in range(4)]
    rden = [P.sb([128, 1], F32, f"rden{i}") for i in range(4)]
    pcn = [P.sb([128, 256], BF16, f"pcn{i}") for i in range(4)]
    pcs = [P.sb([128, 264], F32, f"pcs{i}") for i in range(2)]
    pcT = [P.sb([128, 2, 128], BF16, f"pcT{i}") for i in range(4)]
    snear = [P.sb([128, 16], F32, f"snear{i}") for i in range(4)]
    imp = [P.sb([128, 64], F32, f"imp{i}") for i in range(2)]
    score = [P.sb([128, 64], F32, f"score{i}") for i in range(2)]
    sc2 = [P.sb([128, 64], F32, f"sc2{i}") for i in range(2)]
    mx8 = [P.sb([128, 16], F32, f"mx8{i}") for i in range(2)]
    selm = [P.sb([128, 64], F32, f"selm{i}") for i in range(2)]
    nmb = [P.sb([128, 64], BF16, f"nmb{i}") for i in range(2)]
    hid = P.sb([128, 2, 64], BF16, "hid")
    kcf = P.sb([128, 2, 64], F32, "kcf")
    kcst = P.sb([128, 128], BF16, "kcst")
    vcst = P.sb([128, 2, 64], BF16, "vcst")
    pb = [P.ps([128, 512], F32, f"pb{i}") for i in range(8)]
    pbb = [p[:].bitcast(BF16) for p in pb]

    st = {"tr": 0, "acc": 0, "wa": 0, "wb": 0, "sg": 0, "pt": 0, "ssb": 0, "oa": 0, "cm": 0}

    def bank():
        b = st["tr"] % 6
        st["tr"] += 1
        return b

    def accbank():
        b = 6 + st["acc"] % 2
        st["acc"] += 1
        return b

    def rot(name, n):
        i = st[name] % n
        st[name] += 1
        return i

    def ld(q, dst, src, key, dsem="c0"):
        P.dma(q, lambda e, d=dst, s=src: e.dma_start(out=d, in_=s), dsem, writes=[key])

    ld("sp", gT[:].rearrange("p a b -> p (a b)"), gT_d, "gT")
    ld("sp", hg[:].rearrange("p a b -> p (a b)"), hg_d, "hg")
    ld("sp", tblr[:].rearrange("p a b -> p (a b)"), tbl_d, "tblr")
    ld("sp", sinkr[:], sink_d, "sinkr")
    ld("sp", b2r[:].rearrange("p a b -> p (a b)"), b2_d, "b2r")
    ld("sp", b1T[:], b1_d, "b1T")
    ld("sp", MC[:].rearrange("p a b c -> p (a b c)"), mc_d, "MC")
    ld("sp", FE[:], fe_d, "FE")
    ld("sp", WADD[:], wadd_d, "WADD")
    ld("sp", identf[:], id_d, "identf")
    ld("pool", posT[:], pos_d, "posT", "c1")
    ld("pool", W1[:].rearrange("p a b -> p (a b)"), w1c_d, "W1", "c1")
    ld("pool", w2c[:].rearrange("p a b -> p (a b)"), w2c_d, "w2c", "c1")
    ld("pool", identb[:], id_d, "identb", "c1")
    ld("pool", Esb[:], e_d, "Esb", "c1")

    cvn = [0]

    def wload(tb, dst_tile, slotkey, s_w, w_d, key, j):
        if tb == 0:
            P.dma("pool", lambda e: e.dma_start(out=dst_tile[:], in_=w_d[j]), slotkey, writes=[slotkey])
            k = cvn[0] % 8
            cvn[0] += 1
            P.dma("sp", lambda e: e.dma_start(out=s_w[j], in_=dst_tile[:]), ("wst", k), reads=[slotkey],
                  writes=[(key, j), ("wstslot", k)])
        else:
            P.dma("sp", lambda e: e.dma_start(out=dst_tile[:], in_=s_w[j]), slotkey, reads=[(key, j)], writes=[slotkey])

    P.dve(lambda e: e.memset(v_a[:], 1.0), writes=["v_a_all"])
    P.dve(lambda e: e.memset(v_w[:], 1.0), writes=["v_w_all"])
    P.dve(lambda e: e.memset(v_s[:], 1.0), writes=["v_s_all"])
    P.dve(lambda e: e.memset(v_c[:], 0.0), writes=["v_c"])
    P.dve(lambda e: e.memset(rawT[:], 0.0), writes=["rawT"])
    P.dve(lambda e: e.memset(negmT[:], 0.0), writes=["negmT"])
    for i in range(2):
        P.dve(lambda e, i=i: e.memset(pcs[i][:], 0.0), writes=[("pcs", i)])
    for i in range(4):
        P.dve(lambda e, i=i: e.memset(pcn[i][:], 0.0), writes=[("pcn", i)])
    P.dve(lambda e: e.memset(k_cT[:], 0.0), writes=["k_cT"])
    P.dve(lambda e: e.memset(qT_a[:], 0.0), writes=["qT" + str(id(qT_a))])
    P.dve(lambda e: e.memset(qT_b[:], 0.0), writes=["qT" + str(id(qT_b))])

    P.dve(lambda e: e.tensor_scalar(out=hg[:, 0, :], in0=hg[:, 0, :], scalar1=0.125, scalar2=None, op0=ALU.mult),
          reads=["hg"], writes=["hg"])
    P.dve(lambda e: e.tensor_scalar(out=hg[:, 2, :], in0=hg[:, 2, :], scalar1=0.125, scalar2=None, op0=ALU.mult),
          reads=["hg"], writes=["hg"])
    for br, col in ((0, 0), (2, 1)):
        P.dve(lambda e, br=br: e.tensor_tensor(out=small[:, 0:64], in0=hg[:, br, :], in1=hg[:, br + 1, :], op=ALU.mult),
              reads=["hg", "small"], writes=["small"])
        P.dve(lambda e, col=col: e.tensor_reduce(out=ssh[:, col:col + 1], in_=small[:, 0:64], axis=AX.X, op=ALU.max,
                                                 apply_absolute_value=True),
              reads=["small"], writes=["ssh"])
    P.dve(lambda e: e.tensor_reduce(out=Ch[:], in_=tblr[:].rearrange("p b h -> p h b"), axis=AX.X, op=ALU.max),
          reads=["tblr"], writes=["Ch"])
    for half, col in ((0, 0), (1, 1)):
        P.dve(lambda e, half=half, col=col: e.scalar_tensor_tensor(
            out=Ch[:, half * 8:(half + 1) * 8], in0=ssh[:, col:col + 1].to_broadcast([128, 8]), scalar=64.0,
            in1=Ch[:, half * 8:(half + 1) * 8], op0=ALU.mult, op1=ALU.add), reads=["ssh", "Ch"], writes=["Ch"])
    P.dve(lambda e: e.tensor_scalar(out=negC[:], in0=Ch[:], scalar1=-1.0, scalar2=None, op0=ALU.mult),
          reads=["Ch"], writes=["negC"])
    P.dve(lambda e: e.tensor_tensor(out=b31mC[:], in0=tblr[:, 31, :], in1=Ch[:], op=ALU.subtract),
          reads=["tblr", "Ch"], writes=["b31mC"])
    P.dve(lambda e: e.tensor_copy(out=FEb[:], in_=FE[:]), reads=["FE"], writes=["FEb"])
    for h in range(16):
        i = h % 2
        P.dma("sp", lambda e, h=h, i=i: e.dma_start(out=tabf[i][:], in_=tab_d[:, h * 256:(h + 1) * 256]), ("tabf", i),
              writes=[("tabf", i)])
        P.dve(lambda e, h=h, i=i: e.tensor_scalar(out=tabf[i][:], in0=tabf[i][:], scalar1=tblr[:, 31, h:h + 1], scalar2=None,
                                                  op0=ALU.subtract), reads=[("tabf", i), "tblr"], writes=[("tabf", i)])
        P.dve(lambda e, h=h, i=i: e.tensor_copy(out=TABh[:, h, :], in_=tabf[i][:]), reads=[("tabf", i)], writes=["TAB"])
        P.dve(lambda e, h=h, i=i: e.tensor_tensor(out=tabf[i][:], in0=tabf[i][:], in1=TABh[:, h, :], op=ALU.subtract),
              reads=[("tabf", i), "TAB"], writes=[("tabf", i)])
        P.dve(lambda e, h=h, i=i: e.tensor_copy(out=TABl[:, h, :], in_=tabf[i][:]), reads=[("tabf", i)], writes=["TAB"])
    P.dve(lambda e: e.tensor_tensor(out=esink[:], in0=sinkr[:], in1=Ch[:, 0:8], op=ALU.subtract),
          reads=["sinkr", "Ch"], writes=["esink"])
    P.act(lambda e: e.activation(out=esink[:], in_=esink[:], func=AF.Exp), reads=["esink"], writes=["esink"])
    for kv in range(2):
        def f(e, kv=kv):
            r = slice(64 * kv, 64 * kv + 64)
            ins = None
            for p in range(32):
                ins = e.matmul(pb[kv][:, 0:1], lhsT=W1[r, p, :], rhs=posT[r, p:p + 1], start=(p == 0), stop=(p == 31))
            return ins
        P.pe(f, reads=["W1", "posT"], writes=[("pb", kv)])
        P.dve(lambda e, kv=kv: e.tensor_tensor(out=b1pw[:, kv:kv + 1], in0=pb[kv][:, 0:1], in1=b1T[:, kv:kv + 1], op=ALU.add),
              reads=[("pb", kv), "b1T"], writes=["b1pw"])

    def rms_to_hT(tb, nidx):
        P.dve(lambda e: e.memset(ss4[:], 0.0), reads=["ss4"], writes=["ss4"])
        junkf = t1[:].rearrange("p t c -> p (t c)")
        for tt in range(4):
            P.act(lambda e, tt=tt: e.activation(out=junkf, in_=xs[:, tt, :], func=AF.Square, accum_out=ss4[:, tt:tt + 1]),
                  reads=[("xs", tt), "ss4"], writes=[("t1", t) for t in range(4)] + [("ss4", tt)])
            P.act(lambda e, tt=tt: e.activation(out=rstd4[:, tt:tt + 1], in_=ss4[:, tt:tt + 1], func=AF.Ln, scale=1.0 / D, bias=EPS),
                  reads=[("ss4", tt)], writes=[("rstd4", tt)])
            P.act(lambda e, tt=tt: e.activation(out=rstd4[:, tt:tt + 1], in_=rstd4[:, tt:tt + 1], func=AF.Exp, scale=-0.5),
                  reads=[("rstd4", tt)], writes=[("rstd4", tt)])

        def scale(tt):
            P.dve(lambda e: e.tensor_scalar(out=hb[tt % 2][:], in0=xs[:, tt, :], scalar1=rstd4[:, tt:tt + 1], scalar2=None,
                                            op0=ALU.mult), reads=[("xs", tt), ("rstd4", tt)], writes=[("hb", tt % 2)])

        def trev(tt):
            b = bank()

            def f(e):
                ins = None
                for c in range(8):
                    ins = e.transpose(out=pbb[b][:, c * 128:(c + 1) * 128], in_=hb[tt % 2][:, c * 128:(c + 1) * 128],
                                      identity=identb[:])
                return ins
            P.pe(f, reads=[("hb", tt % 2), "identb"], writes=[("pb", b)])
            P.dve(lambda e: e.tensor_tensor(
                out=hT[:, :, tt * 128:(tt + 1) * 128], in0=pbb[b].rearrange("p (c t) -> p c t", c=8),
                in1=gT[:, nidx, :].unsqueeze(2).to_broadcast([128, 8, 128]), op=ALU.mult),
                reads=[("pb", b), "gT"], writes=["hT"])
        scale(0)
        scale(1)
        trev(0)
        scale(2)
        trev(1)
        scale(3)
        trev(2)
        trev(3)

    def ffn(tb, s_in, s_out, kin, kout, w_in_d, w_out_d):
        for j in range(NJ):
            sl = rot("wa", NWA)
            wload(tb, wA[sl], ("wa", sl), s_in, w_in_d, kin, j)
            wv = wA[sl][:].rearrange("p (u c f) -> p u c f", u=2, c=8)
            bg, bu = bank(), bank()
            for u, b in ((0, bg), (1, bu)):
                def f(e, u=u, b=b, wv=wv):
                    ins = None
                    for c in range(8):
                        ins = e.matmul(pb[b][:, :], lhsT=wv[:, u, c, :], rhs=hT[:, c, :], start=(c == 0), stop=(c == 7))
                    return ins
                P.pe(f, reads=[("wa", sl), "hT"], writes=[("pb", b)])
            k = rot("sg", 2)
            P.act(lambda e, k=k, bg=bg: e.activation(out=sg[k][:], in_=pb[bg][:, :], func=AF.Silu),
                  reads=[("pb", bg)], writes=[("sg", k)])
            P.dve(lambda e, k=k, bu=bu, j=j: e.tensor_tensor(out=aT[:, j, :], in0=pb[bu][:, :], in1=sg[k][:], op=ALU.mult),
                  reads=[("pb", bu), ("sg", k), "arena"], writes=[("aT", j)])
        rowproj(tb, s_out, kout, NJ, lambda j, tt: aT[:, j, tt * 128:(tt + 1) * 128], lambda j: [("aT", j)], 0.5, w_out_d)

    def rowproj(tb, s_w, kw, nchunk, lhs_of, lhs_keys, scale, w_d):
        for j in range(nchunk):
            sl = rot("wb", NWB)
            wload(tb, wB[sl], ("wb", sl), s_w, w_d, kw, j)

            def f(e, sl=sl, j=j):
                ins = None
                for tt in range(4):
                    for hf in range(2):
                        ins = e.matmul(pb[tt * 2 + hf][:, :], lhsT=lhs_of(j, tt), rhs=wB[sl][:, hf * 512:(hf + 1) * 512],
                                       start=(j == 0), stop=(j == nchunk - 1))
                return ins
            P.pe(f, reads=[("wb", sl)] + lhs_keys(j), writes=[("pb", i) for i in range(8)])
        for tt in range(4):
            for hf in range(2):
                b = tt * 2 + hf
                P.dve(lambda e, tt=tt, hf=hf, b=b: e.scalar_tensor_tensor(
                    out=xs[:, tt, hf * 512:(hf + 1) * 512], in0=pb[b][:, :], scalar=scale,
                    in1=xs[:, tt, hf * 512:(hf + 1) * 512], op0=ALU.mult, op1=ALU.add),
                    reads=[("pb", b), ("xs", tt)], writes=[("xs", tt)])

    def tap(name, tb, src_of_tt, keys_of_tt):
        if name not in tap_d:
            return
        for tt in range(4):
            r0 = tb * 512 + tt * 128
            P.dma("pool", lambda e, tt=tt, r0=r0: e.dma_start(out=tap_d[name][r0:r0 + 128, :], in_=src_of_tt(tt)),
                  ("tap", name, tt), reads=keys_of_tt(tt))

    def norm_heads(srcs, nh, gidx, dst4, rkeys_l, wkeys, npart=128):
        T = len(srcs)
        n = nh * 64
        pr = slice(0, npart)
        for t, src in enumerate(srcs):
            P.act(lambda e, t=t, src=src: e.activation(out=sq[pr, t, 0:n], in_=src, func=AF.Square),
                  reads=rkeys_l[t], writes=[("sq", t)])
        P.dve(lambda e: e.tensor_reduce(out=ssh[pr, 0:T * nh].rearrange("p (t h) -> p t h", t=T),
                                        in_=sq[pr, 0:T, 0:n].rearrange("p t (h d) -> p t h d", h=nh),
                                        axis=AX.X, op=ALU.add), reads=[("sq", t) for t in range(T)], writes=["ssh"])
        P.act(lambda e: e.activation(out=rsh[pr, 0:T * nh], in_=ssh[pr, 0:T * nh], func=AF.Ln, scale=1.0 / 64, bias=EPS),
              reads=["ssh"], writes=["rsh"])
        P.act(lambda e: e.activation(out=rsh[pr, 0:T * nh], in_=rsh[pr, 0:T * nh], func=AF.Exp, scale=-0.5),
              reads=["rsh"], writes=["rsh"])
        for t, src in enumerate(srcs):
            P.dve(lambda e, t=t, src=src: e.tensor_tensor(
                out=t1[pr, t, 0:n].rearrange("p (h d) -> p h d", h=nh), in0=src.rearrange("p (h d) -> p h d", h=nh),
                in1=rsh[pr, t * nh:(t + 1) * nh].unsqueeze(2).to_broadcast([npart, nh, 64]), op=ALU.mult),
                reads=rkeys_l[t] + ["rsh"], writes=[("t1", t)])
        P.dve(lambda e: e.tensor_tensor(out=dst4.rearrange("p t (h d) -> p t h d", h=nh),
                                        in0=t1[pr, 0:T, 0:n].rearrange("p t (h d) -> p t h d", h=nh),
                                        in1=hg[pr, gidx, :].unsqueeze(1).unsqueeze(1).to_broadcast([npart, T, nh, 64]), op=ALU.mult),
              reads=[("t1", t) for t in range(T)] + ["hg"], writes=wkeys)

    def tr_to(dst, src_list, wkeys, rkeys, nrow_out=128, use_act=True, b=None):
        if b is None:
            b = bank()

        def f(e):
            ins = None
            for i, s in enumerate(src_list):
                ins = e.transpose(out=pbb[b][0:nrow_out, i * 128:(i + 1) * 128], in_=s, identity=identb[:])
            return ins
        P.pe(f, reads=rkeys + ["identb"], writes=[("pb", b)])
        n = len(src_list) * 128
        if use_act:
            P.act(lambda e: e.copy(out=dst, in_=pbb[b][0:nrow_out, 0:n]), reads=[("pb", b)], writes=wkeys)
        else:
            P.dve(lambda e: e.tensor_copy(out=dst, in_=pbb[b][0:nrow_out, 0:n]), reads=[("pb", b)], writes=wkeys)

    def pieces_for(j, c0, c1, edge):
        out = []
        for qs in range(c0 // 128, c1 // 128):
            d = qs - j
            kind = "near" if d <= 1 else ("edge" if (edge and d == 4) else "far")
            if out and out[-1][0] == kind and kind != "edge":
                out[-1][2] = (qs + 1) * 128
            else:
                out.append([kind, qs * 128, (qs + 1) * 128])
        return out

    def do_block(tb):
        t0 = tb * 512
        for tt in range(4):
            P.dma("sp", lambda e, t0=t0, tt=tt: e.dma_start(out=xs[:, tt, :], in_=x_d[t0 + tt * 128:t0 + (tt + 1) * 128, :]),
                  ("xld", tt), writes=[("xs", tt)])
        rms_to_hT(tb, 0)
        ffn(tb, s_w1in, s_w1out, "s_w1in", "s_w1out", w1in_d, w1out_d)
        tap("x1", tb, lambda tt: xs[:, tt, :], lambda tt: [("xs", tt)])
        rms_to_hT(tb, 1)
        P.dve(lambda e: e.memset(small[:, 63:64], 0.0), writes=[("aT", j) for j in range(NJ)] + ["arena"])
        P.dve(lambda e: e.memset(accb, 0.0), writes=[("accb", t_, h_) for t_ in range(4) for h_ in range(8)])
        def proj_mm(cg):
            sl = rot("wa", NWA)
            wload(tb, wA[sl], ("wa", sl), s_wmi, wmi_d, "s_wmi", cg)
            wv = wA[sl][:].rearrange("p (c f) -> p c f", c=8)
            bks = [4 * (cg % 2) + i for i in range(4)]
            for tt in range(4):
                def f(e, b=bks[tt], wv=wv, tt=tt):
                    ins = None
                    for c in range(8):
                        ins = e.matmul(pb[b][:, 0:256], lhsT=hT[:, c, tt * 128:(tt + 1) * 128], rhs=wv[:, c, :],
                                       start=(c == 0), stop=(c == 7))
                    return ins
                P.pe(f, reads=[("wa", sl), "hT"], writes=[("pb", bks[tt])])
            return bks

        def proj_post(cg, bks):
            rkl = [[("pb", bks[tt])] for tt in range(4)]
            kt0 = 4 * tb
            if cg in (0, 1, 3, 4):
                off = cg * 256 if cg < 2 else 512 + (cg - 3) * 256
                norm_heads([pb[bks[tt]][:, 0:256] for tt in range(4)], 4, 0 if cg < 2 else 2,
                           qst[:, :, off:off + 256], rkl, [("qst", cg)])
            elif cg in (2, 6, 7):
                norm_heads([pb[bks[tt]][:, 0:128] for tt in range(4)], 2, 1 if cg == 2 else 3, kst[:, :, :], rkl, ["kst"])
                if cg == 2:
                    s0 = (kt0 % 8)
                    dstk, dkey, vt_, vkey, vs0 = kT_a[:, s0 * 128:(s0 + 4) * 128], "kT" + str(id(kT_a)), v_a, "v" + str(id(v_a)), s0
                elif cg == 6:
                    dstk, dkey, vt_, vkey, vs0 = kT_s[:, kt0 * 128:(kt0 + 4) * 128], "kT" + str(id(kT_s)), v_s, "v" + str(id(v_s)), kt0
                else:
                    s0 = (kt0 % 8)
                    dstk, dkey, vt_, vkey, vs0 = kT_w[:, s0 * 128:(s0 + 4) * 128], "kT" + str(id(kT_w)), v_w, "v" + str(id(v_w)), s0
                for tt in range(4):
                    P.act(lambda e, b=bks[tt], dstv=vt_[:, vs0 + tt, :, 0:64]: e.copy(
                        out=dstv, in_=pb[b][:, 128:256].rearrange("p (g d) -> p g d", g=2)),
                        reads=rkl[tt] + ["v_a_all", "v_w_all", "v_s_all"], writes=[vkey])
                tr_to(dstk, [kst[:, tt, :] for tt in range(4)], [dkey], ["kst"], b=bks[0])
            elif cg == 5:
                for tt in range(4):
                    P.act(lambda e, b=bks[tt], tt=tt: e.copy(out=rawst[:, tt, :], in_=pb[b][:, 0:256]), reads=rkl[tt],
                          writes=[("rawst", tt)])
                for tt in range(4):
                    bb = bks[tt]

                    def f2(e, bb=bb, tt=tt):
                        e.transpose(out=pbb[bb][:, 0:128], in_=rawst[:, tt, 0:128], identity=identb[:])
                        return e.transpose(out=pbb[bb][:, 128:256], in_=rawst[:, tt, 128:256], identity=identb[:])
                    P.pe(f2, reads=[("rawst", tt), "identb"], writes=[("pb", bb)])
                    P.dve(lambda e, bb=bb, tt=tt: e.tensor_copy(
                        out=rawT[:, :, 16 + tt * 128:16 + (tt + 1) * 128],
                        in_=pbb[bb][:, 0:256].rearrange("p (g t) -> p g t", g=2)), reads=[("pb", bb)], writes=["rawT"])
            else:
                for tt in range(4):
                    P.act(lambda e, b=bks[tt], tt=tt: e.activation(out=gates[:, tt, :], in_=pb[b][:, 0:24], func=AF.Exp, scale=-1.0),
                          reads=rkl[tt], writes=["gates"])
                P.dve(lambda e: e.tensor_scalar(out=gates[:], in0=gates[:], scalar1=1.0, scalar2=None, op0=ALU.add),
                      reads=["gates"], writes=["gates"])
                P.dve(lambda e: e.reciprocal(out=gates[:], in_=gates[:]), reads=["gates"], writes=["gates"])
            if cg in (1, 4):
                off = 0 if cg == 1 else 512
                qT = qT_a if cg == 1 else qT_b
                for r in range(4):
                    b = bks[r]

                    def ftr(e, b=b, r=r):
                        ins = None
                        for tt in range(4):
                            ins = e.transpose(out=pbb[b][:, tt * 128:(tt + 1) * 128],
                                              in_=qst[:, tt, off + r * 128:off + (r + 1) * 128], identity=identb[:])
                        return ins
                    P.pe(ftr, reads=[("qst", cg - 1), ("qst", cg), "identb"], writes=[("pb", b)])
                    P.act(lambda e, b=b, r=r: e.copy(out=qT[0:64, r * 2, :], in_=pbb[b][0:64, 0:512]),
                          reads=[("pb", b)], writes=["qT" + str(id(qT))])
                    P.dve(lambda e, b=b, r=r: e.tensor_copy(out=qT[64:128, r * 2 + 1, :], in_=pbb[b][64:128, 0:512]),
                          reads=[("pb", b)], writes=["qT" + str(id(qT))])

        pend_b = proj_mm(0)
        for cg in range(9):
            nxt_b = proj_mm(cg + 1) if cg + 1 < 9 else None
            proj_post(cg, pend_b)
            pend_b = nxt_b

        for kv in range(2):
            bq = bank()

            def f(e, kv=kv, bq=bq):
                rr = slice(64 * kv, 64 * kv + 64)
                ins = None
                for g in range(2):
                    for p in range(32):
                        ins = e.matmul(pb[bq][:, g * 32:(g + 1) * 32], lhsT=W1[rr, p, :], rhs=rawT[rr, g, p:p + 497:16],
                                       start=(p == 0), stop=(p == 31), skip_group_check=True)
                return ins
            P.pe(f, reads=["W1", "rawT"], writes=[("pb", bq)])
            P.act(lambda e, kv=kv, bq=bq: e.activation(out=hid[:, kv, :], in_=pb[bq][:, 0:64], func=AF.Silu,
                                                       bias=b1pw[:, kv:kv + 1]), reads=[("pb", bq), "b1pw"], writes=[("hid", kv)])
            bo = bank()

            def f2(e, kv=kv, bo=bo):
                ins = None
                for g in range(2):
                    ins = e.matmul(pb[bo][0:32, g * 64:(g + 1) * 64], lhsT=hid[:, kv, g * 32:(g + 1) * 32], rhs=w2c[:, kv, :],
                                   start=True, stop=True, skip_group_check=True)
                return ins
            P.pe(f2, reads=[("hid", kv), "w2c"], writes=[("pb", bo)])
            if kv == 0:
                P.dve(lambda e, bo=bo: e.tensor_tensor(out=kcf[0:32, :, :], in0=pb[bo][0:32, 0:128].rearrange("p (g d) -> p g d", g=2),
                                                       in1=b2r[0:32, 0, :].unsqueeze(1).to_broadcast([32, 2, 64]), op=ALU.add),
                      reads=[("pb", bo), "b2r"], writes=["kcf"])
                norm_heads([kcf[0:32, :, :].rearrange("p g d -> p (g d)")], 2, 3, kcst[0:32, :].unsqueeze(1), [["kcf"]], ["kcst"],
                           npart=32)
                bt = bank()
                P.pe(lambda e, bt=bt: e.transpose(out=pbb[bt][:, 0:32], in_=kcst[0:32, :], identity=identb[0:32, 0:32]),
                     reads=["kcst", "identb"], writes=[("pb", bt)])
                P.act(lambda e, bt=bt, tb=tb: e.copy(out=k_cT[:, 32 * tb:32 * tb + 32], in_=pbb[bt][:, 0:32]),
                      reads=[("pb", bt)], writes=["k_cT"])
            else:
                P.dve(lambda e, bo=bo: e.tensor_tensor(out=vcst[0:32, :, :], in0=pb[bo][0:32, 0:128].rearrange("p (g d) -> p g d", g=2),
                                                       in1=b2r[0:32, 1, :].unsqueeze(1).to_broadcast([32, 2, 64]), op=ALU.add),
                      reads=[("pb", bo), "b2r"], writes=["vcst"])
                p0 = (32 * tb) % 128
                ct = (32 * tb) // 128
                P.dma("pool", lambda e, p0=p0, ct=ct: e.dma_start(out=v_c[p0:p0 + 32, ct, :, :], in_=vcst[0:32, :, :]),
                      "vc", reads=["vcst"], writes=["v_c"])
        if tb + 1 < NB:
            P.dve(lambda e: e.tensor_copy(out=rawT[:, :, 0:16], in_=rawT[:, :, 512:528]), reads=["rawT"], writes=["rawT"])

        def cmp_tt(tt):
            qt = 4 * tb + tt
            ncol = min(8 * qt + 8, 256)
            ns = max(0, 8 * qt - 8)
            var = min(qt, 2)
            moff = 8 if qt == 0 else 0
            w = ncol - ns
            lo = 0 if qt < 2 else 1
            nct = 1 if ncol <= 128 else 2
            def cmp_g(g):
                rows = slice(64 * g, 64 * g + 64)
                H = [(r, g * 4 + r, 8 + g * 4 + r) for r in range(4)]
                bs = [bank() for _ in range(4)]
                for r, hB, hidx in H:
                    P.pe(lambda e, b=bs[r], r=r: e.matmul(
                        pb[b][:, 0:ncol], lhsT=qT_b[:, r * 2 + g, tt * 128:(tt + 1) * 128], rhs=k_cT[:, 0:ncol],
                        start=True, stop=True), reads=["qT" + str(id(qT_b)), "k_cT"], writes=[("pb", bs[r])])
                    P.dve(lambda e, r=r: e.memset(den2[r][:], 0.0), reads=[("den2", r)], writes=[("den2", r)])
                for r, hB, hidx in H:
                    if ns > 1:
                        P.act(lambda e, b=bs[r], r=r, hidx=hidx: e.activation(
                            out=e_t[r][:, 1:ns], in_=pb[b][:, 1:ns], func=AF.Exp, bias=b31mC[:, hidx:hidx + 1],
                            accum_out=den2[r][:, 0:1]), reads=[("pb", bs[r]), "b31mC", ("den2", r)],
                            writes=[("e_t", r), ("den2", r)])
                    P.dve(lambda e, b=bs[r], r=r, hB=hB: e.tensor_tensor(
                        out=snear[r][:, 0:w], in0=pb[b][:, ns:ncol], in1=MC[:, hB, var, moff:moff + w], op=ALU.add),
                        reads=[("pb", bs[r]), "MC"], writes=[("snear", r)])
                for r, hB, hidx in H:
                    P.act(lambda e, r=r, hidx=hidx: e.activation(
                        out=e_t[r][:, ns:ncol], in_=snear[r][:, 0:w], func=AF.Exp, bias=negC[:, hidx:hidx + 1],
                        accum_out=den2[r][:, 1:2]), reads=[("snear", r), "negC", ("den2", r)],
                        writes=[("e_t", r), ("den2", r)])
                for r, hB, hidx in H:
                    P.dve(lambda e, r=r: e.tensor_scalar(out=rden[r][:], in0=den2[r][:, 0:1], scalar1=den2[r][:, 1:2],
                                                         scalar2=1e-30, op0=ALU.add, op1=ALU.max),
                          reads=[("den2", r)], writes=[("rden", r)])
                for r, hB, hidx in H:
                    P.dve(lambda e, r=r: e.reciprocal(out=rden[r][:], in_=rden[r][:]), reads=[("rden", r)], writes=[("rden", r)])
                for r, hB, hidx in H:
                    P.dve(lambda e, r=r: e.tensor_scalar(
                        out=pcn[r][:, lo:ncol], in0=e_t[r][:, lo:ncol], scalar1=rden[r][:, 0:1], scalar2=None, op0=ALU.mult),
                        reads=[("e_t", r), ("rden", r)], writes=[("pcn", r)])
                bts = []
                for r, hB, hidx in H:
                    bt = bank()
                    bts.append(bt)

                    def f(e, bt=bt, r=r):
                        ins = None
                        for c in range(nct):
                            ins = e.transpose(out=pbb[bt][:, c * 128:(c + 1) * 128], in_=pcn[r][:, c * 128:(c + 1) * 128],
                                              identity=identb[:])
                        return ins
                    P.pe(f, reads=[("pcn", r), "identb"], writes=[("pb", bt)])
                    P.act(lambda e, bt=bt, r=r: e.copy(out=pcT[r][:, 0:nct, :],
                                                       in_=pbb[bt][:, 0:nct * 128].rearrange("p (c q) -> p c q", c=nct)),
                          reads=[("pb", bt)], writes=[("pcT", r)])
                for r, hB, hidx in H:
                    if r == 0:
                        P.dve(lambda e, r=r: e.tensor_scalar(
                            out=pcs[g][:, lo:ncol], in0=e_t[r][:, lo:ncol], scalar1=rden[r][:, 0:1], scalar2=None, op0=ALU.mult),
                            reads=[("e_t", r), ("rden", r)], writes=[("pcs", g)])
                    else:
                        P.dve(lambda e, r=r: e.scalar_tensor_tensor(
                            out=pcs[g][:, lo:ncol], in0=e_t[r][:, lo:ncol], scalar=rden[r][:, 0:1], in1=pcs[g][:, lo:ncol],
                            op0=ALU.mult, op1=ALU.add), reads=[("e_t", r), ("rden", r), ("pcs", g)], writes=[("pcs", g)])
                for r, hB, hidx in H:
                    bo = bank()

                    def f2(e, bo=bo, r=r):
                        ins = None
                        for c in range(nct):
                            ins = e.matmul(pb[bo][:, 0:64], lhsT=pcT[r][:, c, :], rhs=v_c[:, c, g, :], start=(c == 0),
                                           stop=(c == nct - 1))
                        return ins
                    P.pe(f2, reads=[("pcT", r), "v_c"], writes=[("pb", bo)])
                    P.dve(lambda e, bo=bo, hB=hB: e.scalar_tensor_tensor(
                        out=accb[:, tt, hB * 64:(hB + 1) * 64], in0=pb[bo][:, 0:64], scalar=gates[:, tt, hB:hB + 1],
                        in1=accb[:, tt, hB * 64:(hB + 1) * 64], op0=ALU.mult, op1=ALU.add),
                        reads=[("pb", bo), "gates", ("accb", tt, hB)], writes=[("accb", tt, hB)])
            for g in range(2):
                cmp_g(g)
            for stage in range(10):
                for g in range(2):
                    if stage == 0:
                        P.dve(lambda e, g=g: e.tensor_reduce(out=imp[g][:], in_=pcs[g][:, 0:256].rearrange("p (n f) -> p n f", f=4),
                                                             axis=AX.X, op=ALU.add), reads=[("pcs", g)], writes=[("imp", g)])
                    elif stage == 1:
                        P.dve(lambda e, g=g: e.tensor_tensor(
                            out=imp[g][:], in0=imp[g][:], in1=pcs[g][:, 4:260].rearrange("p (n f) -> p n f", f=4)[:, :, 0],
                            op=ALU.add), reads=[("pcs", g), ("imp", g)], writes=[("imp", g)])
                    elif stage == 2:
                        P.dve(lambda e, g=g: e.tensor_tensor(out=score[g][:], in0=imp[g][:], in1=WADD[:, 62 - 2 * qt:126 - 2 * qt],
                                                             op=ALU.add), reads=[("imp", g), "WADD"], writes=[("score", g)])
                    elif stage == 3:
                        if qt >= 1:
                            P.dve(lambda e, g=g: e.tensor_scalar(out=score[g][:, 0:1], in0=score[g][:, 0:1], scalar1=1000.0,
                                                                 scalar2=None, op0=ALU.add), reads=[("score", g)], writes=[("score", g)])
                    elif stage == 4:
                        P.dve(lambda e, g=g: e.max(out=mx8[g][:, 0:8], in_=score[g][:]), reads=[("score", g)], writes=[("mx8", g)])
                    elif stage == 5:
                        P.dve(lambda e, g=g: e.match_replace(out=sc2[g][:], in_to_replace=mx8[g][:, 0:8], in_values=score[g][:],
                                                             imm_value=-3.0e38), reads=[("mx8", g), ("score", g)], writes=[("sc2", g)])
                    elif stage == 6:
                        P.dve(lambda e, g=g: e.max(out=mx8[g][:, 8:16], in_=sc2[g][:]), reads=[("sc2", g), ("mx8", g)],
                              writes=[("mx8b", g)])
                    elif stage == 7:
                        P.dve(lambda e, g=g: e.tensor_scalar(out=selm[g][:], in0=score[g][:], scalar1=mx8[g][:, 15:16], scalar2=None,
                                                             op0=ALU.is_ge), reads=[("score", g), ("mx8b", g)], writes=[("selm", g)])
                    elif stage == 8:
                        P.dve(lambda e, g=g: e.scalar_tensor_tensor(out=selm[g][:], in0=score[g][:], scalar=-5.0e8, in1=selm[g][:],
                                                                    op0=ALU.is_gt, op1=ALU.mult),
                              reads=[("score", g), ("selm", g)], writes=[("selm", g)])
                    else:
                        P.dve(lambda e, g=g: e.tensor_scalar(out=nmb[g][:], in0=selm[g][:], scalar1=-1.0, scalar2=-NEGM,
                                                             op0=ALU.add, op1=ALU.mult), reads=[("selm", g)], writes=[("nmb", g)])
            for g in range(2):
                bt = bank()
                P.pe(lambda e, bt=bt, g=g: e.transpose(out=pbb[bt][0:64, 0:128], in_=nmb[g][:], identity=identb[:]),
                     reads=[("nmb", g), "identb"], writes=[("pb", bt)])
                P.act(lambda e, bt=bt, g=g: e.copy(out=negmT[0:64, g, tt * 128:(tt + 1) * 128], in_=pbb[bt][0:64, 0:128]),
                      reads=[("pb", bt)], writes=[("negmT", g)])

        DPIPE = 4
        tiles = []
        heads = []
        for g in range(2):
            for r in range(4):
                hA = g * 4 + r
                kts = [kt for kt in range(4 * tb - 1, 4 * tb + 4) if kt >= 0]
                heads.append(dict(kind="A", g=g, r=r, hidx=hA, h=hA, kts=kts, kT=kT_a, ks=lambda kt: kt % 8, vt=v_a,
                                  vs=lambda kt: kt % 8, qT=qT_a, edge=False, col_end=256, masked=False))
        for g in range(2):
            for r in range(4):
                hB = g * 4 + r
                kts = [kt for kt in range(4 * tb - 4, 4 * tb + 4) if kt >= 0]
                heads.append(dict(kind="W", g=g, r=r, hidx=8 + hB, h=hB, kts=kts, kT=kT_w, ks=lambda kt: kt % 8, vt=v_w,
                                  vs=lambda kt: kt % 8, qT=qT_b, edge=True, col_end=640, masked=False))
        for g in range(2):
            for r in range(4):
                hB = g * 4 + r
                heads.append(dict(kind="S", g=g, r=r, hidx=8 + hB, h=hB, kts=list(range(0, 4 * tb + 4)), kT=kT_s,
                                  ks=lambda kt: kt, vt=v_s, vs=lambda kt: kt, qT=qT_b, edge=False, col_end=100000, masked=True))
        for hi, hd in enumerate(heads):
            for idx, kt in enumerate(hd["kts"]):
                tiles.append((hi, idx, kt))
        NT = len(tiles)
        tinfo = {}

        def stage_qk(i):
            hi, idx, kt = tiles[i]
            hd = heads[hi]
            g, r, hidx = hd["g"], hd["r"], hd["hidx"]
            kT, qT = hd["kT"], hd["qT"]
            rows = slice(64 * g, 64 * g + 64)
            if idx == 0:
                hd["ba"] = accbank()
            j = kt - 4 * tb
            c0 = max(0, 128 * j)
            c1 = min(512, 128 * j + hd["col_end"])
            n = c1 - c0
            bs = bank()
            ksl = hd["ks"](kt)
            ip = i % NPT
            tinfo[i] = (c0, c1, n, ip)
            pcs_ = pieces_for(j, c0, c1, hd["edge"])
            extra = []
            for kind, a, b_ in pcs_:
                lo_, hi_ = a - c0, b_ - c0
                if kind == "near":
                    v0 = a - 128 * j
                    extra.append((lo_, hi_, TABh[:, hidx, v0:v0 + (hi_ - lo_)]))
                    extra.append((lo_, hi_, TABl[:, hidx, v0:v0 + (hi_ - lo_)]))
                elif kind == "edge":
                    extra.append((lo_, hi_, FEb[:, 0:hi_ - lo_]))
            masked = hd["masked"]

            def f(e):
                last_is_qk = (not masked) and (not extra)
                ins = e.matmul(pb[bs][:, 0:n], lhsT=kT[:, ksl * 128:(ksl + 1) * 128], rhs=qT[:, r * 2 + g, c0:c1],
                               start=True, stop=last_is_qk)
                if masked:
                    ins = e.matmul(pb[bs][:, 0:n], lhsT=Esb[:, kt * 128:(kt + 1) * 128], rhs=negmT[:, g, c0:c1],
                                   start=False, stop=(not extra), skip_group_check=True)
                for xi, (lo_, hi_, ap_) in enumerate(extra):
                    ins = e.matmul(pb[bs][:, lo_:hi_], lhsT=identb[:], rhs=ap_, start=False, stop=(xi == len(extra) - 1),
                                   skip_group_check=True)
                return ins
            rk = ["kT" + str(id(kT)), "qT" + str(id(qT)), "TAB", "FEb", "identb"]
            if masked:
                rk += ["Esb", ("negmT", g)]
            P.pe(f, reads=rk, writes=[("pb", bs)])
            P.act(lambda e: e.activation(out=pT[ip][:, 0:n], in_=pb[bs][:, 0:n], func=AF.Exp, bias=b31mC[:, hidx:hidx + 1]),
                  reads=[("pb", bs), "b31mC"], writes=[("pT", ip)])

        def stage_pv(i):
            hi, idx, kt = tiles[i]
            hd = heads[hi]
            c0, c1, n, ip = tinfo[i]
            ba = hd["ba"]
            vsl = hd["vs"](kt)
            vt, g = hd["vt"], hd["g"]
            last = (idx == len(hd["kts"]) - 1)
            P.pe(lambda e: e.matmul(pb[ba][0:65, c0:c1], lhsT=vt[:, vsl, g, :], rhs=pT[ip][:, 0:n], start=(idx == 0),
                                    stop=last, skip_group_check=True),
                 reads=[("pT", ip), "v" + str(id(vt))], writes=[("pb", ba)])
            return last

        def fin_copy(hi):
            hd = heads[hi]
            io = rot("oa", NOA)
            hd["io"] = io
            ba = hd["ba"]
            if hd["kind"] == "S":
                P.dve(lambda e: e.tensor_copy(out=oaug[io][0:65, :], in_=pb[ba][0:65, :]), reads=[("pb", ba)], writes=[("oaug", io)])
            else:
                P.act(lambda e: e.copy(out=oaug[io][0:65, :], in_=pb[ba][0:65, :]), reads=[("pb", ba)], writes=[("oaug", io)])

        def fin_rest(hi):
            hd = heads[hi]
            io = hd["io"]
            h, kind = hd["h"], hd["kind"]
            b = bank()

            def f(e):
                ins = None
                for tt in range(4):
                    ins = e.transpose(out=pb[b][:, tt * 65:(tt + 1) * 65], in_=oaug[io][0:65, tt * 128:(tt + 1) * 128],
                                      identity=identf[0:65, 0:65])
                return ins
            P.pe(f, reads=[("oaug", io), "identf"], writes=[("pb", b)])
            pv = pb[b][:, 0:260].rearrange("p (t c) -> p t c", t=4)
            if kind == "A":
                P.dve(lambda e: e.tensor_scalar(out=s4[io][:], in0=pv[:, :, 64], scalar1=esink[:, h:h + 1], scalar2=None,
                                                op0=ALU.add), reads=[("pb", b), "esink"], writes=[("s4", io)])
                P.dve(lambda e: e.reciprocal(out=s4[io][:], in_=s4[io][:]), reads=[("s4", io)], writes=[("s4", io)])
                for tt in range(4):
                    P.dve(lambda e, tt=tt: e.tensor_scalar(out=mixtm[:, tt, h * 64:(h + 1) * 64], in0=pv[:, tt, 0:64],
                                                           scalar1=s4[io][:, tt:tt + 1], scalar2=None, op0=ALU.mult),
                          reads=[("pb", b), ("s4", io)], writes=[("mixtm", tt, h)])
            else:
                branch = 2 if kind == "W" else 1
                P.dve(lambda e: e.tensor_scalar(out=s4[io][:], in0=pv[:, :, 64], scalar1=1e-30, scalar2=None, op0=ALU.max),
                      reads=[("pb", b)], writes=[("s4", io)])
                P.dve(lambda e: e.reciprocal(out=s4[io][:], in_=s4[io][:]), reads=[("s4", io)], writes=[("s4", io)])
                P.dve(lambda e: e.tensor_tensor(out=s4[io][:], in0=s4[io][:], in1=gates[:, :, branch * 8 + h], op=ALU.mult),
                      reads=[("s4", io), "gates"], writes=[("s4", io)])
                for tt in range(4):
                    if kind == "W":
                        P.dve(lambda e, tt=tt: e.scalar_tensor_tensor(
                            out=accb[:, tt, h * 64:(h + 1) * 64], in0=pv[:, tt, 0:64], scalar=s4[io][:, tt:tt + 1],
                            in1=accb[:, tt, h * 64:(h + 1) * 64], op0=ALU.mult, op1=ALU.add),
                            reads=[("pb", b), ("s4", io), ("accb", tt, h)], writes=[("accb", tt, h)])
                    else:
                        P.dve(lambda e, tt=tt: e.scalar_tensor_tensor(
                            out=mixtm[:, tt, 512 + h * 64:512 + (h + 1) * 64], in0=pv[:, tt, 0:64],
                            scalar=s4[io][:, tt:tt + 1], in1=accb[:, tt, h * 64:(h + 1) * 64], op0=ALU.mult, op1=ALU.add),
                            reads=[("pb", b), ("s4", io), ("accb", tt, h)], writes=[("mixtm", tt, 8 + h)])

        pending = []
        n1 = sum(len(hd["kts"]) for hd in heads if hd["kind"] != "S")
        inject = {1 + k * (n1 // 5): k for k in range(4)}
        def mix_tr(c):
            tr_to(hT[:, c, :], [mixtm[:, tt, c * 128:(c + 1) * 128] for tt in range(4)], ["hT"],
                  [("mixtm", tt, h) for tt in range(4) for h in (2 * c, 2 * c + 1)], use_act=False)
        early = {n1 + 8 + 3 * c: c for c in range(4)} if NT > n1 + 24 else {}
        for i in range(NT + DPIPE + 3):
            if i in inject:
                cmp_tt(inject[i])
            if i in early:
                mix_tr(early[i])
            if i < NT:
                stage_qk(i)
            if DPIPE <= i < NT + DPIPE:
                if stage_pv(i - DPIPE):
                    hi_ = tiles[i - DPIPE][0]
                    fin_copy(hi_)
                    pending.append((i + 2, hi_))
            while pending and pending[0][0] <= i:
                fin_rest(pending.pop(0)[1])
        while pending:
            fin_rest(pending.pop(0)[1])
        tap("mix", tb, lambda tt: mixtm[:, tt, :], lambda tt: [("mixtm", tt, h) for h in range(16)])
        for c in range(8):
            if c < 4 and early:
                continue
            tr_to(hT[:, c, :], [mixtm[:, tt, c * 128:(c + 1) * 128] for tt in range(4)], ["hT"],
                  [("mixtm", tt, h) for tt in range(4) for h in (2 * c, 2 * c + 1)], use_act=(c % 2 == 0))
        rowproj(tb, s_wmo, "s_wmo", 8, lambda j, tt: hT[:, j, tt * 128:(tt + 1) * 128], lambda j: ["hT"], 1.0, wmo_d)
        tap("x2", tb, lambda tt: xs[:, tt, :], lambda tt: [("xs", tt)])
        rms_to_hT(tb, 2)
        P.dve(lambda e: e.memset(small[:, 62:63], 0.0), reads=["arena"], writes=["arena"])
        ffn(tb, s_w2in, s_w2out, "s_w2in", "s_w2out", w2in_d, w2out_d)
        for tt in range(4):
            r0 = t0 + tt * 128
            P.dma("pool", lambda e, tt=tt, r0=r0: e.dma_start(out=out_d[r0:r0 + 128, :], in_=xs[:, tt, :]), "ost",
                  reads=[("xs", tt)])
    for tb in range(NB):
        do_block(tb)
    P.emit()
    return nc


def _perm_cols():
    QA, KVA, QB = 512, 128, 512
    OFF_KA, OFF_VA, OFF_QB, OFF_KVB, OFF_GB = 512, 640, 768, 1280, 2048
    cols = []
    for r in range(4):
        for g in range(2):
            h = g * 4 + r
            cols += list(range(h * 64, h * 64 + 64))
    cols += list(range(OFF_KA, OFF_KA + 256))
    for r in range(4):
        for g in range(2):
            h = g * 4 + r
            cols += list(range(OFF_QB + h * 64, OFF_QB + h * 64 + 64))
    kvb = lambda i, g: list(range(OFF_KVB + i * 128 + g * 64, OFF_KVB + i * 128 + g * 64 + 64))
    cols += kvb(0, 0) + kvb(1, 0) + kvb(0, 1) + kvb(1, 1)
    cols += list(range(OFF_KVB + 256, OFF_KVB + 512))
    cols += list(range(OFF_KVB + 512, OFF_KVB + 768))
    cols += list(range(OFF_GB, OFF_GB + 24))
    return np.array(cols)


def prep_shared(inp):
    f = np.float32
    out = {}

    def win(w):
        w = np.asarray(w, f).reshape(8, 128, 2, NJ, 128)
        return np.ascontiguousarray(w.transpose(3, 1, 2, 0, 4)).reshape(NJ, 128, 2048)

    out["w1in"] = win(inp["ffn1_w_in"][0])
    out["w2in"] = win(inp["ffn2_w_in"][0])
    out["w1out"] = np.ascontiguousarray(np.asarray(inp["ffn1_w_out"][0], f).reshape(NJ, 128, D))
    out["w2out"] = np.ascontiguousarray(np.asarray(inp["ffn2_w_out"][0], f).reshape(NJ, 128, D))
    wm = np.asarray(inp["w_mix_in"][0], f)[:, _perm_cols()]
    wmp = np.zeros((1024, 9 * 256), f)
    wmp[:, :wm.shape[1]] = wm
    wmp = wmp.reshape(8, 128, 9, 256)
    out["wmi"] = np.ascontiguousarray(wmp.transpose(2, 1, 0, 3)).reshape(9, 128, 2048)
    out["wmo"] = np.ascontiguousarray(np.asarray(inp["w_mix_out"][0], f).reshape(8, 128, D))
    g3 = np.stack([inp["ffn1_norm"][0], inp["mix_norm"][0], inp["ffn2_norm"][0]]).astype(f)
    out["gT"] = np.ascontiguousarray(g3.reshape(3, 8, 128).transpose(2, 0, 1)).reshape(128, 24)
    hg = np.stack([inp["q_norm_a"][0], inp["k_norm_a"][0], inp["q_norm_b"][0], inp["k_norm_b"][0]]).astype(f)
    out["hg"] = np.ascontiguousarray(np.broadcast_to(hg.reshape(1, 256), (128, 256)))
    tbl = np.asarray(inp["rel_bias_table"], f)
    out["tblrep"] = np.ascontiguousarray(np.broadcast_to(tbl.reshape(1, 512), (128, 512)))
    out["sinkrep"] = np.ascontiguousarray(np.broadcast_to(np.asarray(inp["sinks_a"][0], f).reshape(1, 8), (128, 8)))
    out["b2rep"] = np.ascontiguousarray(np.broadcast_to(np.asarray(inp["cmp_b2"][0], f).reshape(1, 128), (128, 128)))
    out["b1T"] = np.ascontiguousarray(np.asarray(inp["cmp_b1"][0], f).T)
    pos = np.asarray(inp["cmp_pos"][0], f)
    out["posT"] = np.ascontiguousarray(pos.transpose(0, 2, 1)).reshape(128, 32)
    w1 = np.asarray(inp["cmp_w1"][0], f).reshape(2, 32, 64, 128)
    out["w1c"] = np.ascontiguousarray(w1.transpose(0, 2, 1, 3)).reshape(128, 4096)
    w2 = np.asarray(inp["cmp_w2"][0], f)
    out["w2c"] = np.ascontiguousarray(w2.transpose(1, 0, 2)).reshape(128, 128)
    TAB, MC = _host_tables(tbl)
    out["TAB"] = TAB.reshape(128, 4096)
    out["MC"] = MC.reshape(128, 384)
    FE, WADD, E, ident = _const_tables()
    out["FE"], out["WADD"], out["E"], out["ident"] = FE, WADD, E, ident
    return out


_CACHE = {}


def kernel(**inputs):
    x = np.asarray(inputs["x"], np.float32)
    B = x.shape[0]
    shared = prep_shared(inputs)
    if "nc" not in _CACHE:
        _CACHE["nc"] = build(8)
    nc = _CACHE["nc"]
    in_maps = []
    for b in range(B):
        m = dict(shared)
        m["x"] = np.ascontiguousarray(x[b])
        in_maps.append(m)
    res = run_bass_kernel_spmd(nc, in_maps, core_ids=list(range(B)))
    return np.stack([np.asarray(r["out"], np.float32) for r in res.results], axis=0)
```

```python
import contextlib
import numpy as np
import concourse.bass as bass
import concourse.mybir as mybir
from concourse.bass_utils import run_bass_kernel_spmd

F32 = mybir.dt.float32
BF16 = mybir.dt.bfloat16
ALU = mybir.AluOpType
AF = mybir.ActivationFunctionType
AX = mybir.AxisListType

D = 1024
DFF = 2816
NJ = DFF // 128
SEQ = 4096
NEGM = -30000.0
ARENA_KEYS = {"pT", "s_sb", "oaug", "accb"}
EPS = 1e-6


class Op:
    __slots__ = ("q", "fn", "reads", "writes", "dsem", "waits", "inc", "idx", "hasdep")

    def __init__(self, q, fn, reads, writes, dsem):
        self.q = q
        self.fn = fn
        self.reads = tuple(reads)
        self.writes = tuple(writes)
        self.dsem = dsem
        self.waits = []
        self.inc = None
        self.hasdep = False


class Prog:
    def __init__(self, nc):
        self.nc = nc
        self.ops = []
        self.stack = contextlib.ExitStack()
        self.sems = {}
        self.nsb = 0

    def sb(self, shape, dtype, name=None):
        self.nsb += 1
        name = "sb_" + (name or f"t{self.nsb}")
        return self.stack.enter_context(self.nc.sbuf_tensor(name, list(shape), dtype))

    def ps(self, shape, dtype, name=None):
        self.nsb += 1
        name = "ps_" + (name or f"t{self.nsb}")
        return self.stack.enter_context(self.nc.psum_tensor(name, list(shape), dtype))

    def sem(self, key):
        if key not in self.sems:
            nm = "s_" + "_".join(str(k) for k in (key if isinstance(key, tuple) else (key,)))
            nm = nm.replace("(", "").replace(")", "").replace(",", "_").replace(" ", "").replace("'", "")
            self.sems[key] = self.stack.enter_context(self.nc.semaphore(nm))
        return self.sems[key]

    def op(self, q, fn, reads=(), writes=(), dsem=None):
        pr = [k for k in reads if isinstance(k, tuple) and k[0] == "pb"]
        if pr:
            reads = [k for k in reads if k not in pr]
            writes = list(writes) + [k for k in pr if k not in writes]
        if any((k if isinstance(k, str) else k[0]) in ARENA_KEYS for k in list(reads) + list(writes)):
            if "arena" not in reads and "arena" not in writes:
                reads = list(reads) + ["arena"]
        o = Op(q, fn, reads, writes, dsem)
        o.idx = len(self.ops)
        self.ops.append(o)
        return o

    def pe(self, fn, reads=(), writes=()):
        return self.op("pe", fn, reads, writes)

    def act(self, fn, reads=(), writes=()):
        return self.op("act", fn, reads, writes)

    def dve(self, fn, reads=(), writes=()):
        return self.op("dve", fn, reads, writes)

    def dma(self, q, fn, dsem, reads=(), writes=()):
        return self.op(q, fn, reads, writes, dsem=dsem)

    def schedule(self):
        last_write = {}
        readers = {}
        deps_of = []
        for o in self.ops:
            deps = {}
            for r in o.reads:
                j = last_write.get(r)
                if j is not None:
                    deps[j] = True
            for w in o.writes:
                j = last_write.get(w)
                if j is not None:
                    deps.setdefault(j, False)
                for j in readers.get(w, ()):
                    deps.setdefault(j, False)
            keep = []
            for j, raw in deps.items():
                p = self.ops[j]
                if j == o.idx:
                    continue
                if p.dsem is None and p.q == o.q and o.dsem is None:
                    if not raw or o.q == "pe":
                        continue
                keep.append(j)
            deps_of.append(keep)
            for j in keep:
                self.ops[j].hasdep = True
            for r in o.reads:
                readers.setdefault(r, []).append(o.idx)
            for w in o.writes:
                last_write[w] = o.idx
                readers[w] = []
        cnt = {}
        for o in self.ops:
            if o.dsem is not None:
                k = ("d", o.dsem)
                cnt[k] = cnt.get(k, 0) + 16
                o.inc = (k, 16, cnt[k])
            elif o.hasdep:
                k = ("e", o.q)
                cnt[k] = cnt.get(k, 0) + 1
                o.inc = (k, 1, cnt[k])
        self.final_counts = dict(cnt)
        for o in self.ops:
            if o.dsem in ("c0", "c1"):
                k = ("d", o.dsem)
                o.inc = (k, 16, cnt[k])
        known = {}
        for o in self.ops:
            kn = known.setdefault(o.q, {})
            need = {}
            for j in deps_of[o.idx]:
                k, _, v = self.ops[j].inc
                if kn.get(k, 0) >= v:
                    continue
                need[k] = max(need.get(k, 0), v)
            for k, v in need.items():
                kn[k] = v
            o.waits = list(need.items())
        for k in cnt:
            self.sem(k)

    def replay(self, q, eng):
        for o in self.ops:
            if o.q != q:
                continue
            for k, v in o.waits:
                eng.wait_ge(self.sems[k], v)
            ins = o.fn(eng)
            if o.inc is not None:
                k, step, _ = o.inc
                ins.then_inc(self.sems[k], step)

    def emit(self, final_q="sp"):
        self.schedule()
        nc = self.nc
        qs = sorted({o.q for o in self.ops} | {final_q})
        with nc.Block() as block:
            def mk(q):
                def body(eng):
                    self.replay(q, eng)
                    if q == final_q:
                        for k, v in self.final_counts.items():
                            if k[0] == "d":
                                eng.wait_ge(self.sems[k], v)
                return body
            for q in qs:
                dec = {"pe": block.tensor, "act": block.scalar, "dve": block.vector,
                       "pool": block.gpsimd, "sp": block.sync}[q]
                dec(mk(q))
        self.stack.close()


def _bucket(n):
    n = np.maximum(n, 0)
    nf = np.maximum(n, 1).astype(np.float32)
    large = 16 + (np.log(nf / np.float32(16)) / np.float32(np.log(8.0)) * np.float32(16)).astype(np.int32)
    large = np.minimum(large, 31)
    return np.where(n < 16, n, large)


def _host_tables(tbl):
    ext = np.concatenate([tbl, np.full((1, 16), NEGM, np.float32)], axis=0)
    kl = np.arange(128)[:, None]
    v = np.arange(256)[None, :]
    rel = v - kl
    idxA = np.where((rel >= 0) & (rel < 128), _bucket(rel), 32)
    idxB = np.where(rel >= 0, _bucket(rel), 32)
    TAB = np.empty((128, 16, 256), np.float32)
    for h in range(8):
        TAB[:, h, :] = ext[idxA, h]
        TAB[:, 8 + h, :] = ext[idxB, 8 + h]
    ql = np.arange(128)[:, None]
    c2 = np.arange(-8, 8)[None, :]
    relc = ql - 16 * c2 - 15
    idxC = np.where(relc >= 0, _bucket(relc), 32)
    MC = np.empty((128, 8, 3, 16), np.float32)
    for var in range(3):
        idx = idxC.copy()
        if var == 0:
            idx[:, :9] = 32
        elif var == 1:
            idx[:, 0] = 32
        for h in range(8):
            MC[:, h, var, :] = ext[idx, 8 + h]
    return TAB, MC


def _const_tables():
    kl = np.arange(128)[:, None]
    v2 = np.arange(128)[None, :]
    FE = np.where(v2 >= kl, NEGM, 0.0).astype(np.float32)
    ql = np.arange(128)[:, None]
    u = np.arange(128)[None, :]
    m = u - 62
    cur = ql // 64
    WADD = np.where(m > cur, -1e9, np.where(m >= cur - 1, 1000.0, 0.0)).astype(np.float32)
    E = np.zeros((128, SEQ), np.float32)
    k = np.arange(SEQ)
    E[k // 64, k] = 1.0
    ident = np.eye(128, dtype=np.float32)
    return FE, WADD, E, ident


def build(NB=8, taps=()):
    S = NB * 512
    nc = bass.Bass("TRN2", target_bir_lowering=False)

    def din(name, shape):
        return nc.dram_tensor(name, list(shape), F32, kind="ExternalInput").ap()

    def dscr(name, shape, dt=BF16):
        return nc.dram_tensor(name, list(shape), dt).ap()

    x_d = din("x", [S, D])
    w1in_d = din("w1in", [NJ, 128, 2048])
    w1out_d = din("w1out", [NJ, 128, D])
    w2in_d = din("w2in", [NJ, 128, 2048])
    w2out_d = din("w2out", [NJ, 128, D])
    wmi_d = din("wmi", [9, 128, 2048])
    wmo_d = din("wmo", [8, 128, D])
    gT_d = din("gT", [128, 24])
    hg_d = din("hg", [128, 256])
    tbl_d = din("tblrep", [128, 512])
    sink_d = din("sinkrep", [128, 8])
    b2_d = din("b2rep", [128, 128])
    b1_d = din("b1T", [128, 2])
    pos_d = din("posT", [128, 32])
    w1c_d = din("w1c", [128, 4096])
    w2c_d = din("w2c", [128, 128])
    tab_d = din("TAB", [128, 4096])
    mc_d = din("MC", [128, 384])
    fe_d = din("FE", [128, 128])
    wadd_d = din("WADD", [128, 128])
    e_d = din("E", [128, SEQ])
    id_d = din("ident", [128, 128])
    out_d = nc.dram_tensor("out", [S, D], F32, kind="ExternalOutput").ap()
    tap_d = {}
    for t in taps:
        w = {"x1": D, "mix": D, "x2": D}[t]
        tap_d[t] = nc.dram_tensor("tap_" + t, [S, w], F32, kind="ExternalOutput").ap()

    s_w1in = dscr("s_w1in", [NJ, 128, 2048])
    s_w1out = dscr("s_w1out", [NJ, 128, D])
    s_w2in = dscr("s_w2in", [NJ, 128, 2048])
    s_w2out = dscr("s_w2out", [NJ, 128, D])
    s_wmi = dscr("s_wmi", [9, 128, 2048])
    s_wmo = dscr("s_wmo", [8, 128, D])

    P = Prog(nc)
    xs = P.sb([128, 4, D], F32, "xs")
    hT = P.sb([128, 8, 512], BF16, "hT")
    hb = [P.sb([128, D], BF16, f"hb{i}") for i in range(2)]
    arena = P.sb([128, NJ * 512], BF16, "arena")
    aT = arena[:].rearrange("p (j t) -> p j t", j=NJ)
    NWA, NWB = 4, 4
    wA = [P.sb([128, 2048], BF16, f"wA{i}") for i in range(NWA)]
    wB = [P.sb([128, D], BF16, f"wB{i}") for i in range(NWB)]
    sg = [P.sb([128, 512], F32, f"sg{i}") for i in range(2)]
    qT_a = P.sb([128, 8, 512], BF16, "qT_a")
    qT_b = P.sb([128, 8, 512], BF16, "qT_b")
    qst = P.sb([128, 4, 1024], BF16, "qst")
    kst = P.sb([128, 4, 128], BF16, "kst")
    rawst = P.sb([128, 4, 256], BF16, "rawst")
    negmT = P.sb([128, 2, 512], BF16, "negmT")
    gates = P.sb([128, 4, 24], F32, "gates")
    mixtm = P.sb([128, 4, D], BF16, "mixtm")
    kT_a = P.sb([128, 1024], BF16, "kT_a")
    kT_w = P.sb([128, 1024], BF16, "kT_w")
    kT_s = P.sb([128, SEQ], BF16, "kT_s")
    v_a = P.sb([128, 8, 2, 65], BF16, "v_a")
    v_w = P.sb([128, 8, 2, 65], BF16, "v_w")
    v_s = P.sb([128, 32, 2, 65], BF16, "v_s")
    Esb = P.sb([128, SEQ], BF16, "Esb")
    k_cT = P.sb([128, 256], BF16, "k_cT")
    v_c = P.sb([128, 2, 2, 64], BF16, "v_c")
    rawT = P.sb([128, 2, 528], BF16, "rawT")
    W1 = P.sb([128, 32, 128], BF16, "W1")
    w2c = P.sb([128, 2, 64], BF16, "w2c")
    TABh = P.sb([128, 16, 256], BF16, "TABh")
    TABl = P.sb([128, 16, 256], BF16, "TABl")
    tabf = [P.sb([128, 256], F32, f"tabf{i}") for i in range(2)]
    FEb = P.sb([128, 128], BF16, "FEb")
    MC = P.sb([128, 8, 3, 16], F32, "MC")
    FE = P.sb([128, 128], F32, "FE")
    WADD = P.sb([128, 128], F32, "WADD")
    identf = P.sb([128, 128], F32, "identf")
    identb = P.sb([128, 128], BF16, "identb")
    gT = P.sb([128, 3, 8], F32, "gT")
    hg = P.sb([128, 4, 64], F32, "hg")
    tblr = P.sb([128, 32, 16], F32, "tblr")
    sinkr = P.sb([128, 8], F32, "sinkr")
    b2r = P.sb([128, 2, 64], F32, "b2r")
    b1T = P.sb([128, 2], F32, "b1T")
    posT = P.sb([128, 32], BF16, "posT")
    b1pw = P.sb([128, 2], F32, "b1pw")
    Ch = P.sb([128, 16], F32, "Ch")
    negC = P.sb([128, 16], F32, "negC")
    b31mC = P.sb([128, 16], F32, "b31mC")
    esink = P.sb([128, 8], F32, "esink")
    small = P.sb([128, 64], F32, "small")
    ss4 = P.sb([128, 4], F32, "ss4")
    rstd4 = P.sb([128, 4], F32, "rstd4")
    sq = P.sb([128, 4, 256], F32, "sq")
    t1 = P.sb([128, 4, 256], F32, "t1")
    ssh = P.sb([128, 16], F32, "ssh")
    rsh = P.sb([128, 16], F32, "rsh")
    NPT, NSSB, NOA = 6, 3, 2
    accb = arena[:, 0:4096].bitcast(F32).rearrange("p (t c) -> p t c", t=4)
    oaug = [arena[:, 4096 + i * 1024:4096 + (i + 1) * 1024].bitcast(F32) for i in range(NOA)]
    pT = [arena[:, 6144 + i * 512:6144 + (i + 1) * 512] for i in range(NPT)]
    s_sb = [arena[:, 9216 + i * 512:9216 + (i + 1) * 512].bitcast(F32) for i in range(NSSB)]
    s4 = [P.sb([128, 4], F32, f"s4_{i}") for i in range(NOA)]
    e_t = [P.sb([128, 256], F32, f"e_t{i}") for i in range(4)]
    den2 = [P.sb([128, 2], F32, f"den2_{i}") for i in range(4)]
    rden = [P.sb([128, 1], F32, f"rden{i}") for i in range(4)]
    pcn = [P.sb([128, 256], BF16, f"pcn{i}") for i in range(4)]
    pcs = [P.sb([128, 264], F32, f"pcs{i}") for i in range(2)]
    pcT = [P.sb([128, 2, 128], BF16, f"pcT{i}") for i in range(4)]
    snear = [P.sb([128, 16], F32, f"snear{i}") for i in range(4)]
    imp = [P.sb([128, 64], F32, f"imp{i}") for i in range(2)]
    score = [P.sb([128, 64], F32, f"score{i}") for i in range(2)]
    sc2 = [P.sb([128, 64], F32, f"sc2{i}") for i in range(2)]
    mx8 = [P.sb([128, 16], F32, f"mx8{i}") for i in range(2)]
    selm = [P.sb([128, 64], F32, f"selm{i}") for i in range(2)]
    nmb = [P.sb([128, 64], BF16, f"nmb{i}") for i in range(2)]
    hid = P.sb([128, 2, 64], BF16, "hid")
    kcf = P.sb([128, 2, 64], F32, "kcf")
    kcst = P.sb([128, 128], BF16, "kcst")
    vcst = P.sb([128, 2, 64], BF16, "vcst")
    pb = [P.ps([128, 512], F32, f"pb{i}") for i in range(8)]
    pbb = [p[:].bitcast(BF16) for p in pb]

    st = {"tr": 0, "acc": 0, "wa": 0, "wb": 0, "sg": 0, "pt": 0, "ssb": 0, "oa": 0, "cm": 0}

    def bank():
        b = st["tr"] % 6
        st["tr"] += 1
        return b

    def accbank():
        b = 6 + st["acc"] % 2
        st["acc"] += 1
        return b

    def rot(name, n):
        i = st[name] % n
        st[name] += 1
        return i

    def ld(q, dst, src, key, dsem="c0"):
        P.dma(q, lambda e, d=dst, s=src: e.dma_start(out=d, in_=s), dsem, writes=[key])

    ld("sp", gT[:].rearrange("p a b -> p (a b)"), gT_d, "gT")
    ld("sp", hg[:].rearrange("p a b -> p (a b)"), hg_d, "hg")
    ld("sp", tblr[:].rearrange("p a b -> p (a b)"), tbl_d, "tblr")
    ld("sp", sinkr[:], sink_d, "sinkr")
    ld("sp", b2r[:].rearrange("p a b -> p (a b)"), b2_d, "b2r")
    ld("sp", b1T[:], b1_d, "b1T")
    ld("sp", MC[:].rearrange("p a b c -> p (a b c)"), mc_d, "MC")
    ld("sp", FE[:], fe_d, "FE")
    ld("sp", WADD[:], wadd_d, "WADD")
    ld("sp", identf[:], id_d, "identf")
    ld("pool", posT[:], pos_d, "posT", "c1")
    ld("pool", W1[:].rearrange("p a b -> p (a b)"), w1c_d, "W1", "c1")
    ld("pool", w2c[:].rearrange("p a b -> p (a b)"), w2c_d, "w2c", "c1")
    ld("pool", identb[:], id_d, "identb", "c1")
    ld("pool", Esb[:], e_d, "Esb", "c1")

    cvn = [0]

    def wload(tb, dst_tile, slotkey, s_w, w_d, key, j):
        if tb == 0:
            P.dma("pool", lambda e: e.dma_start(out=dst_tile[:], in_=w_d[j]), slotkey, writes=[slotkey])
            k = cvn[0] % 8
            cvn[0] += 1
            P.dma("sp", lambda e: e.dma_start(out=s_w[j], in_=dst_tile[:]), ("wst", k), reads=[slotkey],
                  writes=[(key, j), ("wstslot", k)])
        else:
            P.dma("sp", lambda e: e.dma_start(out=dst_tile[:], in_=s_w[j]), slotkey, reads=[(key, j)], writes=[slotkey])

    P.dve(lambda e: e.memset(v_a[:], 1.0), writes=["v_a_all"])
    P.dve(lambda e: e.memset(v_w[:], 1.0), writes=["v_w_all"])
    P.dve(lambda e: e.memset(v_s[:], 1.0), writes=["v_s_all"])
    P.dve(lambda e: e.memset(v_c[:], 0.0), writes=["v_c"])
    P.dve(lambda e: e.memset(rawT[:], 0.0), writes=["rawT"])
    P.dve(lambda e: e.memset(negmT[:], 0.0), writes=["negmT"])
    for i in range(2):
        P.dve(lambda e, i=i: e.memset(pcs[i][:], 0.0), writes=[("pcs", i)])
    for i in range(4):
        P.dve(lambda e, i=i: e.memset(pcn[i][:], 0.0), writes=[("pcn", i)])
    P.dve(lambda e: e.memset(k_cT[:], 0.0), writes=["k_cT"])
    P.dve(lambda e: e.memset(qT_a[:], 0.0), writes=["qT" + str(id(qT_a))])
    P.dve(lambda e: e.memset(qT_b[:], 0.0), writes=["qT" + str(id(qT_b))])

    P.dve(lambda e: e.tensor_scalar(out=hg[:, 0, :], in0=hg[:, 0, :], scalar1=0.125, scalar2=None, op0=ALU.mult),
          reads=["hg"], writes=["hg"])
    P.dve(lambda e: e.tensor_scalar(out=hg[:, 2, :], in0=hg[:, 2, :], scalar1=0.125, scalar2=None, op0=ALU.mult),
          reads=["hg"], writes=["hg"])
    for br, col in ((0, 0), (2, 1)):
        P.dve(lambda e, br=br: e.tensor_tensor(out=small[:, 0:64], in0=hg[:, br, :], in1=hg[:, br + 1, :], op=ALU.mult),
              reads=["hg", "small"], writes=["small"])
        P.dve(lambda e, col=col: e.tensor_reduce(out=ssh[:, col:col + 1], in_=small[:, 0:64], axis=AX.X, op=ALU.max,
                                                 apply_absolute_value=True),
              reads=["small"], writes=["ssh"])
    P.dve(lambda e: e.tensor_reduce(out=Ch[:], in_=tblr[:].rearrange("p b h -> p h b"), axis=AX.X, op=ALU.max),
          reads=["tblr"], writes=["Ch"])
    for half, col in ((0, 0), (1, 1)):
        P.dve(lambda e, half=half, col=col: e.scalar_tensor_tensor(
            out=Ch[:, half * 8:(half + 1) * 8], in0=ssh[:, col:col + 1].to_broadcast([128, 8]), scalar=64.0,
            in1=Ch[:, half * 8:(half + 1) * 8], op0=ALU.mult, op1=ALU.add), reads=["ssh", "Ch"], writes=["Ch"])
    P.dve(lambda e: e.tensor_scalar(out=negC[:], in0=Ch[:], scalar1=-1.0, scalar2=None, op0=ALU.mult),
          reads=["Ch"], writes=["negC"])
    P.dve(lambda e: e.tensor_tensor(out=b31mC[:], in0=tblr[:, 31, :], in1=Ch[:], op=ALU.subtract),
          reads=["tblr", "Ch"], writes=["b31mC"])
    P.dve(lambda e: e.tensor_copy(out=FEb[:], in_=FE[:]), reads=["FE"], writes=["FEb"])
    for h in range(16):
        i = h % 2
        P.dma("sp", lambda e, h=h, i=i: e.dma_start(out=tabf[i][:], in_=tab_d[:, h * 256:(h + 1) * 256]), ("tabf", i),
              writes=[("tabf", i)])
        P.dve(lambda e, h=h, i=i: e.tensor_scalar(out=tabf[i][:], in0=tabf[i][:], scalar1=tblr[:, 31, h:h + 1], scalar2=None,
                                                  op0=ALU.subtract), reads=[("tabf", i), "tblr"], writes=[("tabf", i)])
        P.dve(lambda e, h=h, i=i: e.tensor_copy(out=TABh[:, h, :], in_=tabf[i][:]), reads=[("tabf", i)], writes=["TAB"])
        P.dve(lambda e, h=h, i=i: e.tensor_tensor(out=tabf[i][:], in0=tabf[i][:], in1=TABh[:, h, :], op=ALU.subtract),
              reads=[("tabf", i), "TAB"], writes=[("tabf", i)])
        P.dve(lambda e, h=h, i=i: e.tensor_copy(out=TABl[:, h, :], in_=tabf[i][:]), reads=[("tabf", i)], writes=["TAB"])
    P.dve(lambda e: e.tensor_tensor(out=esink[:], in0=sinkr[:], in1=Ch[:, 0:8], op=ALU.subtract),
          reads=["sinkr", "Ch"], writes=["esink"])
    P.act(lambda e: e.activation(out=esink[:], in_=esink[:], func=AF.Exp), reads=["esink"], writes=["esink"])
    for kv in range(2):
        def f(e, kv=kv):
            r = slice(64 * kv, 64 * kv + 64)
            ins = None
            for p in range(32):
                ins = e.matmul(pb[kv][:, 0:1], lhsT=W1[r, p, :], rhs=posT[r, p:p + 1], start=(p == 0), stop=(p == 31))
            return ins
        P.pe(f, reads=["W1", "posT"], writes=[("pb", kv)])
        P.dve(lambda e, kv=kv: e.tensor_tensor(out=b1pw[:, kv:kv + 1], in0=pb[kv][:, 0:1], in1=b1T[:, kv:kv + 1], op=ALU.add),
              reads=[("pb", kv), "b1T"], writes=["b1pw"])

    def rms_to_hT(tb, nidx):
        P.dve(lambda e: e.memset(ss4[:], 0.0), reads=["ss4"], writes=["ss4"])
        junkf = t1[:].rearrange("p t c -> p (t c)")
        for tt in range(4):
            P.act(lambda e, tt=tt: e.activation(out=junkf, in_=xs[:, tt, :], func=AF.Square, accum_out=ss4[:, tt:tt + 1]),
                  reads=[("xs", tt), "ss4"], writes=[("t1", t) for t in range(4)] + [("ss4", tt)])
            P.act(lambda e, tt=tt: e.activation(out=rstd4[:, tt:tt + 1], in_=ss4[:, tt:tt + 1], func=AF.Ln, scale=1.0 / D, bias=EPS),
                  reads=[("ss4", tt)], writes=[("rstd4", tt)])
            P.act(lambda e, tt=tt: e.activation(out=rstd4[:, tt:tt + 1], in_=rstd4[:, tt:tt + 1], func=AF.Exp, scale=-0.5),
                  reads=[("rstd4", tt)], writes=[("rstd4", tt)])

        def scale(tt):
            P.dve(lambda e: e.tensor_scalar(out=hb[tt % 2][:], in0=xs[:, tt, :], scalar1=rstd4[:, tt:tt + 1], scalar2=None,
                                            op0=ALU.mult), reads=[("xs", tt), ("rstd4", tt)], writes=[("hb", tt % 2)])

        def trev(tt):
            b = bank()

            def f(e):
                ins = None
                for c in range(8):
                    ins = e.transpose(out=pbb[b][:, c * 128:(c + 1) * 128], in_=hb[tt % 2][:, c * 128:(c + 1) * 128],
                                      identity=identb[:])
                return ins
            P.pe(f, reads=[("hb", tt % 2), "identb"], writes=[("pb", b)])
            P.dve(lambda e: e.tensor_tensor(
                out=hT[:, :, tt * 128:(tt + 1) * 128], in0=pbb[b].rearrange("p (c t) -> p c t", c=8),
                in1=gT[:, nidx, :].unsqueeze(2).to_broadcast([128, 8, 128]), op=ALU.mult),
                reads=[("pb", b), "gT"], writes=["hT"])
        scale(0)
        scale(1)
        trev(0)
        scale(2)
        trev(1)
        scale(3)
        trev(2)
        trev(3)

    def ffn(tb, s_in, s_out, kin, kout, w_in_d, w_out_d):
        for j in range(NJ):
            sl = rot("wa", NWA)
            wload(tb, wA[sl], ("wa", sl), s_in, w_in_d, kin, j)
            wv = wA[sl][:].rearrange("p (u c f) -> p u c f", u=2, c=8)
            bg, bu = bank(), bank()
            for u, b in ((0, bg), (1, bu)):
                def f(e, u=u, b=b, wv=wv):
                    ins = None
                    for c in range(8):
                        ins = e.matmul(pb[b][:, :], lhsT=wv[:, u, c, :], rhs=hT[:, c, :], start=(c == 0), stop=(c == 7))
                    return ins
                P.pe(f, reads=[("wa", sl), "hT"], writes=[("pb", b)])
            k = rot("sg", 2)
            P.act(lambda e, k=k, bg=bg: e.activation(out=sg[k][:], in_=pb[bg][:, :], func=AF.Silu),
                  reads=[("pb", bg)], writes=[("sg", k)])
            P.dve(lambda e, k=k, bu=bu, j=j: e.tensor_tensor(out=aT[:, j, :], in0=pb[bu][:, :], in1=sg[k][:], op=ALU.mult),
                  reads=[("pb", bu), ("sg", k), "arena"], writes=[("aT", j)])
        rowproj(tb, s_out, kout, NJ, lambda j, tt: aT[:, j, tt * 128:(tt + 1) * 128], lambda j: [("aT", j)], 0.5, w_out_d)

    def rowproj(tb, s_w, kw, nchunk, lhs_of, lhs_keys, scale, w_d):
        for j in range(nchunk):
            sl = rot("wb", NWB)
            wload(tb, wB[sl], ("wb", sl), s_w, w_d, kw, j)

            def f(e, sl=sl, j=j):
                ins = None
                for tt in range(4):
                    for hf in range(2):
                        ins = e.matmul(pb[tt * 2 + hf][:, :], lhsT=lhs_of(j, tt), rhs=wB[sl][:, hf * 512:(hf + 1) * 512],
                                       start=(j == 0), stop=(j == nchunk - 1))
                return ins
            P.pe(f, reads=[("wb", sl)] + lhs_keys(j), writes=[("pb", i) for i in range(8)])
        for tt in range(4):
            for hf in range(2):
                b = tt * 2 + hf
                P.dve(lambda e, tt=tt, hf=hf, b=b: e.scalar_tensor_tensor(
                    out=xs[:, tt, hf * 512:(hf + 1) * 512], in0=pb[b][:, :], scalar=scale,
                    in1=xs[:, tt, hf * 512:(hf + 1) * 512], op0=ALU.mult, op1=ALU.add),
                    reads=[("pb", b), ("xs", tt)], writes=[("xs", tt)])

    def tap(name, tb, src_of_tt, keys_of_tt):
        if name not in tap_d:
            return
        for tt in range(4):
            r0 = tb * 512 + tt * 128
            P.dma("pool", lambda e, tt=tt, r0=r0: e.dma_start(out=tap_d[name][r0:r0 + 128, :], in_=src_of_tt(tt)),
                  ("tap", name, tt), reads=keys_of_tt(tt))

    def norm_heads(srcs, nh, gidx, dst4, rkeys_l, wkeys, npart=128):
        T = len(srcs)
        n = nh * 64
        pr = slice(0, npart)
        for t, src in enumerate(srcs):
            P.act(lambda e, t=t, src=src: e.activation(out=sq[pr, t, 0:n], in_=src, func=AF.Square),
                  reads=rkeys_l[t], writes=[("sq", t)])
        P.dve(lambda e: e.tensor_reduce(out=ssh[pr, 0:T * nh].rearrange("p (t h) -> p t h", t=T),
                                        in_=sq[pr, 0:T, 0:n].rearrange("p t (h d) -> p t h d", h=nh),
                                        axis=AX.X, op=ALU.add), reads=[("sq", t) for t in range(T)], writes=["ssh"])
        P.act(lambda e: e.activation(out=rsh[pr, 0:T * nh], in_=ssh[pr, 0:T * nh], func=AF.Ln, scale=1.0 / 64, bias=EPS),
              reads=["ssh"], writes=["rsh"])
        P.act(lambda e: e.activation(out=rsh[pr, 0:T * nh], in_=rsh[pr, 0:T * nh], func=AF.Exp, scale=-0.5),
              reads=["rsh"], writes=["rsh"])
        for t, src in enumerate(srcs):
            P.dve(lambda e, t=t, src=src: e.tensor_tensor(
                out=t1[pr, t, 0:n].rearrange("p (h d) -> p h d", h=nh), in0=src.rearrange("p (h d) -> p h d", h=nh),
                in1=rsh[pr, t * nh:(t + 1) * nh].unsqueeze(2).to_broadcast([npart, nh, 64]), op=ALU.mult),
                reads=rkeys_l[t] + ["rsh"], writes=[("t1", t)])
        P.dve(lambda e: e.tensor_tensor(out=dst4.rearrange("p t (h d) -> p t h d", h=nh),
                                        in0=t1[pr, 0:T, 0:n].rearrange("p t (h d) -> p t h d", h=nh),
                                        in1=hg[pr, gidx, :].unsqueeze(1).unsqueeze(1).to_broadcast([npart, T, nh, 64]), op=ALU.mult),
              reads=[("t1", t) for t in range(T)] + ["hg"], writes=wkeys)

    def tr_to(dst, src_list, wkeys, rkeys, nrow_out=128, use_act=True, b=None):
        if b is None:
            b = bank()

        def f(e):
            ins = None
            for i, s in enumerate(src_list):
                ins = e.transpose(out=pbb[b][0:nrow_out, i * 128:(i + 1) * 128], in_=s, identity=identb[:])
            return ins
        P.pe(f, reads=rkeys + ["identb"], writes=[("pb", b)])
        n = len(src_list) * 128
        if use_act:
            P.act(lambda e: e.copy(out=dst, in_=pbb[b][0:nrow_out, 0:n]), reads=[("pb", b)], writes=wkeys)
        else:
            P.dve(lambda e: e.tensor_copy(out=dst, in_=pbb[b][0:nrow_out, 0:n]), reads=[("pb", b)], writes=wkeys)

    def pieces_for(j, c0, c1, edge):
        out = []
        for qs in range(c0 // 128, c1 // 128):
            d = qs - j
            kind = "near" if d <= 1 else ("edge" if (edge and d == 4) else "far")
            if out and out[-1][0] == kind and kind != "edge":
                out[-1][2] = (qs + 1) * 128
            else:
                out.append([kind, qs * 128, (qs + 1) * 128])
        return out

    def do_block(tb):
        t0 = tb * 512
        for tt in range(4):
            P.dma("sp", lambda e, t0=t0, tt=tt: e.dma_start(out=xs[:, tt, :], in_=x_d[t0 + tt * 128:t0 + (tt + 1) * 128, :]),
                  ("xld", tt), writes=[("xs", tt)])
        rms_to_hT(tb, 0)
        ffn(tb, s_w1in, s_w1out, "s_w1in", "s_w1out", w1in_d, w1out_d)
        tap("x1", tb, lambda tt: xs[:, tt, :], lambda tt: [("xs", tt)])
        rms_to_hT(tb, 1)
        P.dve(lambda e: e.memset(small[:, 63:64], 0.0), writes=[("aT", j) for j in range(NJ)] + ["arena"])
        P.dve(lambda e: e.memset(accb, 0.0), writes=[("accb", t_, h_) for t_ in range(4) for h_ in range(8)])
        def proj_mm(cg):
            sl = rot("wa", NWA)
            wload(tb, wA[sl], ("wa", sl), s_wmi, wmi_d, "s_wmi", cg)
            wv = wA[sl][:].rearrange("p (c f) -> p c f", c=8)
            bks = [4 * (cg % 2) + i for i in range(4)]
            for tt in range(4):
                def f(e, b=bks[tt], wv=wv, tt=tt):
                    ins = None
                    for c in range(8):
                        ins = e.matmul(pb[b][:, 0:256], lhsT=hT[:, c, tt * 128:(tt + 1) * 128], rhs=wv[:, c, :],
                                       start=(c == 0), stop=(c == 7))
                    return ins
                P.pe(f, reads=[("wa", sl), "hT"], writes=[("pb", bks[tt])])
            return bks

        def proj_post(cg, bks):
            rkl = [[("pb", bks[tt])] for tt in range(4)]
            kt0 = 4 * tb
            if cg in (0, 1, 3, 4):
                off = cg * 256 if cg < 2 else 512 + (cg - 3) * 256
                norm_heads([pb[bks[tt]][:, 0:256] for tt in range(4)], 4, 0 if cg < 2 else 2,
                           qst[:, :, off:off + 256], rkl, [("qst", cg)])
            elif cg in (2, 6, 7):
                norm_heads([pb[bks[tt]][:, 0:128] for tt in range(4)], 2, 1 if cg == 2 else 3, kst[:, :, :], rkl, ["kst"])
                if cg == 2:
                    s0 = (kt0 % 8)
                    dstk, dkey, vt_, vkey, vs0 = kT_a[:, s0 * 128:(s0 + 4) * 128], "kT" + str(id(kT_a)), v_a, "v" + str(id(v_a)), s0
                elif cg == 6:
                    dstk, dkey, vt_, vkey, vs0 = kT_s[:, kt0 * 128:(kt0 + 4) * 128], "kT" + str(id(kT_s)), v_s, "v" + str(id(v_s)), kt0
                else:
                    s0 = (kt0 % 8)
                    dstk, dkey, vt_, vkey, vs0 = kT_w[:, s0 * 128:(s0 + 4) * 128], "kT" + str(id(kT_w)), v_w, "v" + str(id(v_w)), s0
                for tt in range(4):
                    P.act(lambda e, b=bks[tt], dstv=vt_[:, vs0 + tt, :, 0:64]: e.copy(
                        out=dstv, in_=pb[b][:, 128:256].rearrange("p (g d) -> p g d", g=2)),
                        reads=rkl[tt] + ["v_a_all", "v_w_all", "v_s_all"], writes=[vkey])
                tr_to(dstk, [kst[:, tt, :] for tt in range(4)], [dkey], ["kst"], b=bks[0])
            elif cg == 5:
                for tt in range(4):
                    P.act(lambda e, b=bks[tt], tt=tt: e.copy(out=rawst[:, tt, :], in_=pb[b][:, 0:256]), reads=rkl[tt],
                          writes=[("rawst", tt)])
                for tt in range(4):
                    bb = bks[tt]

                    def f2(e, bb=bb, tt=tt):
                        e.transpose(out=pbb[bb][:, 0:128], in_=rawst[:, tt, 0:128], identity=identb[:])
                        return e.transpose(out=pbb[bb][:, 128:256], in_=rawst[:, tt, 128:256], identity=identb[:])
                    P.pe(f2, reads=[("rawst", tt), "identb"], writes=[("pb", bb)])
                    P.dve(lambda e, bb=bb, tt=tt: e.tensor_copy(
                        out=rawT[:, :, 16 + tt * 128:16 + (tt + 1) * 128],
                        in_=pbb[bb][:, 0:256].rearrange("p (g t) -> p g t", g=2)), reads=[("pb", bb)], writes=["rawT"])
            else:
                for tt in range(4):
                    P.act(lambda e, b=bks[tt], tt=tt: e.activation(out=gates[:, tt, :], in_=pb[b][:, 0:24], func=AF.Exp, scale=-1.0),
                          reads=rkl[tt], writes=["gates"])
                P.dve(lambda e: e.tensor_scalar(out=gates[:], in0=gates[:], scalar1=1.0, scalar2=None, op0=ALU.add),
                      reads=["gates"], writes=["gates"])
                P.dve(lambda e: e.reciprocal(out=gates[:], in_=gates[:]), reads=["gates"], writes=["gates"])
            if cg in (1, 4):
                off = 0 if cg == 1 else 512
                qT = qT_a if cg == 1 else qT_b
                for r in range(4):
                    b = bks[r]

                    def ftr(e, b=b, r=r):
                        ins = None
                        for tt in range(4):
                            ins = e.transpose(out=pbb[b][:, tt * 128:(tt + 1) * 128],
                                              in_=qst[:, tt, off + r * 128:off + (r + 1) * 128], identity=identb[:])
                        return ins
                    P.pe(ftr, reads=[("qst", cg - 1), ("qst", cg), "identb"], writes=[("pb", b)])
                    P.act(lambda e, b=b, r=r: e.copy(out=qT[0:64, r * 2, :], in_=pbb[b][0:64, 0:512]),
                          reads=[("pb", b)], writes=["qT" + str(id(qT))])
                    P.dve(lambda e, b=b, r=r: e.tensor_copy(out=qT[64:128, r * 2 + 1, :], in_=pbb[b][64:128, 0:512]),
                          reads=[("pb", b)], writes=["qT" + str(id(qT))])

        pend_b = proj_mm(0)
        for cg in range(9):
            nxt_b = proj_mm(cg + 1) if cg + 1 < 9 else None
            proj_post(cg, pend_b)
            pend_b = nxt_b

        for kv in range(2):
            bq = bank()

            def f(e, kv=kv, bq=bq):
                rr = slice(64 * kv, 64 * kv + 64)
                ins = None
                for g in range(2):
                    for p in range(32):
                        ins = e.matmul(pb[bq][:, g * 32:(g + 1) * 32], lhsT=W1[rr, p, :], rhs=rawT[rr, g, p:p + 497:16],
                                       start=(p == 0), stop=(p == 31), skip_group_check=True)
                return ins
            P.pe(f, reads=["W1", "rawT"], writes=[("pb", bq)])
            P.act(lambda e, kv=kv, bq=bq: e.activation(out=hid[:, kv, :], in_=pb[bq][:, 0:64], func=AF.Silu,
                                                       bias=b1pw[:, kv:kv + 1]), reads=[("pb", bq), "b1pw"], writes=[("hid", kv)])
            bo = bank()

            def f2(e, kv=kv, bo=bo):
                ins = None
                for g in range(2):
                    ins = e.matmul(pb[bo][0:32, g * 64:(g + 1) * 64], lhsT=hid[:, kv, g * 32:(g + 1) * 32], rhs=w2c[:, kv, :],
                                   start=True, stop=True, skip_group_check=True)
                return ins
            P.pe(f2, reads=[("hid", kv), "w2c"], writes=[("pb", bo)])
            if kv == 0:
                P.dve(lambda e, bo=bo: e.tensor_tensor(out=kcf[0:32, :, :], in0=pb[bo][0:32, 0:128].rearrange("p (g d) -> p g d", g=2),
                                                       in1=b2r[0:32, 0, :].unsqueeze(1).to_broadcast([32, 2, 64]), op=ALU.add),
                      reads=[("pb", bo), "b2r"], writes=["kcf"])
                norm_heads([kcf[0:32, :, :].rearrange("p g d -> p (g d)")], 2, 3, kcst[0:32, :].unsqueeze(1), [["kcf"]], ["kcst"],
                           npart=32)
                bt = bank()
                P.pe(lambda e, bt=bt: e.transpose(out=pbb[bt][:, 0:32], in_=kcst[0:32, :], identity=identb[0:32, 0:32]),
                     reads=["kcst", "identb"], writes=[("pb", bt)])
                P.act(lambda e, bt=bt, tb=tb: e.copy(out=k_cT[:, 32 * tb:32 * tb + 32], in_=pbb[bt][:, 0:32]),
                      reads=[("pb", bt)], writes=["k_cT"])
            else:
                P.dve(lambda e, bo=bo: e.tensor_tensor(out=vcst[0:32, :, :], in0=pb[bo][0:32, 0:128].rearrange("p (g d) -> p g d", g=2),
                                                       in1=b2r[0:32, 1, :].unsqueeze(1).to_broadcast([32, 2, 64]), op=ALU.add),
                      reads=[("pb", bo), "b2r"], writes=["vcst"])
                p0 = (32 * tb) % 128
                ct = (32 * tb) // 128
                P.dma("pool", lambda e, p0=p0, ct=ct: e.dma_start(out=v_c[p0:p0 + 32, ct, :, :], in_=vcst[0:32, :, :]),
                      "vc", reads=["vcst"], writes=["v_c"])
        if tb + 1 < NB:
            P.dve(lambda e: e.tensor_copy(out=rawT[:, :, 0:16], in_=rawT[:, :, 512:528]), reads=["rawT"], writes=["rawT"])

        def cmp_tt(tt):
            qt = 4 * tb + tt
            ncol = min(8 * qt + 8, 256)
            ns = max(0, 8 * qt - 8)
            var = min(qt, 2)
            moff = 8 if qt == 0 else 0
            w = ncol - ns
            lo = 0 if qt < 2 else 1
            nct = 1 if ncol <= 128 else 2
            def cmp_g(g):
                rows = slice(64 * g, 64 * g + 64)
                H = [(r, g * 4 + r, 8 + g * 4 + r) for r in range(4)]
                bs = [bank() for _ in range(4)]
                for r, hB, hidx in H:
                    P.pe(lambda e, b=bs[r], r=r: e.matmul(
                        pb[b][:, 0:ncol], lhsT=qT_b[:, r * 2 + g, tt * 128:(tt + 1) * 128], rhs=k_cT[:, 0:ncol],
                        start=True, stop=True), reads=["qT" + str(id(qT_b)), "k_cT"], writes=[("pb", bs[r])])
                    P.dve(lambda e, r=r: e.memset(den2[r][:], 0.0), reads=[("den2", r)], writes=[("den2", r)])
                for r, hB, hidx in H:
                    if ns > 1:
                        P.act(lambda e, b=bs[r], r=r, hidx=hidx: e.activation(
                            out=e_t[r][:, 1:ns], in_=pb[b][:, 1:ns], func=AF.Exp, bias=b31mC[:, hidx:hidx + 1],
                            accum_out=den2[r][:, 0:1]), reads=[("pb", bs[r]), "b31mC", ("den2", r)],
                            writes=[("e_t", r), ("den2", r)])
                    P.dve(lambda e, b=bs[r], r=r, hB=hB: e.tensor_tensor(
                        out=snear[r][:, 0:w], in0=pb[b][:, ns:ncol], in1=MC[:, hB, var, moff:moff + w], op=ALU.add),
                        reads=[("pb", bs[r]), "MC"], writes=[("snear", r)])
                for r, hB, hidx in H:
                    P.act(lambda e, r=r, hidx=hidx: e.activation(
                        out=e_t[r][:, ns:ncol], in_=snear[r][:, 0:w], func=AF.Exp, bias=negC[:, hidx:hidx + 1],
                        accum_out=den2[r][:, 1:2]), reads=[("snear", r), "negC", ("den2", r)],
                        writes=[("e_t", r), ("den2", r)])
                for r, hB, hidx in H:
                    P.dve(lambda e, r=r: e.tensor_scalar(out=rden[r][:], in0=den2[r][:, 0:1], scalar1=den2[r][:, 1:2],
                                                         scalar2=1e-30, op0=ALU.add, op1=ALU.max),
                          reads=[("den2", r)], writes=[("rden", r)])
                for r, hB, hidx in H:
                    P.dve(lambda e, r=r: e.reciprocal(out=rden[r][:], in_=rden[r][:]), reads=[("rden", r)], writes=[("rden", r)])
                for r, hB, hidx in H:
                    P.dve(lambda e, r=r: e.tensor_scalar(
                        out=pcn[r][:, lo:ncol], in0=e_t[r][:, lo:ncol], scalar1=rden[r][:, 0:1], scalar2=None, op0=ALU.mult),
                        reads=[("e_t", r), ("rden", r)], writes=[("pcn", r)])
                bts = []
                for r, hB, hidx in H:
                    bt = bank()
                    bts.append(bt)

                    def f(e, bt=bt, r=r):
                        ins = None
                        for c in range(nct):
                            ins = e.transpose(out=pbb[bt][:, c * 128:(c + 1) * 128], in_=pcn[r][:, c * 128:(c + 1) * 128],
                                              identity=identb[:])
                        return ins
                    P.pe(f, reads=[("pcn", r), "identb"], writes=[("pb", bt)])
                    P.act(lambda e, bt=bt, r=r: e.copy(out=pcT[r][:, 0:nct, :],
                                                       in_=pbb[bt][:, 0:nct * 128].rearrange("p (c q) -> p c q", c=nct)),
                          reads=[("pb", bt)], writes=[("pcT", r)])
                for r, hB, hidx in H:
                    if r == 0:
                        P.dve(lambda e, r=r: e.tensor_scalar(
                            out=pcs[g][:, lo:ncol], in0=e_t[r][:, lo:ncol], scalar1=rden[r][:, 0:1], scalar2=None, op0=ALU.mult),
                            reads=[("e_t", r), ("rden", r)], writes=[("pcs", g)])
                    else:
                        P.dve(lambda e, r=r: e.scalar_tensor_tensor(
                            out=pcs[g][:, lo:ncol], in0=e_t[r][:, lo:ncol], scalar=rden[r][:, 0:1], in1=pcs[g][:, lo:ncol],
                            op0=ALU.mult, op1=ALU.add), reads=[("e_t", r), ("rden", r), ("pcs", g)], writes=[("pcs", g)])
                for r, hB, hidx in H:
                    bo = bank()

                    def f2(e, bo=bo, r=r):
                        ins = None
                        for c in range(nct):
                            ins = e.matmul(pb[bo][:, 0:64], lhsT=pcT[r][:, c, :], rhs=v_c[:, c, g, :], start=(c == 0),
                                           stop=(c == nct - 1))
                        return ins
                    P.pe(f2, reads=[("pcT", r), "v_c"], writes=[("pb", bo)])
                    P.dve(lambda e, bo=bo, hB=hB: e.scalar_tensor_tensor(
                        out=accb[:, tt, hB * 64:(hB + 1) * 64], in0=pb[bo][:, 0:64], scalar=gates[:, tt, hB:hB + 1],
                        in1=accb[:, tt, hB * 64:(hB + 1) * 64], op0=ALU.mult, op1=ALU.add),
                        reads=[("pb", bo), "gates", ("accb", tt, hB)], writes=[("accb", tt, hB)])
            for g in range(2):
                cmp_g(g)
            for stage in range(10):
                for g in range(2):
                    if stage == 0:
                        P.dve(lambda e, g=g: e.tensor_reduce(out=imp[g][:], in_=pcs[g][:, 0:256].rearrange("p (n f) -> p n f", f=4),
                                                             axis=AX.X, op=ALU.add), reads=[("pcs", g)], writes=[("imp", g)])
                    elif stage == 1:
                        P.dve(lambda e, g=g: e.tensor_tensor(
                            out=imp[g][:], in0=imp[g][:], in1=pcs[g][:, 4:260].rearrange("p (n f) -> p n f", f=4)[:, :, 0],
                            op=ALU.add), reads=[("pcs", g), ("imp", g)], writes=[("imp", g)])
                    elif stage == 2:
                        P.dve(lambda e, g=g: e.tensor_tensor(out=score[g][:], in0=imp[g][:], in1=WADD[:, 62 - 2 * qt:126 - 2 * qt],
                                                             op=ALU.add), reads=[("imp", g), "WADD"], writes=[("score", g)])
                    elif stage == 3:
                        if qt >= 1:
                            P.dve(lambda e, g=g: e.tensor_scalar(out=score[g][:, 0:1], in0=score[g][:, 0:1], scalar1=1000.0,
                                                                 scalar2=None, op0=ALU.add), reads=[("score", g)], writes=[("score", g)])
                    elif stage == 4:
                        P.dve(lambda e, g=g: e.max(out=mx8[g][:, 0:8], in_=score[g][:]), reads=[("score", g)], writes=[("mx8", g)])
                    elif stage == 5:
                        P.dve(lambda e, g=g: e.match_replace(out=sc2[g][:], in_to_replace=mx8[g][:, 0:8], in_values=score[g][:],
                                                             imm_value=-3.0e38), reads=[("mx8", g), ("score", g)], writes=[("sc2", g)])
                    elif stage == 6:
                        P.dve(lambda e, g=g: e.max(out=mx8[g][:, 8:16], in_=sc2[g][:]), reads=[("sc2", g), ("mx8", g)],
                              writes=[("mx8b", g)])
                    elif stage == 7:
                        P.dve(lambda e, g=g: e.tensor_scalar(out=selm[g][:], in0=score[g][:], scalar1=mx8[g][:, 15:16], scalar2=None,
                                                             op0=ALU.is_ge), reads=[("score", g), ("mx8b", g)], writes=[("selm", g)])
                    elif stage == 8:
                        P.dve(lambda e, g=g: e.scalar_tensor_tensor(out=selm[g][:], in0=score[g][:], scalar=-5.0e8, in1=selm[g][:],
                                                                    op0=ALU.is_gt, op1=ALU.mult),
                              reads=[("score", g), ("selm", g)], writes=[("selm", g)])
                    else:
                        P.dve(lambda e, g=g: e.tensor_scalar(out=nmb[g][:], in0=selm[g][:], scalar1=-1.0, scalar2=-NEGM,
                                                             op0=ALU.add, op1=ALU.mult), reads=[("selm", g)], writes=[("nmb", g)])
            for g in range(2):
                bt = bank()
                P.pe(lambda e, bt=bt, g=g: e.transpose(out=pbb[bt][0:64, 0:128], in_=nmb[g][:], identity=identb[:]),
                     reads=[("nmb", g), "identb"], writes=[("pb", bt)])
                P.act(lambda e, bt=bt, g=g: e.copy(out=negmT[0:64, g, tt * 128:(tt + 1) * 128], in_=pbb[bt][0:64, 0:128]),
                      reads=[("pb", bt)], writes=[("negmT", g)])

        DPIPE = 5
        tiles = []
        heads = []
        for g in range(2):
            for r in range(4):
                hA = g * 4 + r
                kts = [kt for kt in range(4 * tb - 1, 4 * tb + 4) if kt >= 0]
                heads.append(dict(kind="A", g=g, r=r, hidx=hA, h=hA, kts=kts, kT=kT_a, ks=lambda kt: kt % 8, vt=v_a,
                                  vs=lambda kt: kt % 8, qT=qT_a, edge=False, col_end=256, masked=False))
        for g in range(2):
            for r in range(4):
                hB = g * 4 + r
                kts = [kt for kt in range(4 * tb - 4, 4 * tb + 4) if kt >= 0]
                heads.append(dict(kind="W", g=g, r=r, hidx=8 + hB, h=hB, kts=kts, kT=kT_w, ks=lambda kt: kt % 8, vt=v_w,
                                  vs=lambda kt: kt % 8, qT=qT_b, edge=True, col_end=640, masked=False))
        for g in range(2):
            for r in range(4):
                hB = g * 4 + r
                heads.append(dict(kind="S", g=g, r=r, hidx=8 + hB, h=hB, kts=list(range(0, 4 * tb + 4)), kT=kT_s,
                                  ks=lambda kt: kt, vt=v_s, vs=lambda kt: kt, qT=qT_b, edge=False, col_end=100000, masked=True))
        for hi, hd in enumerate(heads):
            for idx, kt in enumerate(hd["kts"]):
                tiles.append((hi, idx, kt))
        NT = len(tiles)
        tinfo = {}

        def stage_qk(i):
            hi, idx, kt = tiles[i]
            hd = heads[hi]
            g, r, hidx = hd["g"], hd["r"], hd["hidx"]
            kT, qT = hd["kT"], hd["qT"]
            rows = slice(64 * g, 64 * g + 64)
            if idx == 0:
                hd["ba"] = accbank()
            j = kt - 4 * tb
            c0 = max(0, 128 * j)
            c1 = min(512, 128 * j + hd["col_end"])
            n = c1 - c0
            bs = bank()
            ksl = hd["ks"](kt)
            ip = i % NPT
            tinfo[i] = (c0, c1, n, ip)
            pcs_ = pieces_for(j, c0, c1, hd["edge"])
            extra = []
            for kind, a, b_ in pcs_:
                lo_, hi_ = a - c0, b_ - c0
                if kind == "near":
                    v0 = a - 128 * j
                    extra.append((lo_, hi_, TABh[:, hidx, v0:v0 + (hi_ - lo_)]))
                    extra.append((lo_, hi_, TABl[:, hidx, v0:v0 + (hi_ - lo_)]))
                elif kind == "edge":
                    extra.append((lo_, hi_, FEb[:, 0:hi_ - lo_]))
            masked = hd["masked"]

            def f(e):
                last_is_qk = (not masked) and (not extra)
                ins = e.matmul(pb[bs][:, 0:n], lhsT=kT[:, ksl * 128:(ksl + 1) * 128], rhs=qT[:, r * 2 + g, c0:c1],
                               start=True, stop=last_is_qk)
                if masked:
                    ins = e.matmul(pb[bs][:, 0:n], lhsT=Esb[:, kt * 128:(kt + 1) * 128], rhs=negmT[:, g, c0:c1],
                                   start=False, stop=(not extra), skip_group_check=True)
                for xi, (lo_, hi_, ap_) in enumerate(extra):
                    ins = e.matmul(pb[bs][:, lo_:hi_], lhsT=identb[:], rhs=ap_, start=False, stop=(xi == len(extra) - 1),
                                   skip_group_check=True)
                return ins
            rk = ["kT" + str(id(kT)), "qT" + str(id(qT)), "TAB", "FEb", "identb"]
            if masked:
                rk += ["Esb", ("negmT", g)]
            P.pe(f, reads=rk, writes=[("pb", bs)])
            P.act(lambda e: e.activation(out=pT[ip][:, 0:n], in_=pb[bs][:, 0:n], func=AF.Exp, bias=b31mC[:, hidx:hidx + 1]),
                  reads=[("pb", bs), "b31mC"], writes=[("pT", ip)])

        def stage_pv(i):
            hi, idx, kt = tiles[i]
            hd = heads[hi]
            c0, c1, n, ip = tinfo[i]
            ba = hd["ba"]
            vsl = hd["vs"](kt)
            vt, g = hd["vt"], hd["g"]
            last = (idx == len(hd["kts"]) - 1)
            P.pe(lambda e: e.matmul(pb[ba][0:65, c0:c1], lhsT=vt[:, vsl, g, :], rhs=pT[ip][:, 0:n], start=(idx == 0),
                                    stop=last, skip_group_check=True),
                 reads=[("pT", ip), "v" + str(id(vt))], writes=[("pb", ba)])
            return last

        def fin_copy(hi):
            hd = heads[hi]
            io = rot("oa", NOA)
            hd["io"] = io
            ba = hd["ba"]
            if hd["kind"] == "S":
                P.dve(lambda e: e.tensor_copy(out=oaug[io][0:65, :], in_=pb[ba][0:65, :]), reads=[("pb", ba)], writes=[("oaug", io)])
            else:
                P.act(lambda e: e.copy(out=oaug[io][0:65, :], in_=pb[ba][0:65, :]), reads=[("pb", ba)], writes=[("oaug", io)])

        def fin_rest(hi):
            hd = heads[hi]
            io = hd["io"]
            h, kind = hd["h"], hd["kind"]
            b = bank()

            def f(e):
                ins = None
                for tt in range(4):
                    ins = e.transpose(out=pb[b][:, tt * 65:(tt + 1) * 65], in_=oaug[io][0:65, tt * 128:(tt + 1) * 128],
                                      identity=identf[0:65, 0:65])
                return ins
            P.pe(f, reads=[("oaug", io), "identf"], writes=[("pb", b)])
            pv = pb[b][:, 0:260].rearrange("p (t c) -> p t c", t=4)
            if kind == "A":
                P.dve(lambda e: e.tensor_scalar(out=s4[io][:], in0=pv[:, :, 64], scalar1=esink[:, h:h + 1], scalar2=None,
                                                op0=ALU.add), reads=[("pb", b), "esink"], writes=[("s4", io)])
                P.dve(lambda e: e.reciprocal(out=s4[io][:], in_=s4[io][:]), reads=[("s4", io)], writes=[("s4", io)])
                for tt in range(4):
                    P.dve(lambda e, tt=tt: e.tensor_scalar(out=mixtm[:, tt, h * 64:(h + 1) * 64], in0=pv[:, tt, 0:64],
                                                           scalar1=s4[io][:, tt:tt + 1], scalar2=None, op0=ALU.mult),
                          reads=[("pb", b), ("s4", io)], writes=[("mixtm", tt, h)])
            else:
                branch = 2 if kind == "W" else 1
                P.dve(lambda e: e.tensor_scalar(out=s4[io][:], in0=pv[:, :, 64], scalar1=1e-30, scalar2=None, op0=ALU.max),
                      reads=[("pb", b)], writes=[("s4", io)])
                P.dve(lambda e: e.reciprocal(out=s4[io][:], in_=s4[io][:]), reads=[("s4", io)], writes=[("s4", io)])
                P.dve(lambda e: e.tensor_tensor(out=s4[io][:], in0=s4[io][:], in1=gates[:, :, branch * 8 + h], op=ALU.mult),
                      reads=[("s4", io), "gates"], writes=[("s4", io)])
                for tt in range(4):
                    if kind == "W":
                        P.dve(lambda e, tt=tt: e.scalar_tensor_tensor(
                            out=accb[:, tt, h * 64:(h + 1) * 64], in0=pv[:, tt, 0:64], scalar=s4[io][:, tt:tt + 1],
                            in1=accb[:, tt, h * 64:(h + 1) * 64], op0=ALU.mult, op1=ALU.add),
                            reads=[("pb", b), ("s4", io), ("accb", tt, h)], writes=[("accb", tt, h)])
                    else:
                        P.dve(lambda e, tt=tt: e.scalar_tensor_tensor(
                            out=mixtm[:, tt, 512 + h * 64:512 + (h + 1) * 64], in0=pv[:, tt, 0:64],
                            scalar=s4[io][:, tt:tt + 1], in1=accb[:, tt, h * 64:(h + 1) * 64], op0=ALU.mult, op1=ALU.add),
                            reads=[("pb", b), ("s4", io), ("accb", tt, h)], writes=[("mixtm", tt, 8 + h)])

        pending = []
        n1 = sum(len(hd["kts"]) for hd in heads if hd["kind"] != "S")
        inject = {max(1, (k + 1) * n1 // 5): k for k in range(4)}
        def mix_tr(c):
            tr_to(hT[:, c, :], [mixtm[:, tt, c * 128:(c + 1) * 128] for tt in range(4)], ["hT"],
                  [("mixtm", tt, h) for tt in range(4) for h in (2 * c, 2 * c + 1)], use_act=False)
        early = {n1 + 8 + 3 * c: c for c in range(4)} if NT > n1 + 24 else {}
        for i in range(NT + DPIPE + 3):
            if i in inject:
                cmp_tt(inject[i])
            if i in early:
                mix_tr(early[i])
            if i < NT:
                stage_qk(i)
            if DPIPE <= i < NT + DPIPE:
                if stage_pv(i - DPIPE):
                    hi_ = tiles[i - DPIPE][0]
                    fin_copy(hi_)
                    pending.append((i + 2, hi_))
            while pending and pending[0][0] <= i:
                fin_rest(pending.pop(0)[1])
        while pending:
            fin_rest(pending.pop(0)[1])
        tap("mix", tb, lambda tt: mixtm[:, tt, :], lambda tt: [("mixtm", tt, h) for h in range(16)])
        for c in range(8):
            if c < 4 and early:
                continue
            tr_to(hT[:, c, :], [mixtm[:, tt, c * 128:(c + 1) * 128] for tt in range(4)], ["hT"],
                  [("mixtm", tt, h) for tt in range(4) for h in (2 * c, 2 * c + 1)], use_act=(c % 2 == 0))
        rowproj(tb, s_wmo, "s_wmo", 8, lambda j, tt: hT[:, j, tt * 128:(tt + 1) * 128], lambda j: ["hT"], 1.0, wmo_d)
        tap("x2", tb, lambda tt: xs[:, tt, :], lambda tt: [("xs", tt)])
        rms_to_hT(tb, 2)
        P.dve(lambda e: e.memset(small[:, 62:63], 0.0), reads=["arena"], writes=["arena"])
        ffn(tb, s_w2in, s_w2out, "s_w2in", "s_w2out", w2in_d, w2out_d)
        for tt in range(4):
            r0 = t0 + tt * 128
            P.dma("pool", lambda e, tt=tt, r0=r0: e.dma_start(out=out_d[r0:r0 + 128, :], in_=xs[:, tt, :]), "ost",
                  reads=[("xs", tt)])
    for tb in range(NB):
        do_block(tb)
    P.emit()
    return nc


def _perm_cols():
    QA, KVA, QB = 512, 128, 512
    OFF_KA, OFF_VA, OFF_QB, OFF_KVB, OFF_GB = 512, 640, 768, 1280, 2048
    cols = []
    for r in range(4):
        for g in range(2):
            h = g * 4 + r
            cols += list(range(h * 64, h * 64 + 64))
    cols += list(range(OFF_KA, OFF_KA + 256))
    for r in range(4):
        for g in range(2):
            h = g * 4 + r
            cols += list(range(OFF_QB + h * 64, OFF_QB + h * 64 + 64))
    kvb = lambda i, g: list(range(OFF_KVB + i * 128 + g * 64, OFF_KVB + i * 128 + g * 64 + 64))
    cols += kvb(0, 0) + kvb(1, 0) + kvb(0, 1) + kvb(1, 1)
    cols += list(range(OFF_KVB + 256, OFF_KVB + 512))
    cols += list(range(OFF_KVB + 512, OFF_KVB + 768))
    cols += list(range(OFF_GB, OFF_GB + 24))
    return np.array(cols)


def prep_shared(inp):
    f = np.float32
    out = {}

    def win(w):
        w = np.asarray(w, f).reshape(8, 128, 2, NJ, 128)
        return np.ascontiguousarray(w.transpose(3, 1, 2, 0, 4)).reshape(NJ, 128, 2048)

    out["w1in"] = win(inp["ffn1_w_in"][0])
    out["w2in"] = win(inp["ffn2_w_in"][0])
    out["w1out"] = np.ascontiguousarray(np.asarray(inp["ffn1_w_out"][0], f).reshape(NJ, 128, D))
    out["w2out"] = np.ascontiguousarray(np.asarray(inp["ffn2_w_out"][0], f).reshape(NJ, 128, D))
    wm = np.asarray(inp["w_mix_in"][0], f)[:, _perm_cols()]
    wmp = np.zeros((1024, 9 * 256), f)
    wmp[:, :wm.shape[1]] = wm
    wmp = wmp.reshape(8, 128, 9, 256)
    out["wmi"] = np.ascontiguousarray(wmp.transpose(2, 1, 0, 3)).reshape(9, 128, 2048)
    out["wmo"] = np.ascontiguousarray(np.asarray(inp["w_mix_out"][0], f).reshape(8, 128, D))
    g3 = np.stack([inp["ffn1_norm"][0], inp["mix_norm"][0], inp["ffn2_norm"][0]]).astype(f)
    out["gT"] = np.ascontiguousarray(g3.reshape(3, 8, 128).transpose(2, 0, 1)).reshape(128, 24)
    hg = np.stack([inp["q_norm_a"][0], inp["k_norm_a"][0], inp["q_norm_b"][0], inp["k_norm_b"][0]]).astype(f)
    out["hg"] = np.ascontiguousarray(np.broadcast_to(hg.reshape(1, 256), (128, 256)))
    tbl = np.asarray(inp["rel_bias_table"], f)
    out["tblrep"] = np.ascontiguousarray(np.broadcast_to(tbl.reshape(1, 512), (128, 512)))
    out["sinkrep"] = np.ascontiguousarray(np.broadcast_to(np.asarray(inp["sinks_a"][0], f).reshape(1, 8), (128, 8)))
    out["b2rep"] = np.ascontiguousarray(np.broadcast_to(np.asarray(inp["cmp_b2"][0], f).reshape(1, 128), (128, 128)))
    out["b1T"] = np.ascontiguousarray(np.asarray(inp["cmp_b1"][0], f).T)
    pos = np.asarray(inp["cmp_pos"][0], f)
    out["posT"] = np.ascontiguousarray(pos.transpose(0, 2, 1)).reshape(128, 32)
    w1 = np.asarray(inp["cmp_w1"][0], f).reshape(2, 32, 64, 128)
    out["w1c"] = np.ascontiguousarray(w1.transpose(0, 2, 1, 3)).reshape(128, 4096)
    w2 = np.asarray(inp["cmp_w2"][0], f)
    out["w2c"] = np.ascontiguousarray(w2.transpose(1, 0, 2)).reshape(128, 128)
    TAB, MC = _host_tables(tbl)
    out["TAB"] = TAB.reshape(128, 4096)
    out["MC"] = MC.reshape(128, 384)
    FE, WADD, E, ident = _const_tables()
    out["FE"], out["WADD"], out["E"], out["ident"] = FE, WADD, E, ident
    return out


_CACHE = {}


def kernel(**inputs):
    x = np.asarray(inputs["x"], np.float32)
    B = x.shape[0]
    shared = prep_shared(inputs)
    if "nc" not in _CACHE:
        _CACHE["nc"] = build(8)
    nc = _CACHE["nc"]
    in_maps = []
    for b in range(B):
        m = dict(shared)
        m["x"] = np.ascontiguousarray(x[b])
        in_maps.append(m)
    res = run_bass_kernel_spmd(nc, in_maps, core_ids=list(range(B)))
    return np.stack([np.asarray(r["out"], np.float32) for r in res.results], axis=0)
```
